# Optimizing a Trainium2 kernel written in Bass

```python
import math
import jax, jax.numpy as jnp
from jax import lax
import numpy as np

D_MODEL = 1024
BATCH = 8
SEQ = 4096
DEPTH = 4

GRID_W = 64
CTX_LEN = 256
N_MIXERS = 2
N_ATTN = (DEPTH + N_MIXERS - 1) // N_MIXERS
N_RET = DEPTH // N_MIXERS
DA_HEADS = 8
DA_HEAD_DIM = D_MODEL // (2 * DA_HEADS)
RET_HEADS = 4
RET_QK_DIM = D_MODEL // RET_HEADS
RET_V_DIM = 2 * RET_QK_DIM
RET_V_WIDTH = RET_HEADS * RET_V_DIM
D_FF = 4 * D_MODEL
Q_BLOCK = 128
RET_CHUNK = 128
ROPE_BASE = 10000.0
NORM_EPS = 1e-6
N_MOD = 6

kernel_name = "hybrid_diffattn_retention_dit"


def rms_norm(x, eps=NORM_EPS):
    xf = x.astype(jnp.float32)
    return (xf * lax.rsqrt(jnp.mean(xf * xf, axis=-1, keepdims=True) + eps)).astype(x.dtype)


def modulate(x, shift, scale):
    return rms_norm(x) * (1.0 + scale) + shift


def _rotate_half(x, ang):
    x1, x2 = jnp.split(x, 2, axis=-1)
    cos = jnp.cos(ang).astype(x.dtype)
    sin = jnp.sin(ang).astype(x.dtype)
    return jnp.concatenate([x1 * cos - x2 * sin, x1 * sin + x2 * cos], axis=-1)


def axial_rope(x):
    n, d = x.shape[-2], x.shape[-1]
    rows = n // GRID_W
    row = jnp.repeat(jnp.arange(rows), GRID_W).astype(jnp.float32)
    col = jnp.tile(jnp.arange(GRID_W), rows).astype(jnp.float32)
    quarter = d // 4
    inv = ROPE_BASE ** (-jnp.arange(quarter, dtype=jnp.float32) / quarter)
    half = d // 2
    xr = _rotate_half(x[..., :half], row[:, None] * inv)
    xc = _rotate_half(x[..., half:], col[:, None] * inv)
    return jnp.concatenate([xr, xc], axis=-1)


def diff_attention(u, uc, w_qkv, w_o, lam, subln_g, lambda_init, need_ctx):
    H, dh = DA_HEADS, DA_HEAD_DIM

    def proj(z):
        b, n, _ = z.shape
        q, k, v = jnp.split(z @ w_qkv, 3, axis=-1)
        q = q.reshape(b, n, H, 2, dh).transpose(0, 2, 3, 1, 4)
        k = k.reshape(b, n, H, 2, dh).transpose(0, 2, 3, 1, 4)
        v = v.reshape(b, n, H, 2 * dh).transpose(0, 2, 1, 3)
        return q, k, v

    q, k, v = proj(u)
    qc, kc, vc = proj(uc)
    q = axial_rope(q)
    k = axial_rope(k)
    lam_f = lam.astype(jnp.float32)
    lam_full = (jnp.exp(jnp.sum(lam_f[0] * lam_f[1])) - jnp.exp(jnp.sum(lam_f[2] * lam_f[3]))
                + lambda_init)
    scale = dh ** -0.5

    def attend(qb, kk, vv):
        s = jnp.einsum('bhiqd,bhikd->bhiqk', qb, kk).astype(jnp.float32) * scale
        p = jax.nn.softmax(s, axis=-1).astype(vv.dtype)
        o = jnp.einsum('bhiqk,bhkv->bhiqv', p, vv)
        return o[:, :, 0] - lam_full.astype(o.dtype) * o[:, :, 1]

    def finish(o):
        b, _, n, _ = o.shape
        o = rms_norm(o) * subln_g * (1.0 - lambda_init)
        return o.transpose(0, 2, 1, 3).reshape(b, n, H * 2 * dh) @ w_o

    B, _, _, S, _ = q.shape
    k_all = jnp.concatenate([k, kc], axis=3)
    v_all = jnp.concatenate([v, vc], axis=2)
    nb = S // Q_BLOCK
    qb = q.reshape(B, H, 2, nb, Q_BLOCK, dh).transpose(3, 0, 1, 2, 4, 5)
    o = lax.map(lambda blk: attend(blk, k_all, v_all), qb)
    o = o.transpose(1, 2, 0, 3, 4).reshape(B, H, S, 2 * dh)
    y = finish(o)
    yc = finish(attend(qc, kc, vc)) if need_ctx else None
    return y, yc


def retention_scan(q, k, v, log_gamma, state0, strict):
    B, H, n, dk = q.shape
    dv = v.shape[-1]
    C = RET_CHUNK
    nc = n // C
    idx = jnp.arange(C, dtype=jnp.float32)
    diff = idx[:, None] - idx[None, :]
    mask = (diff > 0) if strict else (diff >= 0)
    dmat = jnp.where(mask, jnp.exp(jnp.maximum(diff, 0.0) * log_gamma[:, None, None]), 0.0)
    xi = jnp.exp((idx + 1.0) * log_gamma[:, None])[..., None]
    zeta = jnp.exp((C - 1.0 - idx) * log_gamma[:, None])[..., None]
    g_chunk = jnp.exp(C * log_gamma)[:, None, None]

    def chunks(t):
        return t.astype(jnp.float32).reshape(B, H, nc, C, t.shape[-1]).transpose(2, 0, 1, 3, 4)

    def step(state, blk):
        qc, kc, vc = blk
        inner = jnp.einsum('bhqk,bhkv->bhqv', jnp.einsum('bhqd,bhkd->bhqk', qc, kc) * dmat, vc)
        cross = jnp.einsum('bhqd,bhdv->bhqv', qc, state) * xi
        state = state * g_chunk + jnp.einsum('bhkd,bhkv->bhdv', kc * zeta, vc)
        return state, inner + cross

    state, out = lax.scan(step, state0, (chunks(q), chunks(k), chunks(v)))
    out = out.transpose(1, 2, 0, 3, 4).reshape(B, H, n, dv)
    return out, state


def retention(u, uc, w_in, w_o, decay_logit, need_ctx):
    H, dk, dv = RET_HEADS, RET_QK_DIM, RET_V_DIM
    splits = [H * dk, 2 * H * dk, 2 * H * dk + RET_V_WIDTH]

    def proj(z):
        b, n, _ = z.shape
        q, k, v, g = jnp.split(z @ w_in, splits, axis=-1)
        q = q.reshape(b, n, H, dk).transpose(0, 2, 1, 3)
        k = k.reshape(b, n, H, dk).transpose(0, 2, 1, 3) * (dk ** -0.5)
        v = v.reshape(b, n, H, dv).transpose(0, 2, 1, 3)
        return q, k, v, g

    q, k, v, g = proj(u)
    qc, kc, vc, gc = proj(uc)
    q = axial_rope(q)
    k = axial_rope(k)
    log_gamma = jax.nn.log_sigmoid(decay_logit.astype(jnp.float32))
    B = u.shape[0]
    state0 = jnp.zeros((B, H, dk, dv), jnp.float32)
    flip = lambda t: jnp.flip(t, axis=2)
    oc_f, st_f = retention_scan(qc, kc, vc, log_gamma[0], state0, False)
    o_f, _ = retention_scan(q, k, v, log_gamma[0], st_f, False)
    oc_b, st_b = retention_scan(flip(qc), flip(kc), flip(vc), log_gamma[1], state0, True)
    o_b, _ = retention_scan(flip(q), flip(k), flip(v), log_gamma[1], st_b, True)

    def finish(o, gate):
        b, _, n, _ = o.shape
        mu = jnp.mean(o, axis=-1, keepdims=True)
        var = jnp.mean(jnp.square(o - mu), axis=-1, keepdims=True)
        o = (o - mu) * lax.rsqrt(var + 1e-5)
        o = o.transpose(0, 2, 1, 3).reshape(b, n, RET_V_WIDTH).astype(gate.dtype)
        return (jax.nn.silu(gate) * o) @ w_o

    y = finish(o_f + flip(o_b), g)
    yc = finish(oc_f + flip(oc_b), gc) if need_ctx else None
    return y, yc


def channel_mlp(u, w1, w2):
    return jnp.square(jax.nn.relu(u @ w1)) @ w2


def setup_inputs(seed: int = 0) -> dict:
    key = jax.random.key(seed)
    ks = jax.random.split(key, 20)
    f32 = jnp.float32
    nrm = lambda k, shape, s: jax.random.normal(k, shape, f32) * s
    gam0 = 1.0 - 2.0 ** (-5.0 - jnp.arange(RET_HEADS, dtype=f32))
    logit0 = jnp.log(gam0) - jnp.log1p(-gam0)
    return {
        "x": nrm(ks[0], (BATCH, SEQ, D_MODEL), 1.0),
        "c": nrm(ks[1], (BATCH, D_MODEL), 1.0),
        "ctx": nrm(ks[2], (BATCH, CTX_LEN, D_MODEL), 1.0),
        "c_ctx": nrm(ks[3], (D_MODEL,), 1.0),
        "ada_w": nrm(ks[4], (DEPTH, D_MODEL, N_MOD * D_MODEL), 0.5 * D_MODEL ** -0.5),
        "ada_b": nrm(ks[5], (DEPTH, N_MOD * D_MODEL), 0.02),
        "attn_w_qkv": nrm(ks[6], (N_ATTN, D_MODEL, 3 * D_MODEL), D_MODEL ** -0.5),
        "attn_w_o": nrm(ks[7], (N_ATTN, D_MODEL, D_MODEL), D_MODEL ** -0.5),
        "attn_lambda": nrm(ks[8], (N_ATTN, 4, DA_HEAD_DIM), 0.1),
        "attn_subln_g": 1.0 + nrm(ks[9], (N_ATTN, 2 * DA_HEAD_DIM), 0.02),
        "ret_w_in": nrm(ks[10], (N_RET, D_MODEL, 2 * D_MODEL + 2 * RET_V_WIDTH), D_MODEL ** -0.5),
        "ret_w_o": nrm(ks[11], (N_RET, RET_V_WIDTH, D_MODEL), RET_V_WIDTH ** -0.5),
        "ret_decay_logit": jnp.broadcast_to(logit0, (N_RET, 2, RET_HEADS)) + nrm(ks[12], (N_RET, 2, RET_HEADS), 0.1),
        "mlp_w1": nrm(ks[13], (DEPTH, D_MODEL, D_FF), D_MODEL ** -0.5),
        "mlp_w2": nrm(ks[14], (DEPTH, D_FF, D_MODEL), D_FF ** -0.5),
        "final_norm_g": 1.0 + nrm(ks[15], (D_MODEL,), 0.02),
    }


def reference(x, c, ctx, c_ctx, ada_w, ada_b, attn_w_qkv, attn_w_o, attn_lambda, attn_subln_g,
              ret_w_in, ret_w_o, ret_decay_logit, mlp_w1, mlp_w2, final_norm_g):
    B, _, D = x.shape
    h, hc = x, ctx
    sc = jax.nn.silu(c)
    scc = jax.nn.silu(c_ctx)
    for i in range(DEPTH):
        need_ctx = i < DEPTH - 1
        mod = (sc @ ada_w[i] + ada_b[i]).reshape(B, N_MOD, 1, D)
        mod_c = (scc @ ada_w[i] + ada_b[i]).reshape(N_MOD, D)
        u = modulate(h, mod[:, 0], mod[:, 1])
        uc = modulate(hc, mod_c[0], mod_c[1])
        j = i // N_MIXERS
        if i % N_MIXERS == 0:
            lambda_init = 0.8 - 0.6 * math.exp(-0.3 * i)
            y, yc = diff_attention(u, uc, attn_w_qkv[j], attn_w_o[j], attn_lambda[j],
                                   attn_subln_g[j], lambda_init, need_ctx)
        else:
            y, yc = retention(u, uc, ret_w_in[j], ret_w_o[j], ret_decay_logit[j], need_ctx)
        h = h + mod[:, 2] * y
        h = h + mod[:, 5] * channel_mlp(modulate(h, mod[:, 3], mod[:, 4]), mlp_w1[i], mlp_w2[i])
        if need_ctx:
            hc = hc + mod_c[2] * yc
            hc = hc + mod_c[5] * channel_mlp(modulate(hc, mod_c[3], mod_c[4]), mlp_w1[i], mlp_w2[i])
    return rms_norm(h) * final_norm_g
```

```python
import math
from contextlib import ExitStack

import numpy as np
import ml_dtypes
import concourse.bass as bass
import concourse.mybir as mybir
from concourse.bass_utils import run_bass_kernel_spmd

F32 = mybir.dt.float32
BF16 = mybir.dt.bfloat16
U8 = mybir.dt.uint8
AF = mybir.ActivationFunctionType
ALU = mybir.AluOpType
AX = mybir.AxisListType

D = 1024
LC = 256
DEPTH = 4
EPS = 1e-6
ENG = ('pe', 'act', 'dve', 'pool', 'sp')
HANDLE = {'pe': 'tensor', 'act': 'scalar', 'dve': 'vector', 'pool': 'gpsimd', 'sp': 'sync'}
CH = 16000
NDSEM = {'sp': 24, 'pool': 48, 'act': 2}
SAME_SYNC = {'act', 'dve', 'pool'}
ARENA_BYTES = 204 * 1024
TRUNC = 0


class V:
    __slots__ = ('ap', 'key')

    def __init__(self, ap, key=None):
        self.ap = ap
        self.key = key

    def __getitem__(self, idx):
        return V(self.ap[idx], self.key)

    def rr(self, pat, **kw):
        return V(self.ap.rearrange(pat, **kw), self.key)

    def bc(self, dt):
        return V(self.ap.bitcast(dt), self.key)


class Op:
    __slots__ = ('eng', 'fn', 'dma', 'deps', 'sig', 'signo', 'dsem', 'dval', 'qidx')

    def __init__(self, eng, fn, dma):
        self.eng = eng
        self.fn = fn
        self.dma = dma
        self.deps = {}
        self.sig = False
        self.signo = 0
        self.dsem = None
        self.dval = 0
        self.qidx = 0


class Sched:
    def __init__(self):
        self.ops = {e: [] for e in ENG}
        self.lastw = {}
        self.rd = {}
        self.pending_dmas = []
        self.ndma = {e: 0 for e in ENG}
        self.bg = {}

    def add(self, eng, fn, r=(), w=(), dma=False, bg=False):
        op = Op(eng, fn, dma)
        deps = op.deps
        if bg:
            for x in w:
                self.bg.setdefault(x, []).append(op)
            op.qidx = self.ndma[eng]
            self.ndma[eng] += 1
            self.ops[eng].append(op)
            return op
        for x in r:
            for o in self.bg.get(x, ()):
                deps[o] = 'raw'
        w = list(w) + [x for x in r if isinstance(x, tuple) and x and x[0] == 'PS' and x not in w]

        def dep(o, kind):
            if o is op:
                return
            if kind == 'raw' or o not in deps:
                deps[o] = kind
        for x in r:
            if x is None:
                continue
            o = self.lastw.get(x)
            if o is not None:
                dep(o, 'raw')
        for x in w:
            if x is None:
                continue
            o = self.lastw.get(x)
            if o is not None:
                dep(o, 'waw')
            rdx = self.rd.get(x)
            if rdx:
                for k, o2 in rdx.items():
                    if k == 'dma':
                        for o3 in o2:
                            dep(o3, 'war')
                    else:
                        dep(o2, 'war')
        for x in r:
            if x is None:
                continue
            rdx = self.rd.setdefault(x, {})
            if dma:
                rdx.setdefault('dma', []).append(op)
            else:
                rdx[eng] = op
        for x in w:
            if x is None:
                continue
            self.lastw[x] = op
            self.rd[x] = {}
        if dma:
            op.qidx = self.ndma[eng]
            self.ndma[eng] += 1
            self.pending_dmas.append(op)
        self.ops[eng].append(op)
        return op

    def barrier(self):
        b = Op('sp', lambda e: e.nop(), False)
        for e in ENG:
            for o in reversed(self.ops[e]):
                if not o.dma:
                    b.deps[o] = 'raw'
                    break
        for o in self.pending_dmas:
            b.deps[o] = 'raw'
        self.pending_dmas = []
        self.ops['sp'].append(b)
        for e in ENG:
            if e == 'sp':
                continue
            m = Op(e, lambda en: en.nop(), False)
            m.deps[b] = 'raw'
            self.ops[e].append(m)
        self.lastw = {}
        self.rd = {}

    @staticmethod
    def _needs(op, d, kind):
        if d.dma:
            return True
        if op.dma:
            return True
        if d.eng != op.eng:
            return True
        return op.eng in SAME_SYNC

    def finalize(self):
        for e in ENG:
            for op in self.ops[e]:
                for d, kind in op.deps.items():
                    if not d.dma and self._needs(op, d, kind):
                        d.sig = True
        self.nsig = {}
        for e in ENG:
            n = 0
            for op in self.ops[e]:
                if op.sig:
                    n += 1
                    op.signo = n
            self.nsig[e] = n

    def emit(self, nc, stack, block):
        self.finalize()
        csem = {}
        for e in ENG:
            nep = max(1, (self.nsig[e] + CH - 1) // CH)
            csem[e] = [stack.enter_context(nc.semaphore(f"c_{e}_{i}")) for i in range(nep)]
        dsem = {}
        for e, n in NDSEM.items():
            dsem[e] = [stack.enter_context(nc.semaphore(f"d_{e}_{i}")) for i in range(n)]
        for e in ENG:
            for op in self.ops[e]:
                if op.dma:
                    k = NDSEM[e]
                    op.dsem = dsem[e][op.qidx % k]
                    op.dval = 16 * (op.qidx // k + 1)
        sched = self

        def make_body(eng):
            def body(e):
                known = {f: 0 for f in ENG}
                knownd = {}
                for op in sched.ops[eng]:
                    waits = []
                    for d, kind in op.deps.items():
                        if not sched._needs(op, d, kind):
                            continue
                        if d.dma:
                            kk = id(d.dsem)
                            if knownd.get(kk, 0) < d.dval:
                                knownd[kk] = d.dval
                                waits.append((d.dsem, d.dval))
                        else:
                            n = d.signo
                            if known[d.eng] >= n:
                                continue
                            known[d.eng] = n
                            waits.append((csem[d.eng][(n - 1) // CH], (n - 1) % CH + 1))
                    if op.dma and op.dval > 16:
                        kk = id(op.dsem)
                        if knownd.get(kk, 0) < op.dval - 16:
                            knownd[kk] = op.dval - 16
                            waits.append((op.dsem, op.dval - 16))
                    best = {}
                    for sm, v in waits:
                        kk = id(sm)
                        if kk not in best or best[kk][1] < v:
                            best[kk] = (sm, v)
                    for sm, v in best.values():
                        e.wait_ge(sm, v)
                    ins = op.fn(e)
                    if op.dma:
                        ins.then_inc(op.dsem, 16)
                    elif op.sig:
                        n = op.signo
                        ins.then_inc(csem[eng][(n - 1) // CH], 1)
            return body
        for eng in ENG:
            getattr(block, HANDLE[eng])(make_body(eng))


class Arena:
    def __init__(self, ap_u8, nbytes):
        self.ap = ap_u8
        self.n = nbytes
        self.off = 0
        self.gen = 0

    def reset(self):
        self.off = 0
        self.gen += 1

    def alloc(self, free_shape, dt, name):
        esz = 2 if dt == BF16 else 4
        n = esz
        for s in free_shape:
            n *= s
        n_al = (n + 63) // 64 * 64
        assert self.off + n_al <= self.n, f"arena overflow at {name}: {self.off}+{n_al} > {self.n}"
        ap = self.ap[:, self.off:self.off + n].bitcast(dt)
        self.off += n_al
        if len(free_shape) == 2:
            ap = ap.rearrange("p (a b) -> p a b", b=free_shape[1])
        elif len(free_shape) == 3:
            ap = ap.rearrange("p (a b c) -> p a b c", b=free_shape[1], c=free_shape[2])
        return V(ap, (name, self.gen))


class K:
    def __init__(self, nc, S):
        self.nc = nc
        self.S = S

    @staticmethod
    def _keys(*vs):
        out = []
        for v in vs:
            if isinstance(v, V) and v.key is not None:
                if isinstance(v.key, list):
                    out.extend(v.key)
                else:
                    out.append(v.key)
        return out

    def mm(self, out, lhsT, rhs, start=True, stop=True):
        o, a, b = out.ap, lhsT.ap, rhs.ap
        self.S.add('pe', lambda e: e.matmul(o, lhsT=a, rhs=b, start=start, stop=stop),
                   r=self._keys(lhsT, rhs), w=self._keys(out))

    def tr(self, out, in_, ident):
        o, a, b = out.ap, in_.ap, ident.ap
        self.S.add('pe', lambda e: e.transpose(o, a, b), r=self._keys(in_, ident), w=self._keys(out))

    def act(self, out, in_, func, scale=None, bias=None, accum=None):
        o, a = out.ap, in_.ap
        kw = {}
        rk = [in_]
        wk = [out]
        if scale is not None:
            if isinstance(scale, V):
                kw['scale'] = scale.ap
                rk.append(scale)
            else:
                kw['scale'] = float(scale)
        if bias is not None:
            if isinstance(bias, V):
                kw['bias'] = bias.ap
                rk.append(bias)
            else:
                kw['bias'] = float(bias)
        if accum is not None:
            kw['accum_out'] = accum.ap
            wk.append(accum)
        self.S.add('act', lambda e: e.activation(o, a, func, **kw), r=self._keys(*rk), w=self._keys(*wk))

    def ts(self, eng, out, in0, s1, s2=None, op0=ALU.mult, op1=None):
        o, a = out.ap, in0.ap
        rk = [in0]
        if isinstance(s1, V):
            rk.append(s1)
            s1 = s1.ap
        if isinstance(s2, V):
            rk.append(s2)
            s2 = s2.ap
        if op1 is None:
            fn = lambda e: e.tensor_scalar(o, a, s1, None, op0)
        else:
            fn = lambda e: e.tensor_scalar(o, a, s1, s2, op0, op1)
        self.S.add(eng, fn, r=self._keys(*rk), w=self._keys(out))

    def tt(self, eng, out, in0, in1, op):
        o, a, b = out.ap, in0.ap, in1.ap
        self.S.add(eng, lambda e: e.tensor_tensor(o, a, b, op), r=self._keys(in0, in1), w=self._keys(out))

    def stt(self, out, in0, scalar, in1, op0, op1):
        o, a, b = out.ap, in0.ap, in1.ap
        rk = [in0, in1]
        if isinstance(scalar, V):
            rk.append(scalar)
            scalar = scalar.ap
        self.S.add('dve', lambda e: e.scalar_tensor_tensor(o, a, scalar, b, op0, op1),
                   r=self._keys(*rk), w=self._keys(out))

    def copy(self, eng, out, in_):
        o, a = out.ap, in_.ap
        if eng == 'act':
            fn = lambda e: e.copy(o, a)
        else:
            fn = lambda e: e.tensor_copy(o, a)
        self.S.add(eng, fn, r=self._keys(in_), w=self._keys(out))

    def recip(self, out, in_):
        o, a = out.ap, in_.ap
        self.S.add('dve', lambda e: e.reciprocal(o, a), r=self._keys(in_), w=self._keys(out))

    def memset(self, eng, out, val):
        o = out.ap
        self.S.add(eng, lambda e: e.memset(o, val), w=self._keys(out))

    def reduce(self, out, in_, op=ALU.add):
        o, a = out.ap, in_.ap
        self.S.add('dve', lambda e: e.tensor_reduce(o, a, AX.X, op), r=self._keys(in_), w=self._keys(out))

    def bn_stats(self, out, in_):
        o, a = out.ap, in_.ap
        self.S.add('dve', lambda e: e.bn_stats(o, a), r=self._keys(in_), w=self._keys(out))

    def bn_aggr(self, out, in_):
        o, a = out.ap, in_.ap
        self.S.add('dve', lambda e: e.bn_aggr(o, a), r=self._keys(in_), w=self._keys(out))

    def dma(self, q, out, in_, bg=False):
        o, a = out.ap, in_.ap
        self.S.add(q, lambda e: e.dma_start(out=o, in_=a), r=self._keys(in_), w=self._keys(out), dma=True, bg=bg)

    def barrier(self):
        self.S.barrier()


def host_consts(T):
    t = np.arange(T)
    row = (t // 64).astype(np.float32)
    col = (t % 64).astype(np.float32)
    p = np.arange(128)
    m = p % 64
    quarter, half = 16, 32
    inv = (np.float32(10000.0) ** (-(np.arange(quarter, dtype=np.float32)) / np.float32(quarter))).astype(np.float32)
    isrow = m < half
    mloc = np.where(isrow, m, m - half)
    f = mloc % quarter
    first = mloc < quarter
    pos = np.where(isrow[:, None], row[None, :], col[None, :]).astype(np.float32)
    ang = (pos * inv[f][:, None]).astype(np.float32)
    sgn = np.where(first, -1.0, 1.0)[:, None]
    AC = np.cos(ang.astype(np.float64)).astype(np.float32)
    AS = (np.sin(ang.astype(np.float64)) * sgn).astype(np.float32)
    partner = np.where(first, p + quarter, p - quarter)
    permA = np.zeros((128, 128), np.float32)
    permA[partner, p] = 1.0
    quarter = 64
    inv = (np.float32(10000.0) ** (-(np.arange(quarter, dtype=np.float32)) / np.float32(quarter))).astype(np.float32)
    f = p % 64
    first = p < 64
    sgn = np.where(first, -1.0, 1.0)[:, None]
    angr = (row[None, :] * inv[f][:, None]).astype(np.float32).astype(np.float64)
    angc = (col[None, :] * inv[f][:, None]).astype(np.float32).astype(np.float64)
    RT = np.stack([np.cos(angr), np.sin(angr) * sgn, np.cos(angc), np.sin(angc) * sgn], 0).astype(np.float32)
    partner = np.where(first, p + 64, p - 64)
    permR = np.zeros((128, 128), np.float32)
    permR[partner, p] = 1.0
    k = np.arange(128)[:, None].astype(np.float32)
    q = np.arange(128)[None, :].astype(np.float32)
    rc = np.stack([np.maximum(q - k, 0), (q >= k).astype(np.float32),
                   np.maximum(k - q, 0), (k > q).astype(np.float32),
                   np.broadcast_to(q + 1, (128, 128)), np.broadcast_to(128 - q, (128, 128))], 1).astype(np.float32)
    ze = np.zeros((128, 8), np.float32)
    ze[:, 0:4] = (127 - np.arange(128))[:, None]
    ze[:, 4:8] = np.arange(128)[:, None]
    bf = ml_dtypes.bfloat16
    return dict(AC=AC, AS=AS, RT=np.ascontiguousarray(RT), permA=permA.astype(bf), permR=permR.astype(bf),
                rcon=np.ascontiguousarray(rc), zec=ze, identb=np.eye(128, dtype=np.float32).astype(bf),
                identf=np.eye(128, dtype=np.float32))


def build(T, layers=(0, 1, 2, 3), dbg=(), stop_after=None):
    NB = T // 512
    NTL = T // 128
    NTOK = T + LC
    NT = NTOK // 128
    nc = bass.Bass("TRN2", target_bir_lowering=False)
    S = Sched()
    kb = K(nc, S)

    def din(name, shape, dt=F32):
        return nc.dram_tensor(name, shape, dt, kind="ExternalInput").ap()

    def dscr(name, shape, dt):
        kind = "ExternalOutput" if name in dbg else "Internal"
        return nc.dram_tensor(name, shape, dt, kind=kind).ap()

    x = din("x", [T, D])
    ctx = din("ctx", [LC, D])
    cT = din("cT", [128, 8])
    cctxT = din("cctxT", [128, 8])
    ada_w = din("ada_w", [4, 128, 8, 6144])
    ada_b = din("ada_b", [4, 6144])
    wqkv = din("wqkv", [2, 128, 8, 3072])
    wo_a = din("wo_a", [2, 128, 8, 1024])
    lam = din("lam", [2, 256])
    subg = din("subg", [128, 2])
    win = din("win", [2, 128, 8, 6144])
    wo_r = din("wo_r", [2, 128, 16, 1024])
    dlog = din("dlog", [2, 8])
    w1 = din("w1", [4, 128, 8, 4096])
    w2 = din("w2", [4, 128, 32, 1024])
    fng = din("fng", [1024])
    AC = din("AC", [128, T])
    AS = din("AS", [128, T])
    RT = din("RT", [4, 128, T])
    permA_d = din("permA", [128, 128], BF16)
    permR_d = din("permR", [128, 128], BF16)
    rcon_d = din("rcon", [128, 6, 128])
    zec_d = din("zec", [128, 8])
    identb_d = din("identb", [128, 128], BF16)
    identf_d = din("identf", [128, 128])
    out = nc.dram_tensor("out", [T, D], F32, kind="ExternalOutput").ap()

    hbuf = dscr("hbuf", [NTOK, D], F32)
    QT = dscr("QT", [1024, NTOK], BF16)
    KT = dscr("KT", [1024, NTOK], BF16)
    VA = dscr("VA", [NTOK, 1024], BF16)
    VR = dscr("VR", [NTOK, 2048], BF16)
    GS = dscr("GS", [NTOK, 2048], BF16)
    KZ = dscr("KZ", [NTOK, 2048], BF16)
    OB = dscr("OB", [NTOK, 2048], F32)
    OT = dscr("OT", [2048, NTOK], BF16)
    modrow = dscr("modrow", [4, 2, 6144], F32)
    UT = dscr("UT", [1024, NTOK], BF16)
    wsrc = {'wqkv': (wqkv, [2, 128, 8, 3072]), 'wo_a': (wo_a, [2, 128, 8, 1024]), 'win': (win, [2, 128, 8, 6144]),
            'wo_r': (wo_r, [2, 128, 16, 1024]), 'w1': (w1, [4, 128, 8, 4096]), 'w2': (w2, [4, 128, 32, 1024])}
    wbf = {k_: nc.dram_tensor(k_ + "_bf", shp, BF16, kind="Internal").ap() for k_, (_, shp) in wsrc.items()}

    stack = ExitStack()
    with stack:
        arena_t = stack.enter_context(nc.sbuf_tensor("arena", [128, ARENA_BYTES], U8))
        A = Arena(arena_t[:], ARENA_BYTES)

        def pers(name, shape, dt):
            tt_ = stack.enter_context(nc.sbuf_tensor(name, [128] + shape, dt))
            return V(tt_[:], name)
        modT = pers("modT", [4, 2, 48], F32)
        identb = pers("identb_s", [128], BF16)
        identf = pers("identf_s", [128], F32)
        onesb = pers("onesb", [128], BF16)
        onesf = pers("onesf", [128], F32)
        permA = pers("permA_s", [128], BF16)
        permR = pers("permR_s", [128], BF16)
        epst = pers("epst", [2], F32)
        psall_t = stack.enter_context(nc.psum_tensor("psall", [128, 8, 512], F32))
        psall = psall_t[:]
        PS = [V(psall[:, i, :], ('PS', i)) for i in range(8)]

        def ps2(j):
            return V(psall[:, 2 * j:2 * j + 2, :], [('PS', 2 * j), ('PS', 2 * j + 1)])

        def psb(i):
            return PS[i].bc(BF16)

        kb.dma('sp', identb, V(identb_d))
        kb.dma('sp', identf, V(identf_d))
        kb.dma('sp', permA, V(permA_d))
        kb.dma('sp', permR, V(permR_d))
        kb.memset('dve', onesb, 1.0)
        kb.memset('dve', onesf, 1.0)
        kb.memset('dve', epst[:, 0:1], EPS)
        kb.memset('dve', epst[:, 1:2], 1e-5)

        blocks = [(i * 512, 512, False) for i in range(NB)] + [(T, 256, True)]

        def hsrc(l, tok0, n=128):
            if l == layers[0] and l == 0:
                if tok0 >= T:
                    return V(ctx[tok0 - T:tok0 - T + n, :])
                return V(x[tok0:tok0 + n, :])
            return V(hbuf[tok0:tok0 + n, :])

        def p0():
            A.reset()
            cc0 = A.alloc([8], F32, 'cc0')
            cc1 = A.alloc([8], F32, 'cc1')
            sc = A.alloc([8, 2], F32, 'sc')
            adar = A.alloc([6144], F32, 'adar')
            mrows = [A.alloc([6144], F32, f'mrows{i}') for i in range(2)]
            pieces = [A.alloc([8, 512], F32, f'piece{i}') for i in range(4)]
            kb.dma('sp', cc0, V(cT))
            kb.dma('sp', cc1, V(cctxT))
            kb.act(sc[:, :, 0], cc0, AF.Silu)
            kb.act(sc[:, :, 1], cc1, AF.Silu)
            pi = 0
            for l in range(4):
                mr = mrows[l % 2]
                kb.dma('sp', adar[0:2, :], V(ada_b[l].partition_broadcast(2)))
                for n0 in range(12):
                    pc = pieces[pi % 4]
                    pi += 1
                    kb.dma('sp', pc, V(ada_w[l, :, :, n0 * 512:(n0 + 1) * 512]))
                    ps = PS[n0 % 2]
                    for k in range(8):
                        kb.mm(ps[0:2, :], sc[:, k, :], pc[:, k, :], start=(k == 0), stop=(k == 7))
                    kb.tt('dve', mr[0:2, n0 * 512:(n0 + 1) * 512], ps[0:2, :], adar[0:2, n0 * 512:(n0 + 1) * 512], ALU.add)
                kb.dma('sp', V(modrow[l]), mr[0:2, :])
                for ci in range(48):
                    kb.tr(PS[2][:, ci:ci + 49:48], mr[0:2, ci * 128:(ci + 1) * 128], identf[0:2, 0:2])
                kb.copy('act', modT[:, l].rr("p b c -> p (b c)"), PS[2][:, 0:96])
                for jm in (1, 4):
                    kb.ts('dve', modT[:, l, :, jm * 8:(jm + 1) * 8], modT[:, l, :, jm * 8:(jm + 1) * 8], 1.0, None, ALU.add)
            kb.barrier()

        def prologue_s1(ht, t, bufs):
            junk, ssq, lnv, rstd, xns = bufs
            xn = xns[t % len(xns)]
            kb.act(junk, ht, AF.Square, accum=ssq[:, t:t + 1])
            kb.act(lnv[:, t:t + 1], ssq[:, t:t + 1], AF.Ln, scale=1.0 / D, bias=epst[:, 0:1])
            kb.act(rstd[:, t:t + 1], lnv[:, t:t + 1], AF.Exp, scale=-0.5)
            kb.ts('dve', xn, ht, rstd[:, t:t + 1], None, ALU.mult)

        def prologue_s2(l, a, b, t, u, bufs, banks=(7,)):
            junk, ssq, lnv, rstd, xns = bufs
            xn = xns[t % len(xns)]
            pT = psb(banks[t % len(banks)])
            for c in range(8):
                kb.tr(pT[:, c * 128:(c + 1) * 128], xn[:, c * 128:(c + 1) * 128], identb)
            for c in range(8):
                shift = modT[:, l, b, (3 * a) * 8 + c:(3 * a) * 8 + c + 1]
                scl = modT[:, l, b, (3 * a + 1) * 8 + c:(3 * a + 1) * 8 + c + 1]
                if c % 2 == 0:
                    kb.ts('dve', u[:, c, t * 128:(t + 1) * 128], pT[:, c * 128:(c + 1) * 128], scl, shift, ALU.mult, ALU.add)
                else:
                    kb.act(u[:, c, t * 128:(t + 1) * 128], pT[:, c * 128:(c + 1) * 128], AF.Identity, scale=scl, bias=shift)

        def prologue(l, a, b, ht, t, u, bufs):
            prologue_s1(ht, t, bufs)
            prologue_s2(l, a, b, t, u, bufs)

        def prologue_bufs(nx=2):
            junk = A.alloc([1024], BF16, 'junk')
            ssq = A.alloc([4], F32, 'ssq')
            lnv = A.alloc([4], F32, 'lnv')
            rstd = A.alloc([4], F32, 'rstd')
            xns = [A.alloc([1024], BF16, f'xn{i}') for i in range(nx)]
            return (junk, ssq, lnv, rstd, xns)

        conv_pending = []

        def convert_w(name, idx, defer=False):
            src, shp = wsrc[name]
            for c in range(shp[2]):
                def one(c=c):
                    kb.dma('pool', V(wbf[name][idx, :, c, :], ('wbf', name, idx)), V(src[idx, :, c, :]), bg=True)
                if defer:
                    conv_pending.append(one)
                else:
                    one()

        def emit_conversions(frac):
            n = int(math.ceil(len(conv_pending) * frac)) if conv_pending else 0
            for _ in range(min(n, len(conv_pending))):
                conv_pending.pop(0)()

        def load_w(dst, name, idx, nk):
            ncol = wsrc[name][1][3]
            g = max(1, min(nk, (32 * 1024) // (ncol * 2)))
            for c0 in range(0, nk, g):
                c1 = min(nk, c0 + g)
                kb.dma('sp', dst[:, c0:c1, :], V(wbf[name][idx, :, c0:c1, :], ('wbf', name, idx)))

        def p1_attn(l):
            j = l // 2
            A.reset()
            W = A.alloc([8, 3072], BF16, 'W')
            load_w(W, 'wqkv', j, 8)
            pb = prologue_bufs(4)
            hts = [A.alloc([1024], F32, f'ht{i}') for i in range(4)]
            uTs = [A.alloc([8, 512], BF16, f'uT{i}') for i in range(2)]
            Cb = [A.alloc([512], F32, f'Cb{i}') for i in range(2)]
            Sb = [A.alloc([512], F32, f'Sb{i}') for i in range(2)]
            qraw = [A.alloc([512], BF16, f'qraw{i}') for i in range(2)]
            t1 = [A.alloc([512], F32, f't1{i}') for i in range(2)]
            t2 = [A.alloc([512], F32, f't2{i}') for i in range(2)]
            qo = [A.alloc([512], BF16, f'qo{i}') for i in range(3)]
            vo = [A.alloc([1024], BF16, f'vo{i}') for i in range(2)]
            qi = 0
            vi = 0

            def loads(bi_):
                tok0_, ntok_, isctx_ = blocks[bi_]
                for t in range(ntok_ // 128):
                    kb.dma('sp', hts[t], hsrc(l, tok0_ + t * 128))
                if not isctx_:
                    kb.dma('sp', Cb[bi_ % 2], V(AC[:, tok0_:tok0_ + 512]))
                    kb.dma('sp', Sb[bi_ % 2], V(AS[:, tok0_:tok0_ + 512]))

            def s1(bi_):
                for t in range(blocks[bi_][1] // 128):
                    prologue_s1(hts[t], t, pb)

            def s2(bi_):
                for t in range(blocks[bi_][1] // 128):
                    prologue_s2(l, 0, 1 if blocks[bi_][2] else 0, t, uTs[bi_ % 2], pb, banks=(6, 7))
            loads(0)
            s1(0)
            s2(0)
            if len(blocks) > 1:
                loads(1)
            for bi, (tok0, ntok, isctx) in enumerate(blocks):
                nt = ntok // 128
                u = uTs[bi % 2]
                pend = None
                for jj in range(16):
                    if jj == 6 and bi + 1 < len(blocks):
                        s1(bi + 1)
                    ps = PS[jj % 2]
                    for c in range(8):
                        kb.mm(ps[:, :ntok], W[:, c, jj * 128:(jj + 1) * 128], u[:, c, :ntok], start=(c == 0), stop=(c == 7))
                    dst = (QT if jj < 8 else KT)[(jj % 8) * 128:(jj % 8 + 1) * 128, tok0:tok0 + ntok]
                    q_ = qo[qi % 3]
                    qi += 1
                    if isctx:
                        kb.copy('act', q_[:, :ntok], ps[:, :ntok])
                        kb.dma('sp', V(dst), q_[:, :ntok])
                    else:
                        qr = qraw[jj % 2]
                        kb.copy('act', qr, ps)
                        kb.tt('dve', t1[jj % 2], ps, Cb[bi % 2], ALU.mult)

                        def fin(jj=jj, qr=qr, q_=q_, dst=dst):
                            ps2 = PS[2 + jj % 2]
                            kb.mm(ps2, permA, qr)
                            kb.tt('dve', t2[jj % 2], ps2, Sb[bi % 2], ALU.mult)
                            kb.tt('pool', q_, t1[jj % 2], t2[jj % 2], ALU.add)
                            kb.dma('sp', V(dst), q_)
                        if pend is not None:
                            pend()
                        pend = fin
                if pend is not None:
                    pend()
                nt_next = blocks[bi + 1][1] // 128 if bi + 1 < len(blocks) else 0
                for t in range(max(nt, nt_next)):
                    if t < nt_next:
                        prologue_s2(l, 0, 1 if blocks[bi + 1][2] else 0, t, uTs[(bi + 1) % 2], pb, banks=(6, 7))
                        if t == nt_next - 1 and bi + 2 < len(blocks):
                            loads(bi + 2)
                    if t >= nt:
                        continue
                    v_ = vo[vi % 2]
                    vi += 1
                    for n in range(2):
                        ps = PS[4 + n]
                        for c in range(8):
                            kb.mm(ps, u[:, c, t * 128:(t + 1) * 128], W[:, c, 2048 + n * 512:2048 + (n + 1) * 512],
                                  start=(c == 0), stop=(c == 7))
                        kb.copy('act' if n == 0 else 'dve', v_[:, n * 512:(n + 1) * 512], ps)
                    kb.dma('sp', V(VA[tok0 + t * 128:tok0 + (t + 1) * 128, :]), v_)
            kb.barrier()

        def p2_attn(l):
            j = l // 2
            lam_init = 0.8 - 0.6 * math.exp(-0.3 * l)
            A.reset()
            lamt = A.alloc([256], F32, 'lamt')
            prod = A.alloc([2, 64], F32, 'prod')
            s2 = A.alloc([2], F32, 's2')
            e2 = A.alloc([2], F32, 'e2')
            neglam = A.alloc([1], F32, 'neglam')
            gsub = A.alloc([1], F32, 'gsub')
            sgt = A.alloc([2], F32, 'sgt')
            kb.dma('sp', lamt, V(lam[j].partition_broadcast(128)))
            kb.dma('sp', sgt, V(subg))
            kb.tt('dve', prod[:, 0, :], lamt[:, 0:64], lamt[:, 64:128], ALU.mult)
            kb.tt('dve', prod[:, 1, :], lamt[:, 128:192], lamt[:, 192:256], ALU.mult)
            kb.reduce(s2, prod)
            kb.act(e2, s2, AF.Exp)
            kb.tt('dve', neglam, e2[:, 1:2], e2[:, 0:1], ALU.subtract)
            kb.ts('dve', neglam, neglam, -lam_init, None, ALU.add)
            kb.ts('dve', gsub, sgt[:, j:j + 1], 1.0 - lam_init, None, ALU.mult)
            QTm = [[A.alloc([NTOK], BF16, f'QTm{r_}_{i}') for i in range(2)] for r_ in range(2)]
            KTh = [A.alloc([NTOK], BF16, f'KTh{i}') for i in range(2)]
            Vh = [A.alloc([NT, 128], BF16, f'Vh{i}') for i in range(2)]
            NPT = 6
            LAG = 2
            PT = [A.alloc([2, 512], BF16, f'PT{i}') for i in range(NPT)]
            accD = [A.alloc([2, 512], F32, f'accD{i}') for i in range(4)]
            accP = [A.alloc([2, 512], F32, f'accP{i}') for i in range(4)]
            rec = [A.alloc([512], F32, f'rec{i}') for i in range(4)]
            oa2 = [A.alloc([512], F32, f'oa{i}') for i in range(2)]
            ob2 = [A.alloc([512], F32, f'ob{i}') for i in range(2)]
            oo2 = [A.alloc([512], F32, f'oo{i}') for i in range(2)]
            sq2 = [A.alloc([512], BF16, f'sq{i}') for i in range(2)]
            lnq2 = [A.alloc([512], F32, f'lnq{i}') for i in range(2)]
            rs2 = [A.alloc([512], F32, f'rs{i}') for i in range(2)]
            oT = [A.alloc([512], BF16, f'oT{i}') for i in range(2)]
            qblocks = [(i * 512, 512, list(range(NT))) for i in range(NB)] + [(T, 256, [NTL, NTL + 1])]
            nqb = len(qblocks)
            for r_ in range(2):
                for i in range(2):
                    kb.memset('dve', QTm[r_][i], 0.0)

            def load_head(h):
                for i in range(2):
                    kb.dma('sp', QTm[h % 2][i][64 * i:64 * i + 64, :], V(QT[h * 128 + 64 * i:h * 128 + 64 * i + 64, :]))
                kb.dma('sp', KTh[h % 2], V(KT[h * 128:(h + 1) * 128, :]))
                kb.dma('sp', Vh[h % 2], V(VA[:, h * 128:(h + 1) * 128].rearrange("(t p) d -> p t d", p=128)))

            def pso_bank(g, i):
                return PS[4 + i]

            def den_bank(g, i):
                return PS[6 + (2 * g + i) % 2]

            steps = []
            for h in range(8):
                for qb, (q0, nq, keys) in enumerate(qblocks):
                    for i in range(2):
                        nss = len(keys) // 2
                        for ss in range(nss):
                            steps.append((h, qb, i, ss, keys[2 * ss], keys[2 * ss + 1], q0, nq, nss))
            deferred = []

            def epilogue_a(g, h, q0, nq):
                p1_ = pso_bank(g, 1)
                r1 = rec[(g % 2) * 2 + 1]
                oa, ob_, oo = oa2[g % 2], ob2[g % 2], oo2[g % 2]
                kb.tt('dve', ob_[:, :nq], p1_[:, :nq], r1[:, :nq], ALU.mult)
                kb.stt(oo[:, :nq], ob_[:, :nq], neglam[:, 0:1], oa[:, :nq], ALU.mult, ALU.add)

            def epilogue_b(g, h, q0, nq):
                kb.act(sq2[g % 2][:, :nq], oo2[g % 2][:, :nq], AF.Square)

            def epilogue_c(g, h, q0, nq):
                kb.mm(PS[7][:, :nq], onesb, sq2[g % 2][:, :nq])
                kb.act(lnq2[g % 2][:, :nq], PS[7][:, :nq], AF.Ln, scale=1.0 / 128, bias=epst[:, 0:1])
                kb.act(rs2[g % 2][:, :nq], lnq2[g % 2][:, :nq], AF.Exp, scale=-0.5)

            def epilogue_d(g, h, q0, nq):
                o_ = oT[g % 2]
                kb.stt(o_[:, :nq], oo2[g % 2][:, :nq], gsub[:, 0:1], rs2[g % 2][:, :nq], ALU.mult, ALU.mult)
                kb.dma('sp', V(OT[h * 128:(h + 1) * 128, q0:q0 + nq]), o_[:, :nq])

            load_head(0)
            ns = len(steps)
            s_i = 0
            while s_i < ns + LAG or deferred:
                if s_i < ns:
                    (h, qb, i, ss, kt0, kt1, q0, nq, nss) = steps[s_i]
                    sc_ = ps2(s_i % 2)
                    for jj, kt in enumerate((kt0, kt1)):
                        kb.mm(sc_[:, jj, :nq], KTh[h % 2][:, kt * 128:(kt + 1) * 128], QTm[h % 2][i][:, q0:q0 + nq])
                    kb.act(PT[s_i % NPT][:, :, :nq], sc_[:, :, :nq], AF.Exp, scale=0.125)
                if LAG <= s_i < ns + LAG:
                    (h, qb, i, ss, kt0, kt1, q0, nq, nss) = steps[s_i - LAG]
                    p_ = PT[(s_i - LAG) % NPT]
                    if qb == 0 and i == 0 and ss == 0 and h + 1 < 8:
                        load_head(h + 1)
                        emit_conversions(1.0 / (7 - h))
                    g = h * nqb + qb
                    pso = pso_bank(g, i)
                    dnb = den_bank(g, i)
                    aD = accD[(g % 2) * 2 + i]
                    aP = accP[(g % 2) * 2 + i]
                    for jj, kt in enumerate((kt0, kt1)):
                        kb.mm(pso[:, :nq], Vh[h % 2][:, kt, :], p_[:, jj, :nq],
                              start=(ss == 0 and jj == 0), stop=(ss == nss - 1 and jj == 1))
                    if ss % 4 != 3:
                        if ss == 0:
                            kb.copy('dve', aD[:, :, :nq], p_[:, :, :nq])
                        else:
                            kb.tt('dve', aD[:, :, :nq], aD[:, :, :nq], p_[:, :, :nq], ALU.add)
                    else:
                        for jj in range(2):
                            kb.mm(dnb[:, :nq], onesb, p_[:, jj, :nq], start=(ss == 3 and jj == 0), stop=False)
                    if ss == nss - 1:
                        def den(g=g, i=i, nq=nq, aD=aD, aP=aP, nss=nss, dnb=dnb, pso=pso):
                            parts = [aD[:, 0, :nq], aD[:, 1, :nq]]
                            for pi_, pa in enumerate(parts):
                                kb.mm(dnb[:, :nq], onesf, pa, start=(pi_ == 0 and nss <= 3), stop=(pi_ == len(parts) - 1))
                            kb.recip(rec[(g % 2) * 2 + i][:, :nq], dnb[:, :nq])
                            if i == 0:
                                kb.tt('dve', oa2[g % 2][:, :nq], pso[:, :nq], rec[(g % 2) * 2][:, :nq], ALU.mult)
                        deferred.append((s_i + 1, den))
                        if i == 1:
                            args = (g, h, q0, nq)
                            deferred.append((s_i + 1, lambda a=args: epilogue_a(*a)))
                            deferred.append((s_i + 3, lambda a=args: epilogue_b(*a)))
                            deferred.append((s_i + 5, lambda a=args: epilogue_c(*a)))
                            deferred.append((s_i + 7, lambda a=args: epilogue_d(*a)))
                rest = []
                for due, fn in deferred:
                    if due <= s_i:
                        fn()
                    else:
                        rest.append((due, fn))
                deferred = rest
                s_i += 1
            kb.barrier()

        def ret_consts(j):
            dl = A.alloc([8], F32, 'dl')
            e1 = A.alloc([8], F32, 'e1')
            lg = A.alloc([8], F32, 'lg')
            kb.dma('sp', dl, V(dlog[j].partition_broadcast(128)))
            kb.act(e1, dl, AF.Exp, scale=-1.0)
            kb.ts('dve', e1, e1, 1.0, None, ALU.add)
            kb.act(lg, e1, AF.Ln)
            kb.ts('dve', lg, lg, -1.0, None, ALU.mult)
            return lg

        def p1_ret(l):
            j = l // 2
            A.reset()
            W = A.alloc([8, 6144], BF16, 'W')
            load_w(W, 'win', j, 8)
            lg = ret_consts(j)
            ze = A.alloc([8], F32, 'ze')
            zl = A.alloc([8], F32, 'zl')
            zeta = A.alloc([8], F32, 'zeta')
            kb.dma('sp', ze, V(zec_d))
            kb.tt('dve', zl, ze, lg, ALU.mult)
            kb.act(zeta, zl, AF.Exp)
            pb = prologue_bufs(4)
            hts = [A.alloc([1024], F32, f'ht{i}') for i in range(4)]
            uTs = [A.alloc([8, 512], BF16, f'uT{i}') for i in range(2)]
            tab2 = [[A.alloc([512], F32, f'tab{r_}_{i}') for i in range(4)] for r_ in range(2)]
            qraw = [A.alloc([512], BF16, f'qraw{i}') for i in range(2)]
            t1 = [A.alloc([512], F32, f't1{i}') for i in range(2)]
            t2 = [A.alloc([512], F32, f't2{i}') for i in range(2)]
            qk = A.alloc([16, 512], BF16, 'qkT')
            kzo = [A.alloc([2048], BF16, f'kzo{i}') for i in range(2)]
            vgo = [A.alloc([2048], BF16, f'vgo{i}') for i in range(2)]
            ki_ = 0
            vi = 0

            def loads(bi_):
                tok0_, ntok_, isctx_ = blocks[bi_]
                for t in range(ntok_ // 128):
                    kb.dma('sp', hts[t], hsrc(l, tok0_ + t * 128))
                if not isctx_:
                    for i4 in range(4):
                        kb.dma('sp', tab2[bi_ % 2][i4], V(RT[i4, :, tok0_:tok0_ + 512]))

            def s1(bi_):
                for t in range(blocks[bi_][1] // 128):
                    prologue_s1(hts[t], t, pb)

            def s2(bi_):
                for t in range(blocks[bi_][1] // 128):
                    prologue_s2(l, 0, 1 if blocks[bi_][2] else 0, t, uTs[bi_ % 2], pb, banks=(7, 4))
            loads(0)
            s1(0)
            s2(0)
            if len(blocks) > 1:
                loads(1)
            for bi, (tok0, ntok, isctx) in enumerate(blocks):
                nt = ntok // 128
                tab = tab2[bi % 2]
                u = uTs[bi % 2]
                pend = None
                for jj in range(16):
                    if jj == 6 and bi + 1 < len(blocks):
                        s1(bi + 1)
                    ps = PS[jj % 2]
                    col0 = jj * 128
                    for c in range(8):
                        kb.mm(ps[:, :ntok], W[:, c, col0:col0 + 128], u[:, c, :ntok], start=(c == 0), stop=(c == 7))
                    scale = 1.0 if jj < 8 else 1.0 / 16.0
                    dstv = qk[:, jj, :ntok]
                    dst = (QT if jj < 8 else KT)[(jj % 8) * 128:(jj % 8 + 1) * 128, tok0:tok0 + ntok]
                    if isctx:
                        kb.act(dstv, ps[:, :ntok], AF.Copy, scale=scale)
                        kb.dma('sp', V(dst), dstv)
                    else:
                        par = jj % 2
                        qr = qraw[jj % 2]
                        kb.act(qr, ps, AF.Copy, scale=scale)
                        kb.stt(t1[jj % 2], ps, scale, tab[2 * par], ALU.mult, ALU.mult)

                        def fin(jj=jj, qr=qr, dstv=dstv, dst=dst, par=par):
                            ps2 = PS[2 + jj % 2]
                            kb.mm(ps2, permR, qr)
                            kb.tt('dve', t2[jj % 2], ps2, tab[2 * par + 1], ALU.mult)
                            kb.tt('pool', dstv, t1[jj % 2], t2[jj % 2], ALU.add)
                            kb.dma('sp', V(dst), dstv)
                        if pend is not None:
                            pend()
                        pend = fin
                if pend is not None:
                    pend()
                nt_next = blocks[bi + 1][1] // 128 if bi + 1 < len(blocks) else 0
                for t in range(max(nt, nt_next)):
                    if t < nt:
                        pT = psb(4)
                        for c in range(8):
                            kb.tr(pT[:, c * 128:(c + 1) * 128], qk[:, 8 + c, t * 128:(t + 1) * 128], identb)
                        kz = kzo[ki_ % 2]
                        ki_ += 1
                        for dd in range(2):
                            for hh in range(4):
                                o_ = kz[:, dd * 1024 + hh * 256:dd * 1024 + (hh + 1) * 256]
                                i_ = pT[:, hh * 256:(hh + 1) * 256]
                                z_ = zeta[:, dd * 4 + hh:dd * 4 + hh + 1]
                                if dd == 0:
                                    kb.ts('dve', o_, i_, z_, None, ALU.mult)
                                else:
                                    kb.act(o_, i_, AF.Copy, scale=z_)
                        kb.dma('sp', V(KZ[tok0 + t * 128:tok0 + (t + 1) * 128, :]), kz)
                    if t < nt_next:
                        prologue_s2(l, 0, 1 if blocks[bi + 1][2] else 0, t, uTs[(bi + 1) % 2], pb, banks=(7,))
                        if t == nt_next - 1 and bi + 2 < len(blocks):
                            loads(bi + 2)
                    if t >= nt:
                        continue
                    for part in range(2):
                        vg = vgo[vi % 2]
                        vi += 1
                        for n in range(4):
                            ps = PS[5 + n % 2]
                            c0 = 2048 + part * 2048 + n * 512
                            for c in range(8):
                                kb.mm(ps, u[:, c, t * 128:(t + 1) * 128], W[:, c, c0:c0 + 512], start=(c == 0), stop=(c == 7))
                            if part == 0:
                                kb.copy('act' if n % 2 == 0 else 'dve', vg[:, n * 512:(n + 1) * 512], ps)
                            else:
                                kb.act(vg[:, n * 512:(n + 1) * 512], ps, AF.Silu)
                        kb.dma('sp', V((VR if part == 0 else GS)[tok0 + t * 128:tok0 + (t + 1) * 128, :]), vg)
            kb.barrier()

        def p2_ret(l):
            j = l // 2
            need_ctx = l < DEPTH - 1
            A.reset()
            lg = ret_consts(j)
            rc = A.alloc([6, 128], F32, 'rc')
            kb.dma('sp', rc, V(rcon_d))
            gch = A.alloc([8], F32, 'gch')
            kb.act(gch, lg, AF.Exp, scale=128.0)
            mtmp = A.alloc([128], F32, 'mtmp')
            maskT = [[A.alloc([128], F32, f'mask{d}{hh}') for hh in range(4)] for d in range(2)]
            xi2 = [[A.alloc([2, 128], F32, f'xi{d}{hh}') for hh in range(4)] for d in range(2)]
            for d in range(2):
                for hh in range(4):
                    lcol = lg[:, d * 4 + hh:d * 4 + hh + 1]
                    kb.act(mtmp, rc[:, 2 * d, :], AF.Exp, scale=lcol)
                    kb.tt('dve', maskT[d][hh], mtmp, rc[:, 2 * d + 1, :], ALU.mult)
                    for j2 in range(2):
                        kb.act(xi2[d][hh][:, j2, :], rc[:, 4 + d, :], AF.Exp, scale=lcol)
            S32 = [A.alloc([2, 512], F32, f'S32_{hh}') for hh in range(4)]
            Sbf = [[A.alloc([2, 512], BF16, f'Sbf{hh}_{pp}') for pp in range(2)] for hh in range(4)]
            QTc = [A.alloc([8, 128], BF16, f'QTc{i}') for i in range(2)]
            KTc = [A.alloc([8, 128], BF16, f'KTc{i}') for i in range(2)]
            Kzc = [A.alloc([1024], BF16, f'Kzc{i}') for i in range(2)]
            Vc = [A.alloc([2048], BF16, f'Vc{i}') for i in range(2)]
            obl = [A.alloc([2048], F32, f'obl{i}') for i in range(2)]
            gsl = [A.alloc([2048], BF16, f'gsl{i}') for i in range(2)]
            AT = [A.alloc([128], BF16, f'AT{i}') for i in range(4)]
            Qx = [A.alloc([2, 128], BF16, f'Qx{i}') for i in range(4)]
            obw = [A.alloc([2048], F32, f'obw{i}') for i in range(2)]
            o32s = [A.alloc([2048], F32, f'o32_{i}') for i in range(2)]
            st6 = A.alloc([4, 6], F32, 'st6')
            mvs = [A.alloc([4, 2], F32, f'mv{i}') for i in range(2)]
            lnv4 = A.alloc([4], F32, 'lnv4')
            rs4 = A.alloc([4], F32, 'rs4')
            nmr4 = A.alloc([4], F32, 'nmr4')
            ons = [A.alloc([2048], BF16, f'on{i}') for i in range(2)]
            gos = [A.alloc([2048], BF16, f'go{i}') for i in range(2)]
            goT = [A.alloc([16, 128], BF16, f'goT{i}') for i in range(2)]
            QTv = QT.rearrange("(j p) t -> p j t", p=128)
            KTv = KT.rearrange("(j p) t -> p j t", p=128)
            OTv = OT.rearrange("(j p) t -> p j t", p=128)

            def p2r_loads(d_, order_, ci_):
                cidx_ = order_[ci_]
                isctx_ = cidx_ >= NTL
                do_out_ = (not isctx_) or need_ctx
                last_ = ci_ == len(order_) - 1
                tok0_ = cidx_ * 128
                rb_ = ci_ % 2
                if do_out_:
                    kb.dma('sp', QTc[rb_], V(QTv[:, :, tok0_:tok0_ + 128]))
                    kb.dma('sp', KTc[rb_], V(KTv[:, :, tok0_:tok0_ + 128]))
                if not last_:
                    kb.dma('sp', Kzc[rb_], V(KZ[tok0_:tok0_ + 128, d_ * 1024:(d_ + 1) * 1024]))
                kb.dma('sp', Vc[rb_], V(VR[tok0_:tok0_ + 128, :]))
                if d_ == 0 and do_out_:
                    kb.dma('sp', obl[rb_], V(OB[tok0_:tok0_ + 128, :]))
                    kb.dma('sp', gsl[rb_], V(GS[tok0_:tok0_ + 128, :]))
            for d in (1, 0):
                for hh in range(4):
                    kb.memset('dve', S32[hh], 0.0)
                    kb.memset('pool', Sbf[hh][0], 0.0)
                if d == 0:
                    order = [NTL, NTL + 1] + list(range(NTL))
                else:
                    order = [NTL + 1, NTL] + list(range(NTL - 1, -1, -1))
                pend_tail = [None]
                for ci, cidx in enumerate(order):
                    isctx = cidx >= NTL
                    do_out = (not isctx) or need_ctx
                    last = ci == len(order) - 1
                    tok0 = cidx * 128
                    rb = ci % 2
                    if ci == 0:
                        p2r_loads(d, order, 0)
                    if ci + 1 < len(order):
                        p2r_loads(d, order, ci + 1)
                    pp = ci % 2
                    o32 = o32s[ci % 2]
                    mv = mvs[ci % 2]
                    if do_out:
                        for hh in range(4):
                            pA = PS[hh % 2][:, 0:128]
                            kb.mm(pA, KTc[rb][:, 2 * hh, :], QTc[rb][:, 2 * hh, :], start=True, stop=False)
                            kb.mm(pA, KTc[rb][:, 2 * hh + 1, :], QTc[rb][:, 2 * hh + 1, :], start=False, stop=True)
                            kb.tt('dve', AT[hh], pA, maskT[d][hh], ALU.mult)
                            kb.tt('pool', Qx[hh], QTc[rb][:, 2 * hh:2 * hh + 2, :], xi2[d][hh], ALU.mult)
                    if not last:
                        for hh in range(4):
                            for j2 in range(2):
                                pst = PS[2 + j2]
                                kb.mm(pst, Kzc[rb][:, hh * 256 + j2 * 128:hh * 256 + (j2 + 1) * 128],
                                      Vc[rb][:, hh * 512:(hh + 1) * 512])
                                kb.stt(S32[hh][:, j2, :], S32[hh][:, j2, :], gch[:, d * 4 + hh:d * 4 + hh + 1], pst,
                                       ALU.mult, ALU.add)
                                kb.copy('act', Sbf[hh][1 - pp][:, j2, :], S32[hh][:, j2, :])
                    if not do_out:
                        continue
                    if pend_tail[0] is not None:
                        pend_tail[0]()
                        pend_tail[0] = None
                    ow = obw[ci % 2]
                    for hh in range(4):
                        sl = slice(hh * 512, (hh + 1) * 512)
                        pso = PS[4 + hh % 2]
                        kb.mm(pso, AT[hh], Vc[rb][:, sl], start=True, stop=False)
                        kb.mm(pso, Qx[hh][:, 0, :], Sbf[hh][pp][:, 0, :], start=False, stop=False)
                        kb.mm(pso, Qx[hh][:, 1, :], Sbf[hh][pp][:, 1, :], start=False, stop=True)
                        if d == 1:
                            kb.copy('act' if hh % 2 == 0 else 'dve', ow[:, sl], pso)
                        else:
                            kb.tt('dve', o32[:, sl], pso, obl[rb][:, sl], ALU.add)
                            kb.bn_stats(st6[:, hh, :], o32[:, sl])
                            kb.bn_aggr(mv[:, hh, :], st6[:, hh, :])
                    if d == 1:
                        kb.dma('sp', V(OB[tok0:tok0 + 128, :]), ow)
                    else:
                        on = ons[ci % 2]
                        go = gos[ci % 2]
                        kb.act(lnv4, mv[:, :, 1], AF.Ln, bias=epst[:, 1:2])
                        kb.act(rs4, lnv4, AF.Exp, scale=-0.5)
                        kb.stt(nmr4, mv[:, :, 0], -1.0, rs4, ALU.mult, ALU.mult)
                        for hh in range(4):
                            sl = slice(hh * 512, (hh + 1) * 512)
                            kb.act(on[:, sl], o32[:, sl], AF.Identity, scale=rs4[:, hh:hh + 1], bias=nmr4[:, hh:hh + 1])
                        kb.tt('pool', go, on, gsl[rb], ALU.mult)

                        def tail(go=go, gt=goT[ci % 2], tok0=tok0):
                            for half in range(2):
                                pT = psb(6 + half)
                                for c in range(8):
                                    kc = half * 8 + c
                                    kb.tr(pT[:, c * 128:(c + 1) * 128], go[:, kc * 128:(kc + 1) * 128], identb)
                                kb.copy('act' if half == 0 else 'dve', gt[:, half * 8:(half + 1) * 8, :],
                                        pT.rr("p (c t) -> p c t", t=128))
                            kb.dma('sp', V(OTv[:, :, tok0:tok0 + 128]), gt)
                        pend_tail[0] = tail
                if pend_tail[0] is not None:
                    pend_tail[0]()
                    pend_tail[0] = None
                kb.barrier()

        def p3a(l):
            j = l // 2
            is_attn = (l % 2 == 0)
            need_ctx = l < DEPTH - 1
            KC = 8 if is_attn else 16
            A.reset()
            W1pre = A.alloc([8, 4096], BF16, 'W1pre')
            Wo = A.alloc([KC, 1024], BF16, 'Wo')
            load_w(Wo, 'wo_a' if is_attn else 'wo_r', j, KC)
            gates = [A.alloc([1024], F32, f'gate{b}') for b in range(2)]
            for b in range(2):
                kb.dma('sp', gates[b], V(modrow[l, b, 2048:3072].partition_broadcast(128)))
            OTb = [A.alloc([KC, 512], BF16, f'OTb{i}') for i in range(2)]
            hts = [A.alloc([1024], F32, f'ht{i}') for i in range(8)]
            tmp = [A.alloc([512], F32, f'tmp{i}') for i in range(2)]
            pb = prologue_bufs()
            uTs = [A.alloc([8, 512], BF16, f'uT{i}') for i in range(2)]
            OTv = OT.rearrange("(j p) t -> p j t", p=128)
            UTv = UT.rearrange("(j p) t -> p j t", p=128)
            ti = 0
            pi = 0
            blks = [bk for bk in blocks if not (bk[2] and not need_ctx)]

            def p3a_loads(bi_):
                tok0_, ntok_, isctx_ = blks[bi_]
                kb.dma('sp', OTb[bi_ % 2][:, :, :ntok_], V(OTv[:, 0:KC, tok0_:tok0_ + ntok_]))
                for t in range(ntok_ // 128):
                    kb.dma('sp', hts[(bi_ % 2) * 4 + t], hsrc(l, tok0_ + t * 128))
            p3a_loads(0)
            pend = [None]
            for bi, (tok0, ntok, isctx) in enumerate(blks):
                nt = ntok // 128
                b = 1 if isctx else 0
                ob = OTb[bi % 2]
                u = uTs[bi % 2]
                if bi + 1 < len(blks):
                    p3a_loads(bi + 1)
                if bi == 0:
                    load_w(W1pre, 'w1', l, 8)
                for t in range(nt):
                    ht = hts[(bi % 2) * 4 + t]
                    for n in range(2):
                        ps = PS[pi % 4]
                        pi += 1
                        for kc in range(KC):
                            kb.mm(ps, ob[:, kc, t * 128:(t + 1) * 128], Wo[:, kc, n * 512:(n + 1) * 512],
                                  start=(kc == 0), stop=(kc == KC - 1))
                        tm = tmp[ti % 2]
                        ti += 1
                        sl = slice(n * 512, (n + 1) * 512)
                        kb.tt('dve', tm, ps, gates[b][:, sl], ALU.mult)
                        kb.tt('pool', ht[:, sl], tm, ht[:, sl], ALU.add)
                    kb.dma('sp', V(hbuf[tok0 + t * 128:tok0 + (t + 1) * 128, :]), ht)
                    prologue_s1(ht, t, pb)
                    if pend[0] is not None:
                        pend[0]()

                    def s2(t=t, b=b, u=u, last=(t == nt - 1), tok0=tok0, ntok=ntok):
                        prologue_s2(l, 1, b, t, u, pb, banks=(6, 7))
                        if last:
                            kb.dma('sp', V(UTv[:, :, tok0:tok0 + ntok]), u[:, :, :ntok])
                    pend[0] = s2
            if pend[0] is not None:
                pend[0]()
            kb.barrier()

        def p3b(l):
            need_ctx = l < DEPTH - 1
            last = (l == DEPTH - 1)
            A.reset()
            W1 = A.alloc([8, 4096], BF16, 'W1')
            W2 = A.alloc([32, 1024], BF16, 'W2')
            load_w(W2, 'w2', l, 32)
            gate = A.alloc([1024], F32, 'gate')
            kb.dma('sp', gate, V(modrow[l, 0, 5120:6144].partition_broadcast(128)))
            hid = A.alloc([32, 512], BF16, 'hid')
            uTs = [A.alloc([8, 512], BF16, f'uT{i}') for i in range(2)]
            hts = [A.alloc([1024], F32, f'ht{i}') for i in range(4)]
            rl = [A.alloc([512], BF16, f'rl{i}') for i in range(1 if last else 2)]
            tmp = A.alloc([512], F32, 'tmp')
            if last:
                fg = A.alloc([1024], F32, 'fg')
                kb.dma('sp', fg, V(fng.partition_broadcast(128)))
                ssq = A.alloc([4], F32, 'ssq')
                lnv = A.alloc([4], F32, 'lnv')
                rstd = A.alloc([4], F32, 'rstd')
                junk = tmp.bc(BF16)
            UTv = UT.rearrange("(j p) t -> p j t", p=128)
            pi = 0
            blks = [bk for bk in blocks if not (bk[2] and not need_ctx)]

            def p3b_uload(bi_):
                tok0_, ntok_, isctx_ = blks[bi_]
                kb.dma('sp', uTs[bi_ % 2][:, :, :ntok_], V(UTv[:, :, tok0_:tok0_ + ntok_]))
            p3b_uload(0)
            for bi, (tok0, ntok, isctx) in enumerate(blks):
                nt = ntok // 128
                u = uTs[bi % 2]
                if bi + 1 < len(blks):
                    p3b_uload(bi + 1)
                if isctx:
                    kb.dma('sp', gate, V(modrow[l, 1, 5120:6144].partition_broadcast(128)))
                for t in range(nt):
                    kb.dma('sp', hts[t], V(hbuf[tok0 + t * 128:tok0 + (t + 1) * 128, :]))
                for f in range(32):
                    ps = PS[f % 3]
                    for c in range(8):
                        kb.mm(ps[:, :ntok], W1[:, c, f * 128:(f + 1) * 128], u[:, c, :ntok], start=(c == 0), stop=(c == 7))
                    r_ = rl[f % len(rl)]
                    kb.act(r_[:, :ntok], ps[:, :ntok], AF.Relu)
                    kb.tt('pool', hid[:, f, :ntok], r_[:, :ntok], r_[:, :ntok], ALU.mult)
                for t in range(nt):
                    ht = hts[t]
                    for n in range(2):
                        ps = PS[3 + pi % 4]
                        pi += 1
                        for f in range(32):
                            kb.mm(ps, hid[:, f, t * 128:(t + 1) * 128], W2[:, f, n * 512:(n + 1) * 512],
                                  start=(f == 0), stop=(f == 31))
                        sl = slice(n * 512, (n + 1) * 512)
                        kb.tt('dve', tmp, ps, gate[:, sl], ALU.mult)
                        kb.tt('dve', ht[:, sl], tmp, ht[:, sl], ALU.add)
                    if last:
                        kb.act(junk, ht, AF.Square, accum=ssq[:, t:t + 1])
                        kb.act(lnv[:, t:t + 1], ssq[:, t:t + 1], AF.Ln, scale=1.0 / D, bias=epst[:, 0:1])
                        kb.act(rstd[:, t:t + 1], lnv[:, t:t + 1], AF.Exp, scale=-0.5)
                        kb.stt(ht, ht, rstd[:, t:t + 1], fg, ALU.mult, ALU.mult)
                        kb.dma('sp', V(out[tok0 + t * 128:tok0 + (t + 1) * 128, :]), ht)
                    else:
                        kb.dma('sp', V(hbuf[tok0 + t * 128:tok0 + (t + 1) * 128, :]), ht)
            kb.barrier()

        def convert_layer(l, defer=False):
            j = l // 2
            if l % 2 == 0:
                convert_w('wqkv', j, defer)
                convert_w('wo_a', j, defer)
            else:
                convert_w('win', j, defer)
                convert_w('wo_r', j, defer)
            convert_w('w1', l, defer)
            convert_w('w2', l, defer)
        convert_layer(layers[0])
        p0()
        for l in layers:
            if l % 2 == 0:
                p1_attn(l)
                if stop_after == ('p1', l):
                    break
                if l == layers[0]:
                    for l2 in layers[1:]:
                        convert_layer(l2, defer=True)
                p2_attn(l)
                emit_conversions(1.0)
            else:
                p1_ret(l)
                if stop_after == ('p1', l):
                    break
                p2_ret(l)
            if stop_after == ('p2', l):
                break
            p3a(l)
            if stop_after == ('p3a', l):
                break
            p3b(l)
            if stop_after == ('p3b', l):
                break
        block = stack.enter_context(nc.Block())
        S.emit(nc, stack, block)
    return nc


def tile_w(w, kc):
    n = w.shape[-1]
    return np.ascontiguousarray(w.reshape(kc, 128, n).transpose(1, 0, 2))


def make_in_maps(inputs, T, nb):
    f = lambda a: np.ascontiguousarray(np.asarray(a, dtype=np.float32))
    hc = host_consts(T)
    shared = dict(hc)
    shared['cctxT'] = np.ascontiguousarray(f(inputs['c_ctx']).reshape(8, 128).T)
    shared['ada_w'] = np.stack([tile_w(f(inputs['ada_w'][i]), 8) for i in range(4)])
    shared['ada_b'] = f(inputs['ada_b'])
    shared['wqkv'] = np.stack([tile_w(f(inputs['attn_w_qkv'][i]), 8) for i in range(2)])
    shared['wo_a'] = np.stack([tile_w(f(inputs['attn_w_o'][i]), 8) for i in range(2)])
    shared['lam'] = f(inputs['attn_lambda']).reshape(2, 256)
    shared['subg'] = np.ascontiguousarray(f(inputs['attn_subln_g']).T)
    shared['win'] = np.stack([tile_w(f(inputs['ret_w_in'][i]), 8) for i in range(2)])
    shared['wo_r'] = np.stack([tile_w(f(inputs['ret_w_o'][i]), 16) for i in range(2)])
    shared['dlog'] = f(inputs['ret_decay_logit']).reshape(2, 8)
    shared['w1'] = np.stack([tile_w(f(inputs['mlp_w1'][i]), 8) for i in range(4)])
    shared['w2'] = np.stack([tile_w(f(inputs['mlp_w2'][i]), 32) for i in range(4)])
    shared['fng'] = f(inputs['final_norm_g'])
    xs = f(inputs['x'])
    cs = f(inputs['c'])
    cx = f(inputs['ctx'])
    maps = []
    for b in range(nb):
        m = dict(shared)
        m['x'] = np.ascontiguousarray(xs[b, :T])
        m['ctx'] = np.ascontiguousarray(cx[b])
        m['cT'] = np.ascontiguousarray(cs[b].reshape(8, 128).T)
        maps.append(m)
    return maps


def kernel(**inputs):
    T = 4096
    nb = 8
    nc = build(T)
    maps = make_in_maps(inputs, T, nb)
    res = run_bass_kernel_spmd(nc, maps, core_ids=list(range(nb)))
    return np.stack([np.asarray(res.results[b]["out"], dtype=np.float32) for b in range(nb)], 0)
```

```python
import math
from contextlib import ExitStack

import numpy as np
import ml_dtypes
import concourse.bass as bass
import concourse.mybir as mybir
from concourse.bass_utils import run_bass_kernel_spmd

F32 = mybir.dt.float32
BF16 = mybir.dt.bfloat16
U8 = mybir.dt.uint8
AF = mybir.ActivationFunctionType
ALU = mybir.AluOpType
AX = mybir.AxisListType

D = 1024
LC = 256
DEPTH = 4
EPS = 1e-6
ENG = ('pe', 'act', 'dve', 'pool', 'sp')
HANDLE = {'pe': 'tensor', 'act': 'scalar', 'dve': 'vector', 'pool': 'gpsimd', 'sp': 'sync'}
CH = 16000
NDSEM = {'sp': 24, 'pool': 48, 'act': 2}
SAME_SYNC = {'act', 'dve', 'pool'}
ARENA_BYTES = 204 * 1024
TRUNC = 0


class V:
    __slots__ = ('ap', 'key')

    def __init__(self, ap, key=None):
        self.ap = ap
        self.key = key

    def __getitem__(self, idx):
        return V(self.ap[idx], self.key)

    def rr(self, pat, **kw):
        return V(self.ap.rearrange(pat, **kw), self.key)

    def bc(self, dt):
        return V(self.ap.bitcast(dt), self.key)


class Op:
    __slots__ = ('eng', 'fn', 'dma', 'deps', 'sig', 'signo', 'dsem', 'dval', 'qidx')

    def __init__(self, eng, fn, dma):
        self.eng = eng
        self.fn = fn
        self.dma = dma
        self.deps = {}
        self.sig = False
        self.signo = 0
        self.dsem = None
        self.dval = 0
        self.qidx = 0


class Sched:
    def __init__(self):
        self.ops = {e: [] for e in ENG}
        self.lastw = {}
        self.rd = {}
        self.pending_dmas = []
        self.ndma = {e: 0 for e in ENG}
        self.bg = {}

    def add(self, eng, fn, r=(), w=(), dma=False, bg=False):
        op = Op(eng, fn, dma)
        deps = op.deps
        if bg:
            for x in w:
                self.bg.setdefault(x, []).append(op)
            op.qidx = self.ndma[eng]
            self.ndma[eng] += 1
            self.ops[eng].append(op)
            return op
        for x in r:
            for o in self.bg.get(x, ()):
                deps[o] = 'raw'
        w = list(w) + [x for x in r if isinstance(x, tuple) and x and x[0] == 'PS' and x not in w]

        def dep(o, kind):
            if o is op:
                return
            if kind == 'raw' or o not in deps:
                deps[o] = kind
        for x in r:
            if x is None:
                continue
            o = self.lastw.get(x)
            if o is not None:
                dep(o, 'raw')
        for x in w:
            if x is None:
                continue
            o = self.lastw.get(x)
            if o is not None:
                dep(o, 'waw')
            rdx = self.rd.get(x)
            if rdx:
                for k, o2 in rdx.items():
                    if k == 'dma':
                        for o3 in o2:
                            dep(o3, 'war')
                    else:
                        dep(o2, 'war')
        for x in r:
            if x is None:
                continue
            rdx = self.rd.setdefault(x, {})
            if dma:
                rdx.setdefault('dma', []).append(op)
            else:
                rdx[eng] = op
        for x in w:
            if x is None:
                continue
            self.lastw[x] = op
            self.rd[x] = {}
        if dma:
            op.qidx = self.ndma[eng]
            self.ndma[eng] += 1
            self.pending_dmas.append(op)
        self.ops[eng].append(op)
        return op

    def barrier(self):
        b = Op('sp', lambda e: e.nop(), False)
        for e in ENG:
            for o in reversed(self.ops[e]):
                if not o.dma:
                    b.deps[o] = 'raw'
                    break
        for o in self.pending_dmas:
            b.deps[o] = 'raw'
        self.pending_dmas = []
        self.ops['sp'].append(b)
        for e in ENG:
            if e == 'sp':
                continue
            m = Op(e, lambda en: en.nop(), False)
            m.deps[b] = 'raw'
            self.ops[e].append(m)
        self.lastw = {}
        self.rd = {}

    @staticmethod
    def _needs(op, d, kind):
        if d.dma:
            return True
        if op.dma:
            return True
        if d.eng != op.eng:
            return True
        return op.eng in SAME_SYNC

    def finalize(self):
        for e in ENG:
            for op in self.ops[e]:
                for d, kind in op.deps.items():
                    if not d.dma and self._needs(op, d, kind):
                        d.sig = True
        self.nsig = {}
        for e in ENG:
            n = 0
            for op in self.ops[e]:
                if op.sig:
                    n += 1
                    op.signo = n
            self.nsig[e] = n

    def emit(self, nc, stack, block):
        self.finalize()
        csem = {}
        for e in ENG:
            nep = max(1, (self.nsig[e] + CH - 1) // CH)
            csem[e] = [stack.enter_context(nc.semaphore(f"c_{e}_{i}")) for i in range(nep)]
        dsem = {}
        for e, n in NDSEM.items():
            dsem[e] = [stack.enter_context(nc.semaphore(f"d_{e}_{i}")) for i in range(n)]
        for e in ENG:
            for op in self.ops[e]:
                if op.dma:
                    k = NDSEM[e]
                    op.dsem = dsem[e][op.qidx % k]
                    op.dval = 16 * (op.qidx // k + 1)
        sched = self

        def make_body(eng):
            def body(e):
                known = {f: 0 for f in ENG}
                knownd = {}
                for op in sched.ops[eng]:
                    waits = []
                    for d, kind in op.deps.items():
                        if not sched._needs(op, d, kind):
                            continue
                        if d.dma:
                            kk = id(d.dsem)
                            if knownd.get(kk, 0) < d.dval:
                                knownd[kk] = d.dval
                                waits.append((d.dsem, d.dval))
                        else:
                            n = d.signo
                            if known[d.eng] >= n:
                                continue
                            known[d.eng] = n
                            waits.append((csem[d.eng][(n - 1) // CH], (n - 1) % CH + 1))
                    if op.dma and op.dval > 16:
                        kk = id(op.dsem)
                        if knownd.get(kk, 0) < op.dval - 16:
                            knownd[kk] = op.dval - 16
                            waits.append((op.dsem, op.dval - 16))
                    best = {}
                    for sm, v in waits:
                        kk = id(sm)
                        if kk not in best or best[kk][1] < v:
                            best[kk] = (sm, v)
                    for sm, v in best.values():
                        e.wait_ge(sm, v)
                    ins = op.fn(e)
                    if op.dma:
                        ins.then_inc(op.dsem, 16)
                    elif op.sig:
                        n = op.signo
                        ins.then_inc(csem[eng][(n - 1) // CH], 1)
            return body
        for eng in ENG:
            getattr(block, HANDLE[eng])(make_body(eng))


class Arena:
    def __init__(self, ap_u8, nbytes):
        self.ap = ap_u8
        self.n = nbytes
        self.off = 0
        self.gen = 0

    def reset(self):
        self.off = 0
        self.gen += 1

    def alloc(self, free_shape, dt, name):
        esz = 2 if dt == BF16 else 4
        n = esz
        for s in free_shape:
            n *= s
        n_al = (n + 63) // 64 * 64
        assert self.off + n_al <= self.n, f"arena overflow at {name}: {self.off}+{n_al} > {self.n}"
        ap = self.ap[:, self.off:self.off + n].bitcast(dt)
        self.off += n_al
        if len(free_shape) == 2:
            ap = ap.rearrange("p (a b) -> p a b", b=free_shape[1])
        elif len(free_shape) == 3:
            ap = ap.rearrange("p (a b c) -> p a b c", b=free_shape[1], c=free_shape[2])
        return V(ap, (name, self.gen))


class K:
    def __init__(self, nc, S):
        self.nc = nc
        self.S = S

    @staticmethod
    def _keys(*vs):
        out = []
        for v in vs:
            if isinstance(v, V) and v.key is not None:
                if isinstance(v.key, list):
                    out.extend(v.key)
                else:
                    out.append(v.key)
        return out

    def mm(self, out, lhsT, rhs, start=True, stop=True):
        o, a, b = out.ap, lhsT.ap, rhs.ap
        self.S.add('pe', lambda e: e.matmul(o, lhsT=a, rhs=b, start=start, stop=stop),
                   r=self._keys(lhsT, rhs), w=self._keys(out))

    def tr(self, out, in_, ident):
        o, a, b = out.ap, in_.ap, ident.ap
        self.S.add('pe', lambda e: e.transpose(o, a, b), r=self._keys(in_, ident), w=self._keys(out))

    def act(self, out, in_, func, scale=None, bias=None, accum=None):
        o, a = out.ap, in_.ap
        kw = {}
        rk = [in_]
        wk = [out]
        if scale is not None:
            if isinstance(scale, V):
                kw['scale'] = scale.ap
                rk.append(scale)
            else:
                kw['scale'] = float(scale)
        if bias is not None:
            if isinstance(bias, V):
                kw['bias'] = bias.ap
                rk.append(bias)
            else:
                kw['bias'] = float(bias)
        if accum is not None:
            kw['accum_out'] = accum.ap
            wk.append(accum)
        self.S.add('act', lambda e: e.activation(o, a, func, **kw), r=self._keys(*rk), w=self._keys(*wk))

    def ts(self, eng, out, in0, s1, s2=None, op0=ALU.mult, op1=None):
        o, a = out.ap, in0.ap
        rk = [in0]
        if isinstance(s1, V):
            rk.append(s1)
            s1 = s1.ap
        if isinstance(s2, V):
            rk.append(s2)
            s2 = s2.ap
        if op1 is None:
            fn = lambda e: e.tensor_scalar(o, a, s1, None, op0)
        else:
            fn = lambda e: e.tensor_scalar(o, a, s1, s2, op0, op1)
        self.S.add(eng, fn, r=self._keys(*rk), w=self._keys(out))

    def tt(self, eng, out, in0, in1, op):
        o, a, b = out.ap, in0.ap, in1.ap
        self.S.add(eng, lambda e: e.tensor_tensor(o, a, b, op), r=self._keys(in0, in1), w=self._keys(out))

    def stt(self, out, in0, scalar, in1, op0, op1):
        o, a, b = out.ap, in0.ap, in1.ap
        rk = [in0, in1]
        if isinstance(scalar, V):
            rk.append(scalar)
            scalar = scalar.ap
        self.S.add('dve', lambda e: e.scalar_tensor_tensor(o, a, scalar, b, op0, op1),
                   r=self._keys(*rk), w=self._keys(out))

    def copy(self, eng, out, in_):
        o, a = out.ap, in_.ap
        if eng == 'act':
            fn = lambda e: e.copy(o, a)
        else:
            fn = lambda e: e.tensor_copy(o, a)
        self.S.add(eng, fn, r=self._keys(in_), w=self._keys(out))

    def recip(self, out, in_):
        o, a = out.ap, in_.ap
        self.S.add('dve', lambda e: e.reciprocal(o, a), r=self._keys(in_), w=self._keys(out))

    def memset(self, eng, out, val):
        o = out.ap
        self.S.add(eng, lambda e: e.memset(o, val), w=self._keys(out))

    def reduce(self, out, in_, op=ALU.add):
        o, a = out.ap, in_.ap
        self.S.add('dve', lambda e: e.tensor_reduce(o, a, AX.X, op), r=self._keys(in_), w=self._keys(out))

    def bn_stats(self, out, in_):
        o, a = out.ap, in_.ap
        self.S.add('dve', lambda e: e.bn_stats(o, a), r=self._keys(in_), w=self._keys(out))

    def bn_aggr(self, out, in_):
        o, a = out.ap, in_.ap
        self.S.add('dve', lambda e: e.bn_aggr(o, a), r=self._keys(in_), w=self._keys(out))

    def dma(self, q, out, in_, bg=False):
        o, a = out.ap, in_.ap
        self.S.add(q, lambda e: e.dma_start(out=o, in_=a), r=self._keys(in_), w=self._keys(out), dma=True, bg=bg)

    def barrier(self):
        self.S.barrier()


def host_consts(T):
    t = np.arange(T)
    row = (t // 64).astype(np.float32)
    col = (t % 64).astype(np.float32)
    p = np.arange(128)
    m = p % 64
    quarter, half = 16, 32
    inv = (np.float32(10000.0) ** (-(np.arange(quarter, dtype=np.float32)) / np.float32(quarter))).astype(np.float32)
    isrow = m < half
    mloc = np.where(isrow, m, m - half)
    f = mloc % quarter
    first = mloc < quarter
    pos = np.where(isrow[:, None], row[None, :], col[None, :]).astype(np.float32)
    ang = (pos * inv[f][:, None]).astype(np.float32)
    sgn = np.where(first, -1.0, 1.0)[:, None]
    AC = np.cos(ang.astype(np.float64)).astype(np.float32)
    AS = (np.sin(ang.astype(np.float64)) * sgn).astype(np.float32)
    partner = np.where(first, p + quarter, p - quarter)
    permA = np.zeros((128, 128), np.float32)
    permA[partner, p] = 1.0
    quarter = 64
    inv = (np.float32(10000.0) ** (-(np.arange(quarter, dtype=np.float32)) / np.float32(quarter))).astype(np.float32)
    f = p % 64
    first = p < 64
    sgn = np.where(first, -1.0, 1.0)[:, None]
    angr = (row[None, :] * inv[f][:, None]).astype(np.float32).astype(np.float64)
    angc = (col[None, :] * inv[f][:, None]).astype(np.float32).astype(np.float64)
    RT = np.stack([np.cos(angr), np.sin(angr) * sgn, np.cos(angc), np.sin(angc) * sgn], 0).astype(np.float32)
    partner = np.where(first, p + 64, p - 64)
    permR = np.zeros((128, 128), np.float32)
    permR[partner, p] = 1.0
    k = np.arange(128)[:, None].astype(np.float32)
    q = np.arange(128)[None, :].astype(np.float32)
    rc = np.stack([np.maximum(q - k, 0), (q >= k).astype(np.float32),
                   np.maximum(k - q, 0), (k > q).astype(np.float32),
                   np.broadcast_to(q + 1, (128, 128)), np.broadcast_to(128 - q, (128, 128))], 1).astype(np.float32)
    ze = np.zeros((128, 8), np.float32)
    ze[:, 0:4] = (127 - np.arange(128))[:, None]
    ze[:, 4:8] = np.arange(128)[:, None]
    bf = ml_dtypes.bfloat16
    return dict(AC=AC, AS=AS, RT=np.ascontiguousarray(RT), permA=permA.astype(bf), permR=permR.astype(bf),
                rcon=np.ascontiguousarray(rc), zec=ze, identb=np.eye(128, dtype=np.float32).astype(bf),
                identf=np.eye(128, dtype=np.float32))


def build(T, layers=(0, 1, 2, 3), dbg=(), stop_after=None):
    NB = T // 512
    NTL = T // 128
    NTOK = T + LC
    NT = NTOK // 128
    nc = bass.Bass("TRN2", target_bir_lowering=False)
    S = Sched()
    kb = K(nc, S)

    def din(name, shape, dt=F32):
        return nc.dram_tensor(name, shape, dt, kind="ExternalInput").ap()

    def dscr(name, shape, dt):
        kind = "ExternalOutput" if name in dbg else "Internal"
        return nc.dram_tensor(name, shape, dt, kind=kind).ap()

    x = din("x", [T, D])
    ctx = din("ctx", [LC, D])
    cT = din("cT", [128, 8])
    cctxT = din("cctxT", [128, 8])
    ada_w = din("ada_w", [4, 128, 8, 6144])
    ada_b = din("ada_b", [4, 6144])
    wqkv = din("wqkv", [2, 128, 8, 3072])
    wo_a = din("wo_a", [2, 128, 8, 1024])
    lam = din("lam", [2, 256])
    subg = din("subg", [128, 2])
    win = din("win", [2, 128, 8, 6144])
    wo_r = din("wo_r", [2, 128, 16, 1024])
    dlog = din("dlog", [2, 8])
    w1 = din("w1", [4, 128, 8, 4096])
    w2 = din("w2", [4, 128, 32, 1024])
    fng = din("fng", [1024])
    AC = din("AC", [128, T])
    AS = din("AS", [128, T])
    RT = din("RT", [4, 128, T])
    permA_d = din("permA", [128, 128], BF16)
    permR_d = din("permR", [128, 128], BF16)
    rcon_d = din("rcon", [128, 6, 128])
    zec_d = din("zec", [128, 8])
    identb_d = din("identb", [128, 128], BF16)
    identf_d = din("identf", [128, 128])
    out = nc.dram_tensor("out", [T, D], F32, kind="ExternalOutput").ap()

    hbuf = dscr("hbuf", [NTOK, D], F32)
    QT = dscr("QT", [1024, NTOK], BF16)
    KT = dscr("KT", [1024, NTOK], BF16)
    VA = dscr("VA", [NTOK, 1024], BF16)
    VR = dscr("VR", [NTOK, 2048], BF16)
    GS = dscr("GS", [NTOK, 2048], BF16)
    KZ = dscr("KZ", [NTOK, 2048], BF16)
    OB = dscr("OB", [NTOK, 2048], F32)
    OT = dscr("OT", [2048, NTOK], BF16)
    modrow = dscr("modrow", [4, 2, 6144], F32)
    UT = dscr("UT", [1024, NTOK], BF16)
    wsrc = {'wqkv': (wqkv, [2, 128, 8, 3072]), 'wo_a': (wo_a, [2, 128, 8, 1024]), 'win': (win, [2, 128, 8, 6144]),
            'wo_r': (wo_r, [2, 128, 16, 1024]), 'w1': (w1, [4, 128, 8, 4096]), 'w2': (w2, [4, 128, 32, 1024])}
    wbf = {k_: nc.dram_tensor(k_ + "_bf", shp, BF16, kind="Internal").ap() for k_, (_, shp) in wsrc.items()}

    stack = ExitStack()
    with stack:
        arena_t = stack.enter_context(nc.sbuf_tensor("arena", [128, ARENA_BYTES], U8))
        A = Arena(arena_t[:], ARENA_BYTES)

        def pers(name, shape, dt):
            tt_ = stack.enter_context(nc.sbuf_tensor(name, [128] + shape, dt))
            return V(tt_[:], name)
        modT = pers("modT", [4, 2, 48], F32)
        identb = pers("identb_s", [128], BF16)
        identf = pers("identf_s", [128], F32)
        onesb = pers("onesb", [128], BF16)
        onesf = pers("onesf", [128], F32)
        permA = pers("permA_s", [128], BF16)
        permR = pers("permR_s", [128], BF16)
        epst = pers("epst", [2], F32)
        psall_t = stack.enter_context(nc.psum_tensor("psall", [128, 8, 512], F32))
        psall = psall_t[:]
        PS = [V(psall[:, i, :], ('PS', i)) for i in range(8)]

        def ps2(j):
            return V(psall[:, 2 * j:2 * j + 2, :], [('PS', 2 * j), ('PS', 2 * j + 1)])

        def psb(i):
            return PS[i].bc(BF16)

        kb.dma('sp', identb, V(identb_d))
        kb.dma('sp', identf, V(identf_d))
        kb.dma('sp', permA, V(permA_d))
        kb.dma('sp', permR, V(permR_d))
        kb.memset('dve', onesb, 1.0)
        kb.memset('dve', onesf, 1.0)
        kb.memset('dve', epst[:, 0:1], EPS)
        kb.memset('dve', epst[:, 1:2], 1e-5)

        blocks = [(i * 512, 512, False) for i in range(NB)] + [(T, 256, True)]

        def hsrc(l, tok0, n=128):
            if l == layers[0] and l == 0:
                if tok0 >= T:
                    return V(ctx[tok0 - T:tok0 - T + n, :])
                return V(x[tok0:tok0 + n, :])
            return V(hbuf[tok0:tok0 + n, :])

        def p0():
            A.reset()
            cc0 = A.alloc([8], F32, 'cc0')
            cc1 = A.alloc([8], F32, 'cc1')
            sc = A.alloc([8, 2], F32, 'sc')
            adar = A.alloc([6144], F32, 'adar')
            mrows = [A.alloc([6144], F32, f'mrows{i}') for i in range(2)]
            pieces = [A.alloc([8, 512], F32, f'piece{i}') for i in range(4)]
            kb.dma('sp', cc0, V(cT))
            kb.dma('sp', cc1, V(cctxT))
            kb.act(sc[:, :, 0], cc0, AF.Silu)
            kb.act(sc[:, :, 1], cc1, AF.Silu)
            pi = 0
            for l in range(4):
                mr = mrows[l % 2]
                kb.dma('sp', adar[0:2, :], V(ada_b[l].partition_broadcast(2)))
                for n0 in range(12):
                    pc = pieces[pi % 4]
                    pi += 1
                    kb.dma('sp', pc, V(ada_w[l, :, :, n0 * 512:(n0 + 1) * 512]))
                    ps = PS[n0 % 2]
                    for k in range(8):
                        kb.mm(ps[0:2, :], sc[:, k, :], pc[:, k, :], start=(k == 0), stop=(k == 7))
                    kb.tt('dve', mr[0:2, n0 * 512:(n0 + 1) * 512], ps[0:2, :], adar[0:2, n0 * 512:(n0 + 1) * 512], ALU.add)
                kb.dma('sp', V(modrow[l]), mr[0:2, :])
                for ci in range(48):
                    kb.tr(PS[2][:, ci:ci + 49:48], mr[0:2, ci * 128:(ci + 1) * 128], identf[0:2, 0:2])
                kb.copy('act', modT[:, l].rr("p b c -> p (b c)"), PS[2][:, 0:96])
                for jm in (1, 4):
                    kb.ts('dve', modT[:, l, :, jm * 8:(jm + 1) * 8], modT[:, l, :, jm * 8:(jm + 1) * 8], 1.0, None, ALU.add)
            kb.barrier()

        def prologue_s1(ht, t, bufs):
            junk, ssq, lnv, rstd, xns = bufs
            xn = xns[t % len(xns)]
            kb.act(junk, ht, AF.Square, accum=ssq[:, t:t + 1])
            kb.act(lnv[:, t:t + 1], ssq[:, t:t + 1], AF.Ln, scale=1.0 / D, bias=epst[:, 0:1])
            kb.act(rstd[:, t:t + 1], lnv[:, t:t + 1], AF.Exp, scale=-0.5)
            kb.ts('dve', xn, ht, rstd[:, t:t + 1], None, ALU.mult)

        def prologue_s2(l, a, b, t, u, bufs, banks=(7,)):
            junk, ssq, lnv, rstd, xns = bufs
            xn = xns[t % len(xns)]
            pT = psb(banks[t % len(banks)])
            for c in range(8):
                kb.tr(pT[:, c * 128:(c + 1) * 128], xn[:, c * 128:(c + 1) * 128], identb)
            for c in range(8):
                shift = modT[:, l, b, (3 * a) * 8 + c:(3 * a) * 8 + c + 1]
                scl = modT[:, l, b, (3 * a + 1) * 8 + c:(3 * a + 1) * 8 + c + 1]
                if c % 2 == 0:
                    kb.ts('dve', u[:, c, t * 128:(t + 1) * 128], pT[:, c * 128:(c + 1) * 128], scl, shift, ALU.mult, ALU.add)
                else:
                    kb.act(u[:, c, t * 128:(t + 1) * 128], pT[:, c * 128:(c + 1) * 128], AF.Identity, scale=scl, bias=shift)

        def prologue(l, a, b, ht, t, u, bufs):
            prologue_s1(ht, t, bufs)
            prologue_s2(l, a, b, t, u, bufs)

        def prologue_bufs(nx=2):
            junk = A.alloc([1024], BF16, 'junk')
            ssq = A.alloc([4], F32, 'ssq')
            lnv = A.alloc([4], F32, 'lnv')
            rstd = A.alloc([4], F32, 'rstd')
            xns = [A.alloc([1024], BF16, f'xn{i}') for i in range(nx)]
            return (junk, ssq, lnv, rstd, xns)

        conv_pending = []

        def convert_w(name, idx, defer=False):
            src, shp = wsrc[name]
            for c in range(shp[2]):
                def one(c=c):
                    kb.dma('pool', V(wbf[name][idx, :, c, :], ('wbf', name, idx)), V(src[idx, :, c, :]), bg=True)
                if defer:
                    conv_pending.append(one)
                else:
                    one()

        def emit_conversions(frac):
            n = int(math.ceil(len(conv_pending) * frac)) if conv_pending else 0
            for _ in range(min(n, len(conv_pending))):
                conv_pending.pop(0)()

        def load_w(dst, name, idx, nk):
            ncol = wsrc[name][1][3]
            g = max(1, min(nk, (32 * 1024) // (ncol * 2)))
            for c0 in range(0, nk, g):
                c1 = min(nk, c0 + g)
                kb.dma('sp', dst[:, c0:c1, :], V(wbf[name][idx, :, c0:c1, :], ('wbf', name, idx)))

        def p1_attn(l):
            j = l // 2
            A.reset()
            W = A.alloc([8, 3072], BF16, 'W')
            load_w(W, 'wqkv', j, 8)
            pb = prologue_bufs(4)
            hts = [A.alloc([1024], F32, f'ht{i}') for i in range(4)]
            uTs = [A.alloc([8, 512], BF16, f'uT{i}') for i in range(2)]
            Cb = [A.alloc([512], F32, f'Cb{i}') for i in range(2)]
            Sb = [A.alloc([512], F32, f'Sb{i}') for i in range(2)]
            qraw = [A.alloc([512], BF16, f'qraw{i}') for i in range(2)]
            t1 = [A.alloc([512], F32, f't1{i}') for i in range(2)]
            t2 = [A.alloc([512], F32, f't2{i}') for i in range(2)]
            qo = [A.alloc([512], BF16, f'qo{i}') for i in range(3)]
            vo = [A.alloc([1024], BF16, f'vo{i}') for i in range(2)]
            qi = 0
            vi = 0

            def loads(bi_):
                tok0_, ntok_, isctx_ = blocks[bi_]
                for t in range(ntok_ // 128):
                    kb.dma('sp', hts[t], hsrc(l, tok0_ + t * 128))
                if not isctx_:
                    kb.dma('sp', Cb[bi_ % 2], V(AC[:, tok0_:tok0_ + 512]))
                    kb.dma('sp', Sb[bi_ % 2], V(AS[:, tok0_:tok0_ + 512]))

            def s1(bi_):
                for t in range(blocks[bi_][1] // 128):
                    prologue_s1(hts[t], t, pb)

            def s2(bi_):
                for t in range(blocks[bi_][1] // 128):
                    prologue_s2(l, 0, 1 if blocks[bi_][2] else 0, t, uTs[bi_ % 2], pb, banks=(6, 7))
            loads(0)
            s1(0)
            s2(0)
            if len(blocks) > 1:
                loads(1)
            for bi, (tok0, ntok, isctx) in enumerate(blocks):
                nt = ntok // 128
                u = uTs[bi % 2]
                pend = None
                for jj in range(16):
                    if jj == 6 and bi + 1 < len(blocks):
                        s1(bi + 1)
                    ps = PS[jj % 2]
                    for c in range(8):
                        kb.mm(ps[:, :ntok], W[:, c, jj * 128:(jj + 1) * 128], u[:, c, :ntok], start=(c == 0), stop=(c == 7))
                    dst = (QT if jj < 8 else KT)[(jj % 8) * 128:(jj % 8 + 1) * 128, tok0:tok0 + ntok]
                    q_ = qo[qi % 3]
                    qi += 1
                    if isctx:
                        kb.copy('act', q_[:, :ntok], ps[:, :ntok])
                        kb.dma('sp', V(dst), q_[:, :ntok])
                    else:
                        qr = qraw[jj % 2]
                        kb.copy('act', qr, ps)
                        kb.tt('dve', t1[jj % 2], ps, Cb[bi % 2], ALU.mult)

                        def fin(jj=jj, qr=qr, q_=q_, dst=dst):
                            ps2 = PS[2 + jj % 2]
                            kb.mm(ps2, permA, qr)
                            kb.tt('dve', t2[jj % 2], ps2, Sb[bi % 2], ALU.mult)
                            kb.tt('pool', q_, t1[jj % 2], t2[jj % 2], ALU.add)
                            kb.dma('sp', V(dst), q_)
                        if pend is not None:
                            pend()
                        pend = fin
                if pend is not None:
                    pend()
                nt_next = blocks[bi + 1][1] // 128 if bi + 1 < len(blocks) else 0
                for t in range(max(nt, nt_next)):
                    if t < nt_next:
                        prologue_s2(l, 0, 1 if blocks[bi + 1][2] else 0, t, uTs[(bi + 1) % 2], pb, banks=(6, 7))
                        if t == nt_next - 1 and bi + 2 < len(blocks):
                            loads(bi + 2)
                    if t >= nt:
                        continue
                    v_ = vo[vi % 2]
                    vi += 1
                    for n in range(2):
                        ps = PS[4 + n]
                        for c in range(8):
                            kb.mm(ps, u[:, c, t * 128:(t + 1) * 128], W[:, c, 2048 + n * 512:2048 + (n + 1) * 512],
                                  start=(c == 0), stop=(c == 7))
                        kb.copy('act' if n == 0 else 'dve', v_[:, n * 512:(n + 1) * 512], ps)
                    kb.dma('sp', V(VA[tok0 + t * 128:tok0 + (t + 1) * 128, :]), v_)
            kb.barrier()

        def p2_attn(l):
            j = l // 2
            lam_init = 0.8 - 0.6 * math.exp(-0.3 * l)
            A.reset()
            lamt = A.alloc([256], F32, 'lamt')
            prod = A.alloc([2, 64], F32, 'prod')
            s2 = A.alloc([2], F32, 's2')
            e2 = A.alloc([2], F32, 'e2')
            neglam = A.alloc([1], F32, 'neglam')
            gsub = A.alloc([1], F32, 'gsub')
            sgt = A.alloc([2], F32, 'sgt')
            kb.dma('sp', lamt, V(lam[j].partition_broadcast(128)))
            kb.dma('sp', sgt, V(subg))
            kb.tt('dve', prod[:, 0, :], lamt[:, 0:64], lamt[:, 64:128], ALU.mult)
            kb.tt('dve', prod[:, 1, :], lamt[:, 128:192], lamt[:, 192:256], ALU.mult)
            kb.reduce(s2, prod)
            kb.act(e2, s2, AF.Exp)
            kb.tt('dve', neglam, e2[:, 1:2], e2[:, 0:1], ALU.subtract)
            kb.ts('dve', neglam, neglam, -lam_init, None, ALU.add)
            kb.ts('dve', gsub, sgt[:, j:j + 1], 1.0 - lam_init, None, ALU.mult)
            QTm = [[A.alloc([NTOK], BF16, f'QTm{r_}_{i}') for i in range(2)] for r_ in range(2)]
            KTh = [A.alloc([NTOK], BF16, f'KTh{i}') for i in range(2)]
            Vh = [A.alloc([NT, 128], BF16, f'Vh{i}') for i in range(2)]
            NPT = 6
            LAG = 2
            PT = [A.alloc([2, 512], BF16, f'PT{i}') for i in range(NPT)]
            accD = [A.alloc([2, 512], F32, f'accD{i}') for i in range(4)]
            accP = [A.alloc([2, 512], F32, f'accP{i}') for i in range(4)]
            rec = [A.alloc([512], F32, f'rec{i}') for i in range(4)]
            oa2 = [A.alloc([512], F32, f'oa{i}') for i in range(2)]
            ob2 = [A.alloc([512], F32, f'ob{i}') for i in range(2)]
            oo2 = [A.alloc([512], F32, f'oo{i}') for i in range(2)]
            sq2 = [A.alloc([512], BF16, f'sq{i}') for i in range(2)]
            lnq2 = [A.alloc([512], F32, f'lnq{i}') for i in range(2)]
            rs2 = [A.alloc([512], F32, f'rs{i}') for i in range(2)]
            oT = [A.alloc([512], BF16, f'oT{i}') for i in range(2)]
            qblocks = [(i * 512, 512, list(range(NT))) for i in range(NB)] + [(T, 256, [NTL, NTL + 1])]
            nqb = len(qblocks)
            for r_ in range(2):
                for i in range(2):
                    kb.memset('dve', QTm[r_][i], 0.0)

            def load_head(h):
                for i in range(2):
                    kb.dma('sp', QTm[h % 2][i][64 * i:64 * i + 64, :], V(QT[h * 128 + 64 * i:h * 128 + 64 * i + 64, :]))
                kb.dma('sp', KTh[h % 2], V(KT[h * 128:(h + 1) * 128, :]))
                kb.dma('sp', Vh[h % 2], V(VA[:, h * 128:(h + 1) * 128].rearrange("(t p) d -> p t d", p=128)))

            def pso_bank(g, i):
                return PS[4 + i]

            def den_bank(g, i):
                return PS[6 + (2 * g + i) % 2]

            steps = []
            for h in range(8):
                for qb, (q0, nq, keys) in enumerate(qblocks):
                    for i in range(2):
                        nss = len(keys) // 2
                        for ss in range(nss):
                            steps.append((h, qb, i, ss, keys[2 * ss], keys[2 * ss + 1], q0, nq, nss))
            deferred = []

            def epilogue_a(g, h, q0, nq):
                p1_ = pso_bank(g, 1)
                r1 = rec[(g % 2) * 2 + 1]
                oa, ob_, oo = oa2[g % 2], ob2[g % 2], oo2[g % 2]
                kb.tt('dve', ob_[:, :nq], p1_[:, :nq], r1[:, :nq], ALU.mult)
                kb.stt(oo[:, :nq], ob_[:, :nq], neglam[:, 0:1], oa[:, :nq], ALU.mult, ALU.add)

            def epilogue_b(g, h, q0, nq):
                kb.act(sq2[g % 2][:, :nq], oo2[g % 2][:, :nq], AF.Square)

            def epilogue_c(g, h, q0, nq):
                kb.mm(PS[7][:, :nq], onesb, sq2[g % 2][:, :nq])
                kb.act(lnq2[g % 2][:, :nq], PS[7][:, :nq], AF.Ln, scale=1.0 / 128, bias=epst[:, 0:1])
                kb.act(rs2[g % 2][:, :nq], lnq2[g % 2][:, :nq], AF.Exp, scale=-0.5)

            def epilogue_d(g, h, q0, nq):
                o_ = oT[g % 2]
                kb.stt(o_[:, :nq], oo2[g % 2][:, :nq], gsub[:, 0:1], rs2[g % 2][:, :nq], ALU.mult, ALU.mult)
                kb.dma('sp', V(OT[h * 128:(h + 1) * 128, q0:q0 + nq]), o_[:, :nq])

            load_head(0)
            ns = len(steps)
            s_i = 0
            while s_i < ns + LAG or deferred:
                if s_i < ns:
                    (h, qb, i, ss, kt0, kt1, q0, nq, nss) = steps[s_i]
                    sc_ = ps2(s_i % 2)
                    for jj, kt in enumerate((kt0, kt1)):
                        kb.mm(sc_[:, jj, :nq], KTh[h % 2][:, kt * 128:(kt + 1) * 128], QTm[h % 2][i][:, q0:q0 + nq])
                    kb.act(PT[s_i % NPT][:, :, :nq], sc_[:, :, :nq], AF.Exp, scale=0.125)
                if LAG <= s_i < ns + LAG:
                    (h, qb, i, ss, kt0, kt1, q0, nq, nss) = steps[s_i - LAG]
                    p_ = PT[(s_i - LAG) % NPT]
                    if qb == 0 and i == 0 and ss == 0 and h + 1 < 8:
                        load_head(h + 1)
                        emit_conversions(1.0 / (7 - h))
                    g = h * nqb + qb
                    pso = pso_bank(g, i)
                    dnb = den_bank(g, i)
                    aD = accD[(g % 2) * 2 + i]
                    aP = accP[(g % 2) * 2 + i]
                    for jj, kt in enumerate((kt0, kt1)):
                        kb.mm(pso[:, :nq], Vh[h % 2][:, kt, :], p_[:, jj, :nq],
                              start=(ss == 0 and jj == 0), stop=(ss == nss - 1 and jj == 1))
                    if ss % 4 != 3:
                        if ss == 0:
                            kb.copy('dve', aD[:, :, :nq], p_[:, :, :nq])
                        else:
                            kb.tt('dve', aD[:, :, :nq], aD[:, :, :nq], p_[:, :, :nq], ALU.add)
                    else:
                        for jj in range(2):
                            kb.mm(dnb[:, :nq], onesb, p_[:, jj, :nq], start=(ss == 3 and jj == 0), stop=False)
                    if ss == nss - 1:
                        def den(g=g, i=i, nq=nq, aD=aD, aP=aP, nss=nss, dnb=dnb, pso=pso):
                            parts = [aD[:, 0, :nq], aD[:, 1, :nq]]
                            for pi_, pa in enumerate(parts):
                                kb.mm(dnb[:, :nq], onesf, pa, start=(pi_ == 0 and nss <= 3), stop=(pi_ == len(parts) - 1))
                            kb.recip(rec[(g % 2) * 2 + i][:, :nq], dnb[:, :nq])
                            if i == 0:
                                kb.tt('dve', oa2[g % 2][:, :nq], pso[:, :nq], rec[(g % 2) * 2][:, :nq], ALU.mult)
                        if nss > 3 and (i == 0 or qb < nqb - 2):
                            dly = 3
                        else:
                            dly = 1
                        deferred.append((s_i + dly, den))
                        if i == 1:
                            args = (g, h, q0, nq)
                            deferred.append((s_i + dly, lambda a=args: epilogue_a(*a)))
                            deferred.append((s_i + dly + 2, lambda a=args: epilogue_b(*a)))
                            deferred.append((s_i + dly + 4, lambda a=args: epilogue_c(*a)))
                            deferred.append((s_i + dly + 6, lambda a=args: epilogue_d(*a)))
                rest = []
                for due, fn in deferred:
                    if due <= s_i:
                        fn()
                    else:
                        rest.append((due, fn))
                deferred = rest
                s_i += 1
            kb.barrier()

        def ret_consts(j):
            dl = A.alloc([8], F32, 'dl')
            e1 = A.alloc([8], F32, 'e1')
            lg = A.alloc([8], F32, 'lg')
            kb.dma('sp', dl, V(dlog[j].partition_broadcast(128)))
            kb.act(e1, dl, AF.Exp, scale=-1.0)
            kb.ts('dve', e1, e1, 1.0, None, ALU.add)
            kb.act(lg, e1, AF.Ln)
            kb.ts('dve', lg, lg, -1.0, None, ALU.mult)
            return lg

        def p1_ret(l):
            j = l // 2
            A.reset()
            W = A.alloc([8, 6144], BF16, 'W')
            load_w(W, 'win', j, 8)
            lg = ret_consts(j)
            ze = A.alloc([8], F32, 'ze')
            zl = A.alloc([8], F32, 'zl')
            zeta = A.alloc([8], F32, 'zeta')
            kb.dma('sp', ze, V(zec_d))
            kb.tt('dve', zl, ze, lg, ALU.mult)
            kb.act(zeta, zl, AF.Exp)
            pb = prologue_bufs(4)
            hts = [A.alloc([1024], F32, f'ht{i}') for i in range(4)]
            uTs = [A.alloc([8, 512], BF16, f'uT{i}') for i in range(2)]
            tab2 = [[A.alloc([512], F32, f'tab{r_}_{i}') for i in range(4)] for r_ in range(2)]
            qraw = [A.alloc([512], BF16, f'qraw{i}') for i in range(2)]
            t1 = [A.alloc([512], F32, f't1{i}') for i in range(2)]
            t2 = [A.alloc([512], F32, f't2{i}') for i in range(2)]
            qk = A.alloc([16, 512], BF16, 'qkT')
            kzo = [A.alloc([2048], BF16, f'kzo{i}') for i in range(2)]
            vgo = [A.alloc([2048], BF16, f'vgo{i}') for i in range(2)]
            ki_ = 0
            vi = 0

            def loads(bi_):
                tok0_, ntok_, isctx_ = blocks[bi_]
                for t in range(ntok_ // 128):
                    kb.dma('sp', hts[t], hsrc(l, tok0_ + t * 128))
                if not isctx_:
                    for i4 in range(4):
                        kb.dma('sp', tab2[bi_ % 2][i4], V(RT[i4, :, tok0_:tok0_ + 512]))

            def s1(bi_):
                for t in range(blocks[bi_][1] // 128):
                    prologue_s1(hts[t], t, pb)

            def s2(bi_):
                for t in range(blocks[bi_][1] // 128):
                    prologue_s2(l, 0, 1 if blocks[bi_][2] else 0, t, uTs[bi_ % 2], pb, banks=(7, 4))
            loads(0)
            s1(0)
            s2(0)
            if len(blocks) > 1:
                loads(1)
            for bi, (tok0, ntok, isctx) in enumerate(blocks):
                nt = ntok // 128
                tab = tab2[bi % 2]
                u = uTs[bi % 2]
                pend = None
                for jj in range(16):
                    if jj == 6 and bi + 1 < len(blocks):
                        s1(bi + 1)
                    ps = PS[jj % 2]
                    col0 = jj * 128
                    for c in range(8):
                        kb.mm(ps[:, :ntok], W[:, c, col0:col0 + 128], u[:, c, :ntok], start=(c == 0), stop=(c == 7))
                    scale = 1.0 if jj < 8 else 1.0 / 16.0
                    dstv = qk[:, jj, :ntok]
                    dst = (QT if jj < 8 else KT)[(jj % 8) * 128:(jj % 8 + 1) * 128, tok0:tok0 + ntok]
                    if isctx:
                        kb.act(dstv, ps[:, :ntok], AF.Copy, scale=scale)
                        kb.dma('sp', V(dst), dstv)
                    else:
                        par = jj % 2
                        qr = qraw[jj % 2]
                        kb.act(qr, ps, AF.Copy, scale=scale)
                        kb.stt(t1[jj % 2], ps, scale, tab[2 * par], ALU.mult, ALU.mult)

                        def fin(jj=jj, qr=qr, dstv=dstv, dst=dst, par=par):
                            ps2 = PS[2 + jj % 2]
                            kb.mm(ps2, permR, qr)
                            kb.tt('dve', t2[jj % 2], ps2, tab[2 * par + 1], ALU.mult)
                            kb.tt('pool', dstv, t1[jj % 2], t2[jj % 2], ALU.add)
                            kb.dma('sp', V(dst), dstv)
                        if pend is not None:
                            pend()
                        pend = fin
                if pend is not None:
                    pend()
                nt_next = blocks[bi + 1][1] // 128 if bi + 1 < len(blocks) else 0
                for t in range(max(nt, nt_next)):
                    if t < nt:
                        pT = psb(4)
                        for c in range(8):
                            kb.tr(pT[:, c * 128:(c + 1) * 128], qk[:, 8 + c, t * 128:(t + 1) * 128], identb)
                        kz = kzo[ki_ % 2]
                        ki_ += 1
                        for dd in range(2):
                            for hh in range(4):
                                o_ = kz[:, dd * 1024 + hh * 256:dd * 1024 + (hh + 1) * 256]
                                i_ = pT[:, hh * 256:(hh + 1) * 256]
                                z_ = zeta[:, dd * 4 + hh:dd * 4 + hh + 1]
                                if dd == 0:
                                    kb.ts('dve', o_, i_, z_, None, ALU.mult)
                                else:
                                    kb.act(o_, i_, AF.Copy, scale=z_)
                        kb.dma('sp', V(KZ[tok0 + t * 128:tok0 + (t + 1) * 128, :]), kz)
                    if t < nt_next:
                        prologue_s2(l, 0, 1 if blocks[bi + 1][2] else 0, t, uTs[(bi + 1) % 2], pb, banks=(7,))
                        if t == nt_next - 1 and bi + 2 < len(blocks):
                            loads(bi + 2)
                    if t >= nt:
                        continue
                    for part in range(2):
                        vg = vgo[vi % 2]
                        vi += 1
                        for n in range(4):
                            ps = PS[5 + n % 2]
                            c0 = 2048 + part * 2048 + n * 512
                            for c in range(8):
                                kb.mm(ps, u[:, c, t * 128:(t + 1) * 128], W[:, c, c0:c0 + 512], start=(c == 0), stop=(c == 7))
                            if part == 0:
                                kb.copy('act' if n % 2 == 0 else 'dve', vg[:, n * 512:(n + 1) * 512], ps)
                            else:
                                kb.act(vg[:, n * 512:(n + 1) * 512], ps, AF.Silu)
                        kb.dma('sp', V((VR if part == 0 else GS)[tok0 + t * 128:tok0 + (t + 1) * 128, :]), vg)
            kb.barrier()

        def p2_ret(l):
            j = l // 2
            need_ctx = l < DEPTH - 1
            A.reset()
            lg = ret_consts(j)
            rc = A.alloc([6, 128], F32, 'rc')
            kb.dma('sp', rc, V(rcon_d))
            gch = A.alloc([8], F32, 'gch')
            kb.act(gch, lg, AF.Exp, scale=128.0)
            mtmp = A.alloc([128], F32, 'mtmp')
            maskT = [[A.alloc([128], F32, f'mask{d}{hh}') for hh in range(4)] for d in range(2)]
            xi2 = [[A.alloc([2, 128], F32, f'xi{d}{hh}') for hh in range(4)] for d in range(2)]
            for d in range(2):
                for hh in range(4):
                    lcol = lg[:, d * 4 + hh:d * 4 + hh + 1]
                    kb.act(mtmp, rc[:, 2 * d, :], AF.Exp, scale=lcol)
                    kb.tt('dve', maskT[d][hh], mtmp, rc[:, 2 * d + 1, :], ALU.mult)
                    for j2 in range(2):
                        kb.act(xi2[d][hh][:, j2, :], rc[:, 4 + d, :], AF.Exp, scale=lcol)
            S32 = [A.alloc([2, 512], F32, f'S32_{hh}') for hh in range(4)]
            Sbf = [[A.alloc([2, 512], BF16, f'Sbf{hh}_{pp}') for pp in range(2)] for hh in range(4)]
            QTc = [A.alloc([8, 128], BF16, f'QTc{i}') for i in range(2)]
            KTc = [A.alloc([8, 128], BF16, f'KTc{i}') for i in range(2)]
            Kzc = [A.alloc([1024], BF16, f'Kzc{i}') for i in range(2)]
            Vc = [A.alloc([2048], BF16, f'Vc{i}') for i in range(2)]
            obl = [A.alloc([2048], F32, f'obl{i}') for i in range(2)]
            gsl = [A.alloc([2048], BF16, f'gsl{i}') for i in range(2)]
            AT = [A.alloc([128], BF16, f'AT{i}') for i in range(4)]
            Qx = [A.alloc([2, 128], BF16, f'Qx{i}') for i in range(4)]
            obw = [A.alloc([2048], F32, f'obw{i}') for i in range(2)]
            o32s = [A.alloc([2048], F32, f'o32_{i}') for i in range(2)]
            st6 = A.alloc([4, 6], F32, 'st6')
            mvs = [A.alloc([4, 2], F32, f'mv{i}') for i in range(2)]
            lnv4 = A.alloc([4], F32, 'lnv4')
            rs4 = A.alloc([4], F32, 'rs4')
            nmr4 = A.alloc([4], F32, 'nmr4')
            ons = [A.alloc([2048], BF16, f'on{i}') for i in range(2)]
            gos = [A.alloc([2048], BF16, f'go{i}') for i in range(2)]
            goT = [A.alloc([16, 128], BF16, f'goT{i}') for i in range(2)]
            QTv = QT.rearrange("(j p) t -> p j t", p=128)
            KTv = KT.rearrange("(j p) t -> p j t", p=128)
            OTv = OT.rearrange("(j p) t -> p j t", p=128)

            def p2r_loads(d_, order_, ci_):
                cidx_ = order_[ci_]
                isctx_ = cidx_ >= NTL
                do_out_ = (not isctx_) or need_ctx
                last_ = ci_ == len(order_) - 1
                tok0_ = cidx_ * 128
                rb_ = ci_ % 2
                if do_out_:
                    kb.dma('sp', QTc[rb_], V(QTv[:, :, tok0_:tok0_ + 128]))
                    kb.dma('sp', KTc[rb_], V(KTv[:, :, tok0_:tok0_ + 128]))
                if not last_:
                    kb.dma('sp', Kzc[rb_], V(KZ[tok0_:tok0_ + 128, d_ * 1024:(d_ + 1) * 1024]))
                kb.dma('sp', Vc[rb_], V(VR[tok0_:tok0_ + 128, :]))
                if d_ == 0 and do_out_:
                    kb.dma('sp', obl[rb_], V(OB[tok0_:tok0_ + 128, :]))
                    kb.dma('sp', gsl[rb_], V(GS[tok0_:tok0_ + 128, :]))
            for d in (1, 0):
                for hh in range(4):
                    kb.memset('dve', S32[hh], 0.0)
                    kb.memset('pool', Sbf[hh][0], 0.0)
                if d == 0:
                    order = [NTL, NTL + 1] + list(range(NTL))
                else:
                    order = [NTL + 1, NTL] + list(range(NTL - 1, -1, -1))
                pend_tail = [None]
                for ci, cidx in enumerate(order):
                    isctx = cidx >= NTL
                    do_out = (not isctx) or need_ctx
                    last = ci == len(order) - 1
                    tok0 = cidx * 128
                    rb = ci % 2
                    if ci == 0:
                        p2r_loads(d, order, 0)
                    if ci + 1 < len(order):
                        p2r_loads(d, order, ci + 1)
                    pp = ci % 2
                    o32 = o32s[ci % 2]
                    mv = mvs[ci % 2]
                    if do_out:
                        for hh in range(4):
                            pA = PS[hh % 2][:, 0:128]
                            kb.mm(pA, KTc[rb][:, 2 * hh, :], QTc[rb][:, 2 * hh, :], start=True, stop=False)
                            kb.mm(pA, KTc[rb][:, 2 * hh + 1, :], QTc[rb][:, 2 * hh + 1, :], start=False, stop=True)
                            kb.tt('dve', AT[hh], pA, maskT[d][hh], ALU.mult)
                            kb.tt('pool', Qx[hh], QTc[rb][:, 2 * hh:2 * hh + 2, :], xi2[d][hh], ALU.mult)
                    if not last:
                        for hh in range(4):
                            for j2 in range(2):
                                pst = PS[2 + j2]
                                kb.mm(pst, Kzc[rb][:, hh * 256 + j2 * 128:hh * 256 + (j2 + 1) * 128],
                                      Vc[rb][:, hh * 512:(hh + 1) * 512])
                                kb.stt(S32[hh][:, j2, :], S32[hh][:, j2, :], gch[:, d * 4 + hh:d * 4 + hh + 1], pst,
                                       ALU.mult, ALU.add)
                                kb.copy('act', Sbf[hh][1 - pp][:, j2, :], S32[hh][:, j2, :])
                    if not do_out:
                        continue
                    if pend_tail[0] is not None:
                        pend_tail[0]()
                        pend_tail[0] = None
                    ow = obw[ci % 2]
                    for hh in range(4):
                        sl = slice(hh * 512, (hh + 1) * 512)
                        pso = PS[4 + hh % 2]
                        kb.mm(pso, AT[hh], Vc[rb][:, sl], start=True, stop=False)
                        kb.mm(pso, Qx[hh][:, 0, :], Sbf[hh][pp][:, 0, :], start=False, stop=False)
                        kb.mm(pso, Qx[hh][:, 1, :], Sbf[hh][pp][:, 1, :], start=False, stop=True)
                        if d == 1:
                            kb.copy('act' if hh % 2 == 0 else 'dve', ow[:, sl], pso)
                        else:
                            kb.tt('dve', o32[:, sl], pso, obl[rb][:, sl], ALU.add)
                            kb.bn_stats(st6[:, hh, :], o32[:, sl])
                            kb.bn_aggr(mv[:, hh, :], st6[:, hh, :])
                    if d == 1:
                        kb.dma('sp', V(OB[tok0:tok0 + 128, :]), ow)
                    else:
                        on = ons[ci % 2]
                        go = gos[ci % 2]
                        kb.act(lnv4, mv[:, :, 1], AF.Ln, bias=epst[:, 1:2])
                        kb.act(rs4, lnv4, AF.Exp, scale=-0.5)
                        kb.stt(nmr4, mv[:, :, 0], -1.0, rs4, ALU.mult, ALU.mult)
                        for hh in range(4):
                            sl = slice(hh * 512, (hh + 1) * 512)
                            kb.act(on[:, sl], o32[:, sl], AF.Identity, scale=rs4[:, hh:hh + 1], bias=nmr4[:, hh:hh + 1])
                        kb.tt('pool', go, on, gsl[rb], ALU.mult)

                        def tail(go=go, gt=goT[ci % 2], tok0=tok0):
                            for half in range(2):
                                pT = psb(6 + half)
                                for c in range(8):
                                    kc = half * 8 + c
                                    kb.tr(pT[:, c * 128:(c + 1) * 128], go[:, kc * 128:(kc + 1) * 128], identb)
                                kb.copy('act' if half == 0 else 'dve', gt[:, half * 8:(half + 1) * 8, :],
                                        pT.rr("p (c t) -> p c t", t=128))
                            kb.dma('sp', V(OTv[:, :, tok0:tok0 + 128]), gt)
                        pend_tail[0] = tail
                if pend_tail[0] is not None:
                    pend_tail[0]()
                    pend_tail[0] = None
                kb.barrier()

        def p3a(l):
            j = l // 2
            is_attn = (l % 2 == 0)
            need_ctx = l < DEPTH - 1
            KC = 8 if is_attn else 16
            A.reset()
            W1pre = A.alloc([8, 4096], BF16, 'W1pre')
            Wo = A.alloc([KC, 1024], BF16, 'Wo')
            load_w(Wo, 'wo_a' if is_attn else 'wo_r', j, KC)
            gates = [A.alloc([1024], F32, f'gate{b}') for b in range(2)]
            for b in range(2):
                kb.dma('sp', gates[b], V(modrow[l, b, 2048:3072].partition_broadcast(128)))
            OTb = [A.alloc([KC, 512], BF16, f'OTb{i}') for i in range(2)]
            hts = [A.alloc([1024], F32, f'ht{i}') for i in range(8)]
            tmp = [A.alloc([512], F32, f'tmp{i}') for i in range(2)]
            pb = prologue_bufs()
            uTs = [A.alloc([8, 512], BF16, f'uT{i}') for i in range(2)]
            OTv = OT.rearrange("(j p) t -> p j t", p=128)
            UTv = UT.rearrange("(j p) t -> p j t", p=128)
            ti = 0
            pi = 0
            blks = [bk for bk in blocks if not (bk[2] and not need_ctx)]

            def p3a_loads(bi_):
                tok0_, ntok_, isctx_ = blks[bi_]
                kb.dma('sp', OTb[bi_ % 2][:, :, :ntok_], V(OTv[:, 0:KC, tok0_:tok0_ + ntok_]))
                for t in range(ntok_ // 128):
                    kb.dma('sp', hts[(bi_ % 2) * 4 + t], hsrc(l, tok0_ + t * 128))
            p3a_loads(0)
            pend = [None]
            for bi, (tok0, ntok, isctx) in enumerate(blks):
                nt = ntok // 128
                b = 1 if isctx else 0
                ob = OTb[bi % 2]
                u = uTs[bi % 2]
                if bi + 1 < len(blks):
                    p3a_loads(bi + 1)
                if bi == 0:
                    load_w(W1pre, 'w1', l, 8)
                for t in range(nt):
                    ht = hts[(bi % 2) * 4 + t]
                    for n in range(2):
                        ps = PS[pi % 4]
                        pi += 1
                        for kc in range(KC):
                            kb.mm(ps, ob[:, kc, t * 128:(t + 1) * 128], Wo[:, kc, n * 512:(n + 1) * 512],
                                  start=(kc == 0), stop=(kc == KC - 1))
                        tm = tmp[ti % 2]
                        ti += 1
                        sl = slice(n * 512, (n + 1) * 512)
                        kb.tt('dve', tm, ps, gates[b][:, sl], ALU.mult)
                        kb.tt('pool', ht[:, sl], tm, ht[:, sl], ALU.add)
                    kb.dma('sp', V(hbuf[tok0 + t * 128:tok0 + (t + 1) * 128, :]), ht)
                    prologue_s1(ht, t, pb)
                    if pend[0] is not None:
                        pend[0]()

                    def s2(t=t, b=b, u=u, last=(t == nt - 1), tok0=tok0, ntok=ntok):
                        prologue_s2(l, 1, b, t, u, pb, banks=(6, 7))
                        if last:
                            kb.dma('sp', V(UTv[:, :, tok0:tok0 + ntok]), u[:, :, :ntok])
                    pend[0] = s2
            if pend[0] is not None:
                pend[0]()
            kb.barrier()

        def p3b(l):
            need_ctx = l < DEPTH - 1
            last = (l == DEPTH - 1)
            A.reset()
            W1 = A.alloc([8, 4096], BF16, 'W1')
            W2 = A.alloc([32, 1024], BF16, 'W2')
            load_w(W2, 'w2', l, 32)
            gate = A.alloc([1024], F32, 'gate')
            kb.dma('sp', gate, V(modrow[l, 0, 5120:6144].partition_broadcast(128)))
            hid = A.alloc([32, 512], BF16, 'hid')
            uTs = [A.alloc([8, 512], BF16, f'uT{i}') for i in range(2)]
            hts = [A.alloc([1024], F32, f'ht{i}') for i in range(4)]
            rl = [A.alloc([512], BF16, f'rl{i}') for i in range(1 if last else 2)]
            tmp = A.alloc([512], F32, 'tmp')
            if last:
                fg = A.alloc([1024], F32, 'fg')
                kb.dma('sp', fg, V(fng.partition_broadcast(128)))
                ssq = A.alloc([4], F32, 'ssq')
                lnv = A.alloc([4], F32, 'lnv')
                rstd = A.alloc([4], F32, 'rstd')
                junk = tmp.bc(BF16)
            UTv = UT.rearrange("(j p) t -> p j t", p=128)
            pi = 0
            blks = [bk for bk in blocks if not (bk[2] and not need_ctx)]

            def p3b_uload(bi_):
                tok0_, ntok_, isctx_ = blks[bi_]
                kb.dma('sp', uTs[bi_ % 2][:, :, :ntok_], V(UTv[:, :, tok0_:tok0_ + ntok_]))
            p3b_uload(0)
            for bi, (tok0, ntok, isctx) in enumerate(blks):
                nt = ntok // 128
                u = uTs[bi % 2]
                if bi + 1 < len(blks):
                    p3b_uload(bi + 1)
                if isctx:
                    kb.dma('sp', gate, V(modrow[l, 1, 5120:6144].partition_broadcast(128)))
                for t in range(nt):
                    kb.dma('sp', hts[t], V(hbuf[tok0 + t * 128:tok0 + (t + 1) * 128, :]))
                for f in range(32):
                    ps = PS[f % 3]
                    for c in range(8):
                        kb.mm(ps[:, :ntok], W1[:, c, f * 128:(f + 1) * 128], u[:, c, :ntok], start=(c == 0), stop=(c == 7))
                    r_ = rl[f % len(rl)]
                    kb.act(r_[:, :ntok], ps[:, :ntok], AF.Relu)
                    kb.tt('pool', hid[:, f, :ntok], r_[:, :ntok], r_[:, :ntok], ALU.mult)
                for t in range(nt):
                    ht = hts[t]
                    for n in range(2):
                        ps = PS[3 + pi % 4]
                        pi += 1
                        for f in range(32):
                            kb.mm(ps, hid[:, f, t * 128:(t + 1) * 128], W2[:, f, n * 512:(n + 1) * 512],
                                  start=(f == 0), stop=(f == 31))
                        sl = slice(n * 512, (n + 1) * 512)
                        kb.tt('dve', tmp, ps, gate[:, sl], ALU.mult)
                        kb.tt('dve', ht[:, sl], tmp, ht[:, sl], ALU.add)
                    if last:
                        kb.act(junk, ht, AF.Square, accum=ssq[:, t:t + 1])
                        kb.act(lnv[:, t:t + 1], ssq[:, t:t + 1], AF.Ln, scale=1.0 / D, bias=epst[:, 0:1])
                        kb.act(rstd[:, t:t + 1], lnv[:, t:t + 1], AF.Exp, scale=-0.5)
                        kb.stt(ht, ht, rstd[:, t:t + 1], fg, ALU.mult, ALU.mult)
                        kb.dma('sp', V(out[tok0 + t * 128:tok0 + (t + 1) * 128, :]), ht)
                    else:
                        kb.dma('sp', V(hbuf[tok0 + t * 128:tok0 + (t + 1) * 128, :]), ht)
            kb.barrier()

        def convert_layer(l, defer=False):
            j = l // 2
            if l % 2 == 0:
                convert_w('wqkv', j, defer)
                convert_w('wo_a', j, defer)
            else:
                convert_w('win', j, defer)
                convert_w('wo_r', j, defer)
            convert_w('w1', l, defer)
            convert_w('w2', l, defer)
        convert_layer(layers[0])
        p0()
        for l in layers:
            if l % 2 == 0:
                p1_attn(l)
                if stop_after == ('p1', l):
                    break
                if l == layers[0]:
                    for l2 in layers[1:]:
                        convert_layer(l2, defer=True)
                p2_attn(l)
                emit_conversions(1.0)
            else:
                p1_ret(l)
                if stop_after == ('p1', l):
                    break
                p2_ret(l)
            if stop_after == ('p2', l):
                break
            p3a(l)
            if stop_after == ('p3a', l):
                break
            p3b(l)
            if stop_after == ('p3b', l):
                break
        block = stack.enter_context(nc.Block())
        S.emit(nc, stack, block)
    return nc


def tile_w(w, kc):
    n = w.shape[-1]
    return np.ascontiguousarray(w.reshape(kc, 128, n).transpose(1, 0, 2))


def make_in_maps(inputs, T, nb):
    f = lambda a: np.ascontiguousarray(np.asarray(a, dtype=np.float32))
    hc = host_consts(T)
    shared = dict(hc)
    shared['cctxT'] = np.ascontiguousarray(f(inputs['c_ctx']).reshape(8, 128).T)
    shared['ada_w'] = np.stack([tile_w(f(inputs['ada_w'][i]), 8) for i in range(4)])
    shared['ada_b'] = f(inputs['ada_b'])
    shared['wqkv'] = np.stack([tile_w(f(inputs['attn_w_qkv'][i]), 8) for i in range(2)])
    shared['wo_a'] = np.stack([tile_w(f(inputs['attn_w_o'][i]), 8) for i in range(2)])
    shared['lam'] = f(inputs['attn_lambda']).reshape(2, 256)
    shared['subg'] = np.ascontiguousarray(f(inputs['attn_subln_g']).T)
    shared['win'] = np.stack([tile_w(f(inputs['ret_w_in'][i]), 8) for i in range(2)])
    shared['wo_r'] = np.stack([tile_w(f(inputs['ret_w_o'][i]), 16) for i in range(2)])
    shared['dlog'] = f(inputs['ret_decay_logit']).reshape(2, 8)
    shared['w1'] = np.stack([tile_w(f(inputs['mlp_w1'][i]), 8) for i in range(4)])
    shared['w2'] = np.stack([tile_w(f(inputs['mlp_w2'][i]), 32) for i in range(4)])
    shared['fng'] = f(inputs['final_norm_g'])
    xs = f(inputs['x'])
    cs = f(inputs['c'])
    cx = f(inputs['ctx'])
    maps = []
    for b in range(nb):
        m = dict(shared)
        m['x'] = np.ascontiguousarray(xs[b, :T])
        m['ctx'] = np.ascontiguousarray(cx[b])
        m['cT'] = np.ascontiguousarray(cs[b].reshape(8, 128).T)
        maps.append(m)
    return maps


def kernel(**inputs):
    T = 4096
    nb = 8
    nc = build(T)
    maps = make_in_maps(inputs, T, nb)
    res = run_bass_kernel_spmd(nc, maps, core_ids=list(range(nb)))
    return np.stack([np.asarray(res.results[b]["out"], dtype=np.float32) for b in range(nb)], 0)
```

```python
import math
from contextlib import ExitStack

import numpy as np
import ml_dtypes
import concourse.bass as bass
import concourse.mybir as mybir
from concourse.bass_utils import run_bass_kernel_spmd

F32 = mybir.dt.float32
BF16 = mybir.dt.bfloat16
U8 = mybir.dt.uint8
AF = mybir.ActivationFunctionType
ALU = mybir.AluOpType
AX = mybir.AxisListType

D = 1024
LC = 256
DEPTH = 4
EPS = 1e-6
ENG = ('pe', 'act', 'dve', 'pool', 'sp')
HANDLE = {'pe': 'tensor', 'act': 'scalar', 'dve': 'vector', 'pool': 'gpsimd', 'sp': 'sync'}
CH = 16000
NDSEM = {'sp': 24, 'pool': 48, 'act': 2}
SAME_SYNC = {'act', 'dve', 'pool'}
ARENA_BYTES = 204 * 1024
TRUNC = 0


class V:
    __slots__ = ('ap', 'key')

    def __init__(self, ap, key=None):
        self.ap = ap
        self.key = key

    def __getitem__(self, idx):
        return V(self.ap[idx], self.key)

    def rr(self, pat, **kw):
        return V(self.ap.rearrange(pat, **kw), self.key)

    def bc(self, dt):
        return V(self.ap.bitcast(dt), self.key)


class Op:
    __slots__ = ('eng', 'fn', 'dma', 'deps', 'sig', 'signo', 'dsem', 'dval', 'qidx')

    def __init__(self, eng, fn, dma):
        self.eng = eng
        self.fn = fn
        self.dma = dma
        self.deps = {}
        self.sig = False
        self.signo = 0
        self.dsem = None
        self.dval = 0
        self.qidx = 0


class Sched:
    def __init__(self):
        self.ops = {e: [] for e in ENG}
        self.lastw = {}
        self.rd = {}
        self.pending_dmas = []
        self.ndma = {e: 0 for e in ENG}
        self.bg = {}

    def add(self, eng, fn, r=(), w=(), dma=False, bg=False):
        op = Op(eng, fn, dma)
        deps = op.deps
        if bg:
            for x in w:
                self.bg.setdefault(x, []).append(op)
            op.qidx = self.ndma[eng]
            self.ndma[eng] += 1
            self.ops[eng].append(op)
            return op
        for x in r:
            for o in self.bg.get(x, ()):
                deps[o] = 'raw'
        w = list(w) + [x for x in r if isinstance(x, tuple) and x and x[0] == 'PS' and x not in w]

        def dep(o, kind):
            if o is op:
                return
            if kind == 'raw' or o not in deps:
                deps[o] = kind
        for x in r:
            if x is None:
                continue
            o = self.lastw.get(x)
            if o is not None:
                dep(o, 'raw')
        for x in w:
            if x is None:
                continue
            o = self.lastw.get(x)
            if o is not None:
                dep(o, 'waw')
            rdx = self.rd.get(x)
            if rdx:
                for k, o2 in rdx.items():
                    if k == 'dma':
                        for o3 in o2:
                            dep(o3, 'war')
                    else:
                        dep(o2, 'war')
        for x in r:
            if x is None:
                continue
            rdx = self.rd.setdefault(x, {})
            if dma:
                rdx.setdefault('dma', []).append(op)
            else:
                rdx[eng] = op
        for x in w:
            if x is None:
                continue
            self.lastw[x] = op
            self.rd[x] = {}
        if dma:
            op.qidx = self.ndma[eng]
            self.ndma[eng] += 1
            self.pending_dmas.append(op)
        self.ops[eng].append(op)
        return op

    def barrier(self):
        b = Op('sp', lambda e: e.nop(), False)
        for e in ENG:
            for o in reversed(self.ops[e]):
                if not o.dma:
                    b.deps[o] = 'raw'
                    break
        for o in self.pending_dmas:
            b.deps[o] = 'raw'
        self.pending_dmas = []
        self.ops['sp'].append(b)
        for e in ENG:
            if e == 'sp':
                continue
            m = Op(e, lambda en: en.nop(), False)
            m.deps[b] = 'raw'
            self.ops[e].append(m)
        self.lastw = {}
        self.rd = {}

    @staticmethod
    def _needs(op, d, kind):
        if d.dma:
            return True
        if op.dma:
            return True
        if d.eng != op.eng:
            return True
        return op.eng in SAME_SYNC

    def finalize(self):
        for e in ENG:
            for op in self.ops[e]:
                for d, kind in op.deps.items():
                    if not d.dma and self._needs(op, d, kind):
                        d.sig = True
        self.nsig = {}
        for e in ENG:
            n = 0
            for op in self.ops[e]:
                if op.sig:
                    n += 1
                    op.signo = n
            self.nsig[e] = n

    def emit(self, nc, stack, block):
        self.finalize()
        csem = {}
        for e in ENG:
            nep = max(1, (self.nsig[e] + CH - 1) // CH)
            csem[e] = [stack.enter_context(nc.semaphore(f"c_{e}_{i}")) for i in range(nep)]
        dsem = {}
        for e, n in NDSEM.items():
            dsem[e] = [stack.enter_context(nc.semaphore(f"d_{e}_{i}")) for i in range(n)]
        for e in ENG:
            for op in self.ops[e]:
                if op.dma:
                    k = NDSEM[e]
                    op.dsem = dsem[e][op.qidx % k]
                    op.dval = 16 * (op.qidx // k + 1)
        sched = self

        def make_body(eng):
            def body(e):
                known = {f: 0 for f in ENG}
                knownd = {}
                for op in sched.ops[eng]:
                    waits = []
                    for d, kind in op.deps.items():
                        if not sched._needs(op, d, kind):
                            continue
                        if d.dma:
                            kk = id(d.dsem)
                            if knownd.get(kk, 0) < d.dval:
                                knownd[kk] = d.dval
                                waits.append((d.dsem, d.dval))
                        else:
                            n = d.signo
                            if known[d.eng] >= n:
                                continue
                            known[d.eng] = n
                            waits.append((csem[d.eng][(n - 1) // CH], (n - 1) % CH + 1))
                    if op.dma and op.dval > 16:
                        kk = id(op.dsem)
                        if knownd.get(kk, 0) < op.dval - 16:
                            knownd[kk] = op.dval - 16
                            waits.append((op.dsem, op.dval - 16))
                    best = {}
                    for sm, v in waits:
                        kk = id(sm)
                        if kk not in best or best[kk][1] < v:
                            best[kk] = (sm, v)
                    for sm, v in best.values():
                        e.wait_ge(sm, v)
                    ins = op.fn(e)
                    if op.dma:
                        ins.then_inc(op.dsem, 16)
                    elif op.sig:
                        n = op.signo
                        ins.then_inc(csem[eng][(n - 1) // CH], 1)
            return body
        for eng in ENG:
            getattr(block, HANDLE[eng])(make_body(eng))


class Arena:
    def __init__(self, ap_u8, nbytes):
        self.ap = ap_u8
        self.n = nbytes
        self.off = 0
        self.gen = 0

    def reset(self):
        self.off = 0
        self.gen += 1

    def alloc(self, free_shape, dt, name):
        esz = 2 if dt == BF16 else 4
        n = esz
        for s in free_shape:
            n *= s
        n_al = (n + 63) // 64 * 64
        assert self.off + n_al <= self.n, f"arena overflow at {name}: {self.off}+{n_al} > {self.n}"
        ap = self.ap[:, self.off:self.off + n].bitcast(dt)
        self.off += n_al
        if len(free_shape) == 2:
            ap = ap.rearrange("p (a b) -> p a b", b=free_shape[1])
        elif len(free_shape) == 3:
            ap = ap.rearrange("p (a b c) -> p a b c", b=free_shape[1], c=free_shape[2])
        return V(ap, (name, self.gen))


class K:
    def __init__(self, nc, S):
        self.nc = nc
        self.S = S

    @staticmethod
    def _keys(*vs):
        out = []
        for v in vs:
            if isinstance(v, V) and v.key is not None:
                if isinstance(v.key, list):
                    out.extend(v.key)
                else:
                    out.append(v.key)
        return out

    def mm(self, out, lhsT, rhs, start=True, stop=True):
        o, a, b = out.ap, lhsT.ap, rhs.ap
        self.S.add('pe', lambda e: e.matmul(o, lhsT=a, rhs=b, start=start, stop=stop),
                   r=self._keys(lhsT, rhs), w=self._keys(out))

    def tr(self, out, in_, ident):
        o, a, b = out.ap, in_.ap, ident.ap
        self.S.add('pe', lambda e: e.transpose(o, a, b), r=self._keys(in_, ident), w=self._keys(out))

    def act(self, out, in_, func, scale=None, bias=None, accum=None):
        o, a = out.ap, in_.ap
        kw = {}
        rk = [in_]
        wk = [out]
        if scale is not None:
            if isinstance(scale, V):
                kw['scale'] = scale.ap
                rk.append(scale)
            else:
                kw['scale'] = float(scale)
        if bias is not None:
            if isinstance(bias, V):
                kw['bias'] = bias.ap
                rk.append(bias)
            else:
                kw['bias'] = float(bias)
        if accum is not None:
            kw['accum_out'] = accum.ap
            wk.append(accum)
        self.S.add('act', lambda e: e.activation(o, a, func, **kw), r=self._keys(*rk), w=self._keys(*wk))

    def ts(self, eng, out, in0, s1, s2=None, op0=ALU.mult, op1=None):
        o, a = out.ap, in0.ap
        rk = [in0]
        if isinstance(s1, V):
            rk.append(s1)
            s1 = s1.ap
        if isinstance(s2, V):
            rk.append(s2)
            s2 = s2.ap
        if op1 is None:
            fn = lambda e: e.tensor_scalar(o, a, s1, None, op0)
        else:
            fn = lambda e: e.tensor_scalar(o, a, s1, s2, op0, op1)
        self.S.add(eng, fn, r=self._keys(*rk), w=self._keys(out))

    def tt(self, eng, out, in0, in1, op):
        o, a, b = out.ap, in0.ap, in1.ap
        self.S.add(eng, lambda e: e.tensor_tensor(o, a, b, op), r=self._keys(in0, in1), w=self._keys(out))

    def stt(self, out, in0, scalar, in1, op0, op1):
        o, a, b = out.ap, in0.ap, in1.ap
        rk = [in0, in1]
        if isinstance(scalar, V):
            rk.append(scalar)
            scalar = scalar.ap
        self.S.add('dve', lambda e: e.scalar_tensor_tensor(o, a, scalar, b, op0, op1),
                   r=self._keys(*rk), w=self._keys(out))

    def copy(self, eng, out, in_):
        o, a = out.ap, in_.ap
        if eng == 'act':
            fn = lambda e: e.copy(o, a)
        else:
            fn = lambda e: e.tensor_copy(o, a)
        self.S.add(eng, fn, r=self._keys(in_), w=self._keys(out))

    def recip(self, out, in_):
        o, a = out.ap, in_.ap
        self.S.add('dve', lambda e: e.reciprocal(o, a), r=self._keys(in_), w=self._keys(out))

    def memset(self, eng, out, val):
        o = out.ap
        self.S.add(eng, lambda e: e.memset(o, val), w=self._keys(out))

    def reduce(self, out, in_, op=ALU.add):
        o, a = out.ap, in_.ap
        self.S.add('dve', lambda e: e.tensor_reduce(o, a, AX.X, op), r=self._keys(in_), w=self._keys(out))

    def bn_stats(self, out, in_):
        o, a = out.ap, in_.ap
        self.S.add('dve', lambda e: e.bn_stats(o, a), r=self._keys(in_), w=self._keys(out))

    def bn_aggr(self, out, in_):
        o, a = out.ap, in_.ap
        self.S.add('dve', lambda e: e.bn_aggr(o, a), r=self._keys(in_), w=self._keys(out))

    def dma(self, q, out, in_, bg=False):
        o, a = out.ap, in_.ap
        self.S.add(q, lambda e: e.dma_start(out=o, in_=a), r=self._keys(in_), w=self._keys(out), dma=True, bg=bg)

    def barrier(self):
        self.S.barrier()


def host_consts(T):
    t = np.arange(T)
    row = (t // 64).astype(np.float32)
    col = (t % 64).astype(np.float32)
    p = np.arange(128)
    m = p % 64
    quarter, half = 16, 32
    inv = (np.float32(10000.0) ** (-(np.arange(quarter, dtype=np.float32)) / np.float32(quarter))).astype(np.float32)
    isrow = m < half
    mloc = np.where(isrow, m, m - half)
    f = mloc % quarter
    first = mloc < quarter
    pos = np.where(isrow[:, None], row[None, :], col[None, :]).astype(np.float32)
    ang = (pos * inv[f][:, None]).astype(np.float32)
    sgn = np.where(first, -1.0, 1.0)[:, None]
    AC = np.cos(ang.astype(np.float64)).astype(np.float32)
    AS = (np.sin(ang.astype(np.float64)) * sgn).astype(np.float32)
    partner = np.where(first, p + quarter, p - quarter)
    permA = np.zeros((128, 128), np.float32)
    permA[partner, p] = 1.0
    quarter = 64
    inv = (np.float32(10000.0) ** (-(np.arange(quarter, dtype=np.float32)) / np.float32(quarter))).astype(np.float32)
    f = p % 64
    first = p < 64
    sgn = np.where(first, -1.0, 1.0)[:, None]
    angr = (row[None, :] * inv[f][:, None]).astype(np.float32).astype(np.float64)
    angc = (col[None, :] * inv[f][:, None]).astype(np.float32).astype(np.float64)
    RT = np.stack([np.cos(angr), np.sin(angr) * sgn, np.cos(angc), np.sin(angc) * sgn], 0).astype(np.float32)
    partner = np.where(first, p + 64, p - 64)
    permR = np.zeros((128, 128), np.float32)
    permR[partner, p] = 1.0
    k = np.arange(128)[:, None].astype(np.float32)
    q = np.arange(128)[None, :].astype(np.float32)
    rc = np.stack([np.maximum(q - k, 0), (q >= k).astype(np.float32),
                   np.maximum(k - q, 0), (k > q).astype(np.float32),
                   np.broadcast_to(q + 1, (128, 128)), np.broadcast_to(128 - q, (128, 128))], 1).astype(np.float32)
    ze = np.zeros((128, 8), np.float32)
    ze[:, 0:4] = (127 - np.arange(128))[:, None]
    ze[:, 4:8] = np.arange(128)[:, None]
    bf = ml_dtypes.bfloat16
    return dict(AC=AC, AS=AS, RT=np.ascontiguousarray(RT), permA=permA.astype(bf), permR=permR.astype(bf),
                rcon=np.ascontiguousarray(rc), zec=ze, identb=np.eye(128, dtype=np.float32).astype(bf),
                identf=np.eye(128, dtype=np.float32))


def build(T, layers=(0, 1, 2, 3), dbg=(), stop_after=None):
    NB = T // 512
    NTL = T // 128
    NTOK = T + LC
    NT = NTOK // 128
    nc = bass.Bass("TRN2", target_bir_lowering=False)
    S = Sched()
    kb = K(nc, S)

    def din(name, shape, dt=F32):
        return nc.dram_tensor(name, shape, dt, kind="ExternalInput").ap()

    def dscr(name, shape, dt):
        kind = "ExternalOutput" if name in dbg else "Internal"
        return nc.dram_tensor(name, shape, dt, kind=kind).ap()

    x = din("x", [T, D])
    ctx = din("ctx", [LC, D])
    cT = din("cT", [128, 8])
    cctxT = din("cctxT", [128, 8])
    ada_w = din("ada_w", [4, 128, 8, 6144])
    ada_b = din("ada_b", [4, 6144])
    wqkv = din("wqkv", [2, 128, 8, 3072])
    wo_a = din("wo_a", [2, 128, 8, 1024])
    lam = din("lam", [2, 256])
    subg = din("subg", [128, 2])
    win = din("win", [2, 128, 8, 6144])
    wo_r = din("wo_r", [2, 128, 16, 1024])
    dlog = din("dlog", [2, 8])
    w1 = din("w1", [4, 128, 8, 4096])
    w2 = din("w2", [4, 128, 32, 1024])
    fng = din("fng", [1024])
    AC = din("AC", [128, T])
    AS = din("AS", [128, T])
    RT = din("RT", [4, 128, T])
    permA_d = din("permA", [128, 128], BF16)
    permR_d = din("permR", [128, 128], BF16)
    rcon_d = din("rcon", [128, 6, 128])
    zec_d = din("zec", [128, 8])
    identb_d = din("identb", [128, 128], BF16)
    identf_d = din("identf", [128, 128])
    out = nc.dram_tensor("out", [T, D], F32, kind="ExternalOutput").ap()

    hbuf = dscr("hbuf", [NTOK, D], F32)
    QT = dscr("QT", [1024, NTOK], BF16)
    KT = dscr("KT", [1024, NTOK], BF16)
    VA = dscr("VA", [NTOK, 1024], BF16)
    VR = dscr("VR", [NTOK, 2048], BF16)
    GS = dscr("GS", [NTOK, 2048], BF16)
    KZ = dscr("KZ", [NTOK, 2048], BF16)
    OB = dscr("OB", [NTOK, 2048], F32)
    OT = dscr("OT", [2048, NTOK], BF16)
    modrow = dscr("modrow", [4, 2, 6144], F32)
    UT = dscr("UT", [1024, NTOK], BF16)
    wsrc = {'wqkv': (wqkv, [2, 128, 8, 3072]), 'wo_a': (wo_a, [2, 128, 8, 1024]), 'win': (win, [2, 128, 8, 6144]),
            'wo_r': (wo_r, [2, 128, 16, 1024]), 'w1': (w1, [4, 128, 8, 4096]), 'w2': (w2, [4, 128, 32, 1024])}
    wbf = {k_: nc.dram_tensor(k_ + "_bf", shp, BF16, kind="Internal").ap() for k_, (_, shp) in wsrc.items()}

    stack = ExitStack()
    with stack:
        arena_t = stack.enter_context(nc.sbuf_tensor("arena", [128, ARENA_BYTES], U8))
        A = Arena(arena_t[:], ARENA_BYTES)

        def pers(name, shape, dt):
            tt_ = stack.enter_context(nc.sbuf_tensor(name, [128] + shape, dt))
            return V(tt_[:], name)
        modT = pers("modT", [4, 2, 48], F32)
        identb = pers("identb_s", [128], BF16)
        identf = pers("identf_s", [128], F32)
        onesb = pers("onesb", [128], BF16)
        onesf = pers("onesf", [128], F32)
        permA = pers("permA_s", [128], BF16)
        permR = pers("permR_s", [128], BF16)
        epst = pers("epst", [2], F32)
        psall_t = stack.enter_context(nc.psum_tensor("psall", [128, 8, 512], F32))
        psall = psall_t[:]
        PS = [V(psall[:, i, :], ('PS', i)) for i in range(8)]

        def ps2(j):
            return V(psall[:, 2 * j:2 * j + 2, :], [('PS', 2 * j), ('PS', 2 * j + 1)])

        def psb(i):
            return PS[i].bc(BF16)

        kb.dma('sp', identb, V(identb_d))
        kb.dma('sp', identf, V(identf_d))
        kb.dma('sp', permA, V(permA_d))
        kb.dma('sp', permR, V(permR_d))
        kb.memset('dve', onesb, 1.0)
        kb.memset('dve', onesf, 1.0)
        kb.memset('dve', epst[:, 0:1], EPS)
        kb.memset('dve', epst[:, 1:2], 1e-5)

        blocks = [(i * 512, 512, False) for i in range(NB)] + [(T, 256, True)]

        def hsrc(l, tok0, n=128):
            if l == layers[0] and l == 0:
                if tok0 >= T:
                    return V(ctx[tok0 - T:tok0 - T + n, :])
                return V(x[tok0:tok0 + n, :])
            return V(hbuf[tok0:tok0 + n, :])

        def p0():
            A.reset()
            cc0 = A.alloc([8], F32, 'cc0')
            cc1 = A.alloc([8], F32, 'cc1')
            sc = A.alloc([8, 2], F32, 'sc')
            adar = A.alloc([6144], F32, 'adar')
            mrows = [A.alloc([6144], F32, f'mrows{i}') for i in range(2)]
            pieces = [A.alloc([8, 512], F32, f'piece{i}') for i in range(4)]
            kb.dma('sp', cc0, V(cT))
            kb.dma('sp', cc1, V(cctxT))
            kb.act(sc[:, :, 0], cc0, AF.Silu)
            kb.act(sc[:, :, 1], cc1, AF.Silu)
            pi = 0
            for l in range(4):
                mr = mrows[l % 2]
                kb.dma('sp', adar[0:2, :], V(ada_b[l].partition_broadcast(2)))
                for n0 in range(12):
                    pc = pieces[pi % 4]
                    pi += 1
                    kb.dma('sp', pc, V(ada_w[l, :, :, n0 * 512:(n0 + 1) * 512]))
                    ps = PS[n0 % 2]
                    for k in range(8):
                        kb.mm(ps[0:2, :], sc[:, k, :], pc[:, k, :], start=(k == 0), stop=(k == 7))
                    kb.tt('dve', mr[0:2, n0 * 512:(n0 + 1) * 512], ps[0:2, :], adar[0:2, n0 * 512:(n0 + 1) * 512], ALU.add)
                kb.dma('sp', V(modrow[l]), mr[0:2, :])
                for ci in range(48):
                    kb.tr(PS[2][:, ci:ci + 49:48], mr[0:2, ci * 128:(ci + 1) * 128], identf[0:2, 0:2])
                kb.copy('act', modT[:, l].rr("p b c -> p (b c)"), PS[2][:, 0:96])
                for jm in (1, 4):
                    kb.ts('dve', modT[:, l, :, jm * 8:(jm + 1) * 8], modT[:, l, :, jm * 8:(jm + 1) * 8], 1.0, None, ALU.add)
            kb.barrier()

        def prologue_s1(ht, t, bufs):
            junk, ssq, lnv, rstd, xns = bufs
            xn = xns[t % len(xns)]
            kb.act(junk, ht, AF.Square, accum=ssq[:, t:t + 1])
            kb.act(lnv[:, t:t + 1], ssq[:, t:t + 1], AF.Ln, scale=1.0 / D, bias=epst[:, 0:1])
            kb.act(rstd[:, t:t + 1], lnv[:, t:t + 1], AF.Exp, scale=-0.5)
            kb.ts('dve', xn, ht, rstd[:, t:t + 1], None, ALU.mult)

        def prologue_s2(l, a, b, t, u, bufs, banks=(7,)):
            junk, ssq, lnv, rstd, xns = bufs
            xn = xns[t % len(xns)]
            pT = psb(banks[t % len(banks)])
            for c in range(8):
                kb.tr(pT[:, c * 128:(c + 1) * 128], xn[:, c * 128:(c + 1) * 128], identb)
            for c in range(8):
                shift = modT[:, l, b, (3 * a) * 8 + c:(3 * a) * 8 + c + 1]
                scl = modT[:, l, b, (3 * a + 1) * 8 + c:(3 * a + 1) * 8 + c + 1]
                if c % 2 == 0:
                    kb.ts('dve', u[:, c, t * 128:(t + 1) * 128], pT[:, c * 128:(c + 1) * 128], scl, shift, ALU.mult, ALU.add)
                else:
                    kb.act(u[:, c, t * 128:(t + 1) * 128], pT[:, c * 128:(c + 1) * 128], AF.Identity, scale=scl, bias=shift)

        def prologue(l, a, b, ht, t, u, bufs):
            prologue_s1(ht, t, bufs)
            prologue_s2(l, a, b, t, u, bufs)

        def prologue_bufs(nx=2):
            junk = A.alloc([1024], BF16, 'junk')
            ssq = A.alloc([4], F32, 'ssq')
            lnv = A.alloc([4], F32, 'lnv')
            rstd = A.alloc([4], F32, 'rstd')
            xns = [A.alloc([1024], BF16, f'xn{i}') for i in range(nx)]
            return (junk, ssq, lnv, rstd, xns)

        conv_pending = []

        def convert_w(name, idx, defer=False):
            src, shp = wsrc[name]
            for c in range(shp[2]):
                def one(c=c):
                    kb.dma('pool', V(wbf[name][idx, :, c, :], ('wbf', name, idx)), V(src[idx, :, c, :]), bg=True)
                if defer:
                    conv_pending.append(one)
                else:
                    one()

        def emit_conversions(frac):
            n = int(math.ceil(len(conv_pending) * frac)) if conv_pending else 0
            for _ in range(min(n, len(conv_pending))):
                conv_pending.pop(0)()

        def load_w(dst, name, idx, nk):
            ncol = wsrc[name][1][3]
            g = max(1, min(nk, (32 * 1024) // (ncol * 2)))
            for c0 in range(0, nk, g):
                c1 = min(nk, c0 + g)
                kb.dma('sp', dst[:, c0:c1, :], V(wbf[name][idx, :, c0:c1, :], ('wbf', name, idx)))

        def p1_attn(l):
            j = l // 2
            A.reset()
            W = A.alloc([8, 3072], BF16, 'W')
            load_w(W, 'wqkv', j, 8)
            pb = prologue_bufs(4)
            hts = [A.alloc([1024], F32, f'ht{i}') for i in range(4)]
            uTs = [A.alloc([8, 512], BF16, f'uT{i}') for i in range(2)]
            Cb = [A.alloc([512], F32, f'Cb{i}') for i in range(2)]
            Sb = [A.alloc([512], F32, f'Sb{i}') for i in range(2)]
            qraw = [A.alloc([512], BF16, f'qraw{i}') for i in range(2)]
            t1 = [A.alloc([512], F32, f't1{i}') for i in range(2)]
            t2 = [A.alloc([512], F32, f't2{i}') for i in range(2)]
            qo = [A.alloc([512], BF16, f'qo{i}') for i in range(3)]
            vo = [A.alloc([1024], BF16, f'vo{i}') for i in range(2)]
            qi = 0
            vi = 0

            def loads(bi_):
                tok0_, ntok_, isctx_ = blocks[bi_]
                for t in range(ntok_ // 128):
                    kb.dma('sp', hts[t], hsrc(l, tok0_ + t * 128))
                if not isctx_:
                    kb.dma('sp', Cb[bi_ % 2], V(AC[:, tok0_:tok0_ + 512]))
                    kb.dma('sp', Sb[bi_ % 2], V(AS[:, tok0_:tok0_ + 512]))

            def s1(bi_):
                for t in range(blocks[bi_][1] // 128):
                    prologue_s1(hts[t], t, pb)

            def s2(bi_):
                for t in range(blocks[bi_][1] // 128):
                    prologue_s2(l, 0, 1 if blocks[bi_][2] else 0, t, uTs[bi_ % 2], pb, banks=(6, 7))
            loads(0)
            s1(0)
            s2(0)
            if len(blocks) > 1:
                loads(1)
            for bi, (tok0, ntok, isctx) in enumerate(blocks):
                nt = ntok // 128
                u = uTs[bi % 2]
                pend = None
                for jj in range(16):
                    if jj == 6 and bi + 1 < len(blocks):
                        s1(bi + 1)
                    ps = PS[jj % 2]
                    for c in range(8):
                        kb.mm(ps[:, :ntok], W[:, c, jj * 128:(jj + 1) * 128], u[:, c, :ntok], start=(c == 0), stop=(c == 7))
                    dst = (QT if jj < 8 else KT)[(jj % 8) * 128:(jj % 8 + 1) * 128, tok0:tok0 + ntok]
                    q_ = qo[qi % 3]
                    qi += 1
                    if isctx:
                        kb.copy('act', q_[:, :ntok], ps[:, :ntok])
                        kb.dma('sp', V(dst), q_[:, :ntok])
                    else:
                        qr = qraw[jj % 2]
                        kb.copy('act', qr, ps)
                        kb.tt('dve', t1[jj % 2], ps, Cb[bi % 2], ALU.mult)

                        def fin(jj=jj, qr=qr, q_=q_, dst=dst):
                            ps2 = PS[2 + jj % 2]
                            kb.mm(ps2, permA, qr)
                            kb.tt('dve', t2[jj % 2], ps2, Sb[bi % 2], ALU.mult)
                            kb.tt('pool', q_, t1[jj % 2], t2[jj % 2], ALU.add)
                            kb.dma('sp', V(dst), q_)
                        if pend is not None:
                            pend()
                        pend = fin
                if pend is not None:
                    pend()
                nt_next = blocks[bi + 1][1] // 128 if bi + 1 < len(blocks) else 0
                for t in range(max(nt, nt_next)):
                    if t < nt_next:
                        prologue_s2(l, 0, 1 if blocks[bi + 1][2] else 0, t, uTs[(bi + 1) % 2], pb, banks=(6, 7))
                        if t == nt_next - 1 and bi + 2 < len(blocks):
                            loads(bi + 2)
                    if t >= nt:
                        continue
                    v_ = vo[vi % 2]
                    vi += 1
                    for n in range(2):
                        ps = PS[4 + n]
                        for c in range(8):
                            kb.mm(ps, u[:, c, t * 128:(t + 1) * 128], W[:, c, 2048 + n * 512:2048 + (n + 1) * 512],
                                  start=(c == 0), stop=(c == 7))
                        kb.copy('act' if n == 0 else 'dve', v_[:, n * 512:(n + 1) * 512], ps)
                    kb.dma('sp', V(VA[tok0 + t * 128:tok0 + (t + 1) * 128, :]), v_)
            kb.barrier()

        def p2_attn(l):
            j = l // 2
            lam_init = 0.8 - 0.6 * math.exp(-0.3 * l)
            A.reset()
            lamt = A.alloc([256], F32, 'lamt')
            prod = A.alloc([2, 64], F32, 'prod')
            s2 = A.alloc([2], F32, 's2')
            e2 = A.alloc([2], F32, 'e2')
            neglam = A.alloc([1], F32, 'neglam')
            gsub = A.alloc([1], F32, 'gsub')
            sgt = A.alloc([2], F32, 'sgt')
            kb.dma('sp', lamt, V(lam[j].partition_broadcast(128)))
            kb.dma('sp', sgt, V(subg))
            kb.tt('dve', prod[:, 0, :], lamt[:, 0:64], lamt[:, 64:128], ALU.mult)
            kb.tt('dve', prod[:, 1, :], lamt[:, 128:192], lamt[:, 192:256], ALU.mult)
            kb.reduce(s2, prod)
            kb.act(e2, s2, AF.Exp)
            kb.tt('dve', neglam, e2[:, 1:2], e2[:, 0:1], ALU.subtract)
            kb.ts('dve', neglam, neglam, -lam_init, None, ALU.add)
            kb.ts('dve', gsub, sgt[:, j:j + 1], 1.0 - lam_init, None, ALU.mult)
            QTm = [[A.alloc([NTOK], BF16, f'QTm{r_}_{i}') for i in range(2)] for r_ in range(2)]
            KTh = [A.alloc([NTOK], BF16, f'KTh{i}') for i in range(2)]
            Vh = [A.alloc([NT, 128], BF16, f'Vh{i}') for i in range(2)]
            NPT = 6
            LAG = 2
            PT = [A.alloc([2, 512], BF16, f'PT{i}') for i in range(NPT)]
            accD = [A.alloc([2, 512], F32, f'accD{i}') for i in range(4)]
            accP = [A.alloc([2, 512], F32, f'accP{i}') for i in range(4)]
            rec = [A.alloc([512], F32, f'rec{i}') for i in range(4)]
            oa2 = [A.alloc([512], F32, f'oa{i}') for i in range(2)]
            ob2 = [A.alloc([512], F32, f'ob{i}') for i in range(2)]
            oo2 = [A.alloc([512], F32, f'oo{i}') for i in range(2)]
            sq2 = [A.alloc([512], BF16, f'sq{i}') for i in range(2)]
            lnq2 = [A.alloc([512], F32, f'lnq{i}') for i in range(2)]
            rs2 = [A.alloc([512], F32, f'rs{i}') for i in range(2)]
            oT = [A.alloc([512], BF16, f'oT{i}') for i in range(2)]
            qblocks = [(i * 512, 512, list(range(NT))) for i in range(NB)] + [(T, 256, [NTL, NTL + 1])]
            nqb = len(qblocks)
            for r_ in range(2):
                for i in range(2):
                    kb.memset('dve', QTm[r_][i], 0.0)

            def load_head(h):
                for i in range(2):
                    kb.dma('sp', QTm[h % 2][i][64 * i:64 * i + 64, :], V(QT[h * 128 + 64 * i:h * 128 + 64 * i + 64, :]))
                kb.dma('sp', KTh[h % 2], V(KT[h * 128:(h + 1) * 128, :]))
                kb.dma('sp', Vh[h % 2], V(VA[:, h * 128:(h + 1) * 128].rearrange("(t p) d -> p t d", p=128)))

            def pso_bank(g, i):
                return PS[4 + i]

            def den_bank(g, i):
                return PS[6 + (2 * g + i) % 2]

            steps = []
            for h in range(8):
                for qb, (q0, nq, keys) in enumerate(qblocks):
                    for i in range(2):
                        nss = len(keys) // 2
                        for ss in range(nss):
                            steps.append((h, qb, i, ss, keys[2 * ss], keys[2 * ss + 1], q0, nq, nss))
            deferred = []

            def epilogue_a(g, h, q0, nq):
                p1_ = pso_bank(g, 1)
                r1 = rec[(g % 2) * 2 + 1]
                oa, ob_, oo = oa2[g % 2], ob2[g % 2], oo2[g % 2]
                kb.tt('dve', ob_[:, :nq], p1_[:, :nq], r1[:, :nq], ALU.mult)
                kb.stt(oo[:, :nq], ob_[:, :nq], neglam[:, 0:1], oa[:, :nq], ALU.mult, ALU.add)

            def epilogue_b(g, h, q0, nq):
                kb.act(sq2[g % 2][:, :nq], oo2[g % 2][:, :nq], AF.Square)

            def epilogue_c(g, h, q0, nq):
                kb.mm(PS[7][:, :nq], onesb, sq2[g % 2][:, :nq])
                kb.act(lnq2[g % 2][:, :nq], PS[7][:, :nq], AF.Ln, scale=1.0 / 128, bias=epst[:, 0:1])
                kb.act(rs2[g % 2][:, :nq], lnq2[g % 2][:, :nq], AF.Exp, scale=-0.5)

            def epilogue_d(g, h, q0, nq):
                o_ = oT[g % 2]
                kb.stt(o_[:, :nq], oo2[g % 2][:, :nq], gsub[:, 0:1], rs2[g % 2][:, :nq], ALU.mult, ALU.mult)
                kb.dma('sp', V(OT[h * 128:(h + 1) * 128, q0:q0 + nq]), o_[:, :nq])

            load_head(0)
            ns = len(steps)
            s_i = 0
            while s_i < ns + LAG or deferred:
                if s_i < ns:
                    (h, qb, i, ss, kt0, kt1, q0, nq, nss) = steps[s_i]
                    sc_ = ps2(s_i % 2)
                    for jj, kt in enumerate((kt0, kt1)):
                        kb.mm(sc_[:, jj, :nq], KTh[h % 2][:, kt * 128:(kt + 1) * 128], QTm[h % 2][i][:, q0:q0 + nq])
                    kb.act(PT[s_i % NPT][:, :, :nq], sc_[:, :, :nq], AF.Exp, scale=0.125)
                if LAG <= s_i < ns + LAG:
                    (h, qb, i, ss, kt0, kt1, q0, nq, nss) = steps[s_i - LAG]
                    p_ = PT[(s_i - LAG) % NPT]
                    if qb == 0 and i == 0 and ss == 0 and h + 1 < 8:
                        load_head(h + 1)
                        emit_conversions(1.0 / (7 - h))
                    g = h * nqb + qb
                    pso = pso_bank(g, i)
                    dnb = den_bank(g, i)
                    aD = accD[(g % 2) * 2 + i]
                    aP = accP[(g % 2) * 2 + i]
                    for jj, kt in enumerate((kt0, kt1)):
                        kb.mm(pso[:, :nq], Vh[h % 2][:, kt, :], p_[:, jj, :nq],
                              start=(ss == 0 and jj == 0), stop=(ss == nss - 1 and jj == 1))
                    if ss % 4 != 3:
                        if ss == 0:
                            kb.copy('dve', aD[:, :, :nq], p_[:, :, :nq])
                        else:
                            kb.tt('dve', aD[:, :, :nq], aD[:, :, :nq], p_[:, :, :nq], ALU.add)
                    else:
                        for jj in range(2):
                            kb.mm(dnb[:, :nq], onesb, p_[:, jj, :nq], start=(ss == 3 and jj == 0), stop=False)
                    if ss == nss - 1:
                        def den(g=g, i=i, nq=nq, aD=aD, aP=aP, nss=nss, dnb=dnb, pso=pso):
                            parts = [aD[:, 0, :nq], aD[:, 1, :nq]]
                            for pi_, pa in enumerate(parts):
                                kb.mm(dnb[:, :nq], onesf, pa, start=(pi_ == 0 and nss <= 3), stop=(pi_ == len(parts) - 1))
                            kb.recip(rec[(g % 2) * 2 + i][:, :nq], dnb[:, :nq])
                            if i == 0:
                                kb.tt('dve', oa2[g % 2][:, :nq], pso[:, :nq], rec[(g % 2) * 2][:, :nq], ALU.mult)
                        if nss > 3 and (i == 0 or qb < nqb - 2):
                            dly = 3
                        else:
                            dly = 1
                        deferred.append((s_i + dly, den))
                        if i == 1:
                            args = (g, h, q0, nq)
                            ob_, oc_, od_ = (5, 8, 11) if nss >= 16 else (2, 4, 6)
                            deferred.append((s_i + dly, lambda a=args: epilogue_a(*a)))
                            deferred.append((s_i + dly + ob_, lambda a=args: epilogue_b(*a)))
                            deferred.append((s_i + dly + oc_, lambda a=args: epilogue_c(*a)))
                            deferred.append((s_i + dly + od_, lambda a=args: epilogue_d(*a)))
                rest = []
                for due, fn in deferred:
                    if due <= s_i:
                        fn()
                    else:
                        rest.append((due, fn))
                deferred = rest
                s_i += 1
            kb.barrier()

        def ret_consts(j):
            dl = A.alloc([8], F32, 'dl')
            e1 = A.alloc([8], F32, 'e1')
            lg = A.alloc([8], F32, 'lg')
            kb.dma('sp', dl, V(dlog[j].partition_broadcast(128)))
            kb.act(e1, dl, AF.Exp, scale=-1.0)
            kb.ts('dve', e1, e1, 1.0, None, ALU.add)
            kb.act(lg, e1, AF.Ln)
            kb.ts('dve', lg, lg, -1.0, None, ALU.mult)
            return lg

        def p1_ret(l):
            j = l // 2
            A.reset()
            W = A.alloc([8, 6144], BF16, 'W')
            load_w(W, 'win', j, 8)
            lg = ret_consts(j)
            ze = A.alloc([8], F32, 'ze')
            zl = A.alloc([8], F32, 'zl')
            zeta = A.alloc([8], F32, 'zeta')
            kb.dma('sp', ze, V(zec_d))
            kb.tt('dve', zl, ze, lg, ALU.mult)
            kb.act(zeta, zl, AF.Exp)
            pb = prologue_bufs(4)
            hts = [A.alloc([1024], F32, f'ht{i}') for i in range(4)]
            uTs = [A.alloc([8, 512], BF16, f'uT{i}') for i in range(2)]
            tab2 = [[A.alloc([512], F32, f'tab{r_}_{i}') for i in range(4)] for r_ in range(2)]
            qraw = [A.alloc([512], BF16, f'qraw{i}') for i in range(2)]
            t1 = [A.alloc([512], F32, f't1{i}') for i in range(2)]
            t2 = [A.alloc([512], F32, f't2{i}') for i in range(2)]
            qk = A.alloc([16, 512], BF16, 'qkT')
            kzo = [A.alloc([2048], BF16, f'kzo{i}') for i in range(2)]
            vgo = [A.alloc([2048], BF16, f'vgo{i}') for i in range(2)]
            ki_ = 0
            vi = 0

            def loads(bi_):
                tok0_, ntok_, isctx_ = blocks[bi_]
                for t in range(ntok_ // 128):
                    kb.dma('sp', hts[t], hsrc(l, tok0_ + t * 128))
                if not isctx_:
                    for i4 in range(4):
                        kb.dma('sp', tab2[bi_ % 2][i4], V(RT[i4, :, tok0_:tok0_ + 512]))

            def s1(bi_):
                for t in range(blocks[bi_][1] // 128):
                    prologue_s1(hts[t], t, pb)

            def s2(bi_):
                for t in range(blocks[bi_][1] // 128):
                    prologue_s2(l, 0, 1 if blocks[bi_][2] else 0, t, uTs[bi_ % 2], pb, banks=(7, 4))
            loads(0)
            s1(0)
            s2(0)
            if len(blocks) > 1:
                loads(1)
            for bi, (tok0, ntok, isctx) in enumerate(blocks):
                nt = ntok // 128
                tab = tab2[bi % 2]
                u = uTs[bi % 2]
                pend = None
                for jj in range(16):
                    if jj == 6 and bi + 1 < len(blocks):
                        s1(bi + 1)
                    ps = PS[jj % 2]
                    col0 = jj * 128
                    for c in range(8):
                        kb.mm(ps[:, :ntok], W[:, c, col0:col0 + 128], u[:, c, :ntok], start=(c == 0), stop=(c == 7))
                    scale = 1.0 if jj < 8 else 1.0 / 16.0
                    dstv = qk[:, jj, :ntok]
                    dst = (QT if jj < 8 else KT)[(jj % 8) * 128:(jj % 8 + 1) * 128, tok0:tok0 + ntok]
                    if isctx:
                        kb.act(dstv, ps[:, :ntok], AF.Copy, scale=scale)
                        kb.dma('sp', V(dst), dstv)
                    else:
                        par = jj % 2
                        qr = qraw[jj % 2]
                        kb.act(qr, ps, AF.Copy, scale=scale)
                        kb.stt(t1[jj % 2], ps, scale, tab[2 * par], ALU.mult, ALU.mult)

                        def fin(jj=jj, qr=qr, dstv=dstv, dst=dst, par=par):
                            ps2 = PS[2 + jj % 2]
                            kb.mm(ps2, permR, qr)
                            kb.tt('dve', t2[jj % 2], ps2, tab[2 * par + 1], ALU.mult)
                            kb.tt('pool', dstv, t1[jj % 2], t2[jj % 2], ALU.add)
                            kb.dma('sp', V(dst), dstv)
                        if pend is not None:
                            pend()
                        pend = fin
                if pend is not None:
                    pend()
                nt_next = blocks[bi + 1][1] // 128 if bi + 1 < len(blocks) else 0
                for t in range(max(nt, nt_next)):
                    if t < nt:
                        pT = psb(4)
                        for c in range(8):
                            kb.tr(pT[:, c * 128:(c + 1) * 128], qk[:, 8 + c, t * 128:(t + 1) * 128], identb)
                        kz = kzo[ki_ % 2]
                        ki_ += 1
                        for dd in range(2):
                            for hh in range(4):
                                o_ = kz[:, dd * 1024 + hh * 256:dd * 1024 + (hh + 1) * 256]
                                i_ = pT[:, hh * 256:(hh + 1) * 256]
                                z_ = zeta[:, dd * 4 + hh:dd * 4 + hh + 1]
                                if dd == 0:
                                    kb.ts('dve', o_, i_, z_, None, ALU.mult)
                                else:
                                    kb.act(o_, i_, AF.Copy, scale=z_)
                        kb.dma('sp', V(KZ[tok0 + t * 128:tok0 + (t + 1) * 128, :]), kz)
                    if t < nt_next:
                        prologue_s2(l, 0, 1 if blocks[bi + 1][2] else 0, t, uTs[(bi + 1) % 2], pb, banks=(7,))
                        if t == nt_next - 1 and bi + 2 < len(blocks):
                            loads(bi + 2)
                    if t >= nt:
                        continue
                    for part in range(2):
                        vg = vgo[vi % 2]
                        vi += 1
                        for n in range(4):
                            ps = PS[5 + n % 2]
                            c0 = 2048 + part * 2048 + n * 512
                            for c in range(8):
                                kb.mm(ps, u[:, c, t * 128:(t + 1) * 128], W[:, c, c0:c0 + 512], start=(c == 0), stop=(c == 7))
                            if part == 0:
                                kb.copy('act' if n % 2 == 0 else 'dve', vg[:, n * 512:(n + 1) * 512], ps)
                            else:
                                kb.act(vg[:, n * 512:(n + 1) * 512], ps, AF.Silu)
                        kb.dma('sp', V((VR if part == 0 else GS)[tok0 + t * 128:tok0 + (t + 1) * 128, :]), vg)
            kb.barrier()

        def p2_ret(l):
            j = l // 2
            need_ctx = l < DEPTH - 1
            A.reset()
            lg = ret_consts(j)
            rc = A.alloc([6, 128], F32, 'rc')
            kb.dma('sp', rc, V(rcon_d))
            gch = A.alloc([8], F32, 'gch')
            kb.act(gch, lg, AF.Exp, scale=128.0)
            mtmp = A.alloc([128], F32, 'mtmp')
            maskT = [[A.alloc([128], F32, f'mask{d}{hh}') for hh in range(4)] for d in range(2)]
            xi2 = [[A.alloc([2, 128], F32, f'xi{d}{hh}') for hh in range(4)] for d in range(2)]
            for d in range(2):
                for hh in range(4):
                    lcol = lg[:, d * 4 + hh:d * 4 + hh + 1]
                    kb.act(mtmp, rc[:, 2 * d, :], AF.Exp, scale=lcol)
                    kb.tt('dve', maskT[d][hh], mtmp, rc[:, 2 * d + 1, :], ALU.mult)
                    for j2 in range(2):
                        kb.act(xi2[d][hh][:, j2, :], rc[:, 4 + d, :], AF.Exp, scale=lcol)
            S32 = [A.alloc([2, 512], F32, f'S32_{hh}') for hh in range(4)]
            Sbf = [[A.alloc([2, 512], BF16, f'Sbf{hh}_{pp}') for pp in range(2)] for hh in range(4)]
            QTc = [A.alloc([8, 128], BF16, f'QTc{i}') for i in range(2)]
            KTc = [A.alloc([8, 128], BF16, f'KTc{i}') for i in range(2)]
            Kzc = [A.alloc([1024], BF16, f'Kzc{i}') for i in range(2)]
            Vc = [A.alloc([2048], BF16, f'Vc{i}') for i in range(2)]
            obl = [A.alloc([2048], F32, f'obl{i}') for i in range(2)]
            gsl = [A.alloc([2048], BF16, f'gsl{i}') for i in range(2)]
            AT = [A.alloc([128], BF16, f'AT{i}') for i in range(4)]
            Qx = [A.alloc([2, 128], BF16, f'Qx{i}') for i in range(4)]
            obw = [A.alloc([2048], F32, f'obw{i}') for i in range(2)]
            o32s = [A.alloc([2048], F32, f'o32_{i}') for i in range(2)]
            st6 = A.alloc([4, 6], F32, 'st6')
            mvs = [A.alloc([4, 2], F32, f'mv{i}') for i in range(2)]
            lnv4 = A.alloc([4], F32, 'lnv4')
            rs4 = A.alloc([4], F32, 'rs4')
            nmr4 = A.alloc([4], F32, 'nmr4')
            ons = [A.alloc([2048], BF16, f'on{i}') for i in range(2)]
            gos = [A.alloc([2048], BF16, f'go{i}') for i in range(2)]
            goT = [A.alloc([16, 128], BF16, f'goT{i}') for i in range(2)]
            QTv = QT.rearrange("(j p) t -> p j t", p=128)
            KTv = KT.rearrange("(j p) t -> p j t", p=128)
            OTv = OT.rearrange("(j p) t -> p j t", p=128)

            def p2r_loads(d_, order_, ci_):
                cidx_ = order_[ci_]
                isctx_ = cidx_ >= NTL
                do_out_ = (not isctx_) or need_ctx
                last_ = ci_ == len(order_) - 1
                tok0_ = cidx_ * 128
                rb_ = ci_ % 2
                if do_out_:
                    kb.dma('sp', QTc[rb_], V(QTv[:, :, tok0_:tok0_ + 128]))
                    kb.dma('sp', KTc[rb_], V(KTv[:, :, tok0_:tok0_ + 128]))
                if not last_:
                    kb.dma('sp', Kzc[rb_], V(KZ[tok0_:tok0_ + 128, d_ * 1024:(d_ + 1) * 1024]))
                kb.dma('sp', Vc[rb_], V(VR[tok0_:tok0_ + 128, :]))
                if d_ == 0 and do_out_:
                    kb.dma('sp', obl[rb_], V(OB[tok0_:tok0_ + 128, :]))
                    kb.dma('sp', gsl[rb_], V(GS[tok0_:tok0_ + 128, :]))
            for d in (1, 0):
                for hh in range(4):
                    kb.memset('dve', S32[hh], 0.0)
                    kb.memset('pool', Sbf[hh][0], 0.0)
                if d == 0:
                    order = [NTL, NTL + 1] + list(range(NTL))
                else:
                    order = [NTL + 1, NTL] + list(range(NTL - 1, -1, -1))
                pend_tail = [None]
                for ci, cidx in enumerate(order):
                    isctx = cidx >= NTL
                    do_out = (not isctx) or need_ctx
                    last = ci == len(order) - 1
                    tok0 = cidx * 128
                    rb = ci % 2
                    if ci == 0:
                        p2r_loads(d, order, 0)
                    if ci + 1 < len(order):
                        p2r_loads(d, order, ci + 1)
                    pp = ci % 2
                    o32 = o32s[ci % 2]
                    mv = mvs[ci % 2]
                    if do_out:
                        for hh in range(4):
                            pA = PS[hh % 2][:, 0:128]
                            kb.mm(pA, KTc[rb][:, 2 * hh, :], QTc[rb][:, 2 * hh, :], start=True, stop=False)
                            kb.mm(pA, KTc[rb][:, 2 * hh + 1, :], QTc[rb][:, 2 * hh + 1, :], start=False, stop=True)
                            kb.tt('dve', AT[hh], pA, maskT[d][hh], ALU.mult)
                            kb.tt('pool', Qx[hh], QTc[rb][:, 2 * hh:2 * hh + 2, :], xi2[d][hh], ALU.mult)
                    if not last:
                        for hh in range(4):
                            for j2 in range(2):
                                pst = PS[2 + j2]
                                kb.mm(pst, Kzc[rb][:, hh * 256 + j2 * 128:hh * 256 + (j2 + 1) * 128],
                                      Vc[rb][:, hh * 512:(hh + 1) * 512])
                                kb.stt(S32[hh][:, j2, :], S32[hh][:, j2, :], gch[:, d * 4 + hh:d * 4 + hh + 1], pst,
                                       ALU.mult, ALU.add)
                                kb.copy('act', Sbf[hh][1 - pp][:, j2, :], S32[hh][:, j2, :])
                    if not do_out:
                        continue
                    if pend_tail[0] is not None:
                        pend_tail[0]()
                        pend_tail[0] = None
                    ow = obw[ci % 2]
                    for hh in range(4):
                        sl = slice(hh * 512, (hh + 1) * 512)
                        pso = PS[4 + hh % 2]
                        kb.mm(pso, AT[hh], Vc[rb][:, sl], start=True, stop=False)
                        kb.mm(pso, Qx[hh][:, 0, :], Sbf[hh][pp][:, 0, :], start=False, stop=False)
                        kb.mm(pso, Qx[hh][:, 1, :], Sbf[hh][pp][:, 1, :], start=False, stop=True)
                        if d == 1:
                            kb.copy('act' if hh % 2 == 0 else 'dve', ow[:, sl], pso)
                        else:
                            kb.tt('dve', o32[:, sl], pso, obl[rb][:, sl], ALU.add)
                            kb.bn_stats(st6[:, hh, :], o32[:, sl])
                            kb.bn_aggr(mv[:, hh, :], st6[:, hh, :])
                    if d == 1:
                        kb.dma('sp', V(OB[tok0:tok0 + 128, :]), ow)
                    else:
                        on = ons[ci % 2]
                        go = gos[ci % 2]
                        kb.act(lnv4, mv[:, :, 1], AF.Ln, bias=epst[:, 1:2])
                        kb.act(rs4, lnv4, AF.Exp, scale=-0.5)
                        kb.stt(nmr4, mv[:, :, 0], -1.0, rs4, ALU.mult, ALU.mult)
                        for hh in range(4):
                            sl = slice(hh * 512, (hh + 1) * 512)
                            kb.act(on[:, sl], o32[:, sl], AF.Identity, scale=rs4[:, hh:hh + 1], bias=nmr4[:, hh:hh + 1])
                        kb.tt('pool', go, on, gsl[rb], ALU.mult)

                        def tail(go=go, gt=goT[ci % 2], tok0=tok0):
                            for half in range(2):
                                pT = psb(6 + half)
                                for c in range(8):
                                    kc = half * 8 + c
                                    kb.tr(pT[:, c * 128:(c + 1) * 128], go[:, kc * 128:(kc + 1) * 128], identb)
                                kb.copy('act' if half == 0 else 'dve', gt[:, half * 8:(half + 1) * 8, :],
                                        pT.rr("p (c t) -> p c t", t=128))
                            kb.dma('sp', V(OTv[:, :, tok0:tok0 + 128]), gt)
                        pend_tail[0] = tail
                if pend_tail[0] is not None:
                    pend_tail[0]()
                    pend_tail[0] = None
                kb.barrier()

        def p3a(l):
            j = l // 2
            is_attn = (l % 2 == 0)
            need_ctx = l < DEPTH - 1
            KC = 8 if is_attn else 16
            A.reset()
            W1pre = A.alloc([8, 4096], BF16, 'W1pre')
            Wo = A.alloc([KC, 1024], BF16, 'Wo')
            load_w(Wo, 'wo_a' if is_attn else 'wo_r', j, KC)
            gates = [A.alloc([1024], F32, f'gate{b}') for b in range(2)]
            for b in range(2):
                kb.dma('sp', gates[b], V(modrow[l, b, 2048:3072].partition_broadcast(128)))
            OTb = [A.alloc([KC, 512], BF16, f'OTb{i}') for i in range(2)]
            hts = [A.alloc([1024], F32, f'ht{i}') for i in range(8)]
            tmp = [A.alloc([512], F32, f'tmp{i}') for i in range(2)]
            pb = prologue_bufs()
            uTs = [A.alloc([8, 512], BF16, f'uT{i}') for i in range(2)]
            OTv = OT.rearrange("(j p) t -> p j t", p=128)
            UTv = UT.rearrange("(j p) t -> p j t", p=128)
            ti = 0
            pi = 0
            blks = [bk for bk in blocks if not (bk[2] and not need_ctx)]

            def p3a_loads(bi_):
                tok0_, ntok_, isctx_ = blks[bi_]
                kb.dma('sp', OTb[bi_ % 2][:, :, :ntok_], V(OTv[:, 0:KC, tok0_:tok0_ + ntok_]))
                for t in range(ntok_ // 128):
                    kb.dma('sp', hts[(bi_ % 2) * 4 + t], hsrc(l, tok0_ + t * 128))
            p3a_loads(0)
            pend = [None]
            for bi, (tok0, ntok, isctx) in enumerate(blks):
                nt = ntok // 128
                b = 1 if isctx else 0
                ob = OTb[bi % 2]
                u = uTs[bi % 2]
                if bi + 1 < len(blks):
                    p3a_loads(bi + 1)
                if bi == 0:
                    load_w(W1pre, 'w1', l, 8)
                for t in range(nt):
                    ht = hts[(bi % 2) * 4 + t]
                    for n in range(2):
                        ps = PS[pi % 4]
                        pi += 1
                        for kc in range(KC):
                            kb.mm(ps, ob[:, kc, t * 128:(t + 1) * 128], Wo[:, kc, n * 512:(n + 1) * 512],
                                  start=(kc == 0), stop=(kc == KC - 1))
                        tm = tmp[ti % 2]
                        ti += 1
                        sl = slice(n * 512, (n + 1) * 512)
                        kb.tt('dve', tm, ps, gates[b][:, sl], ALU.mult)
                        kb.tt('pool', ht[:, sl], tm, ht[:, sl], ALU.add)
                    kb.dma('sp', V(hbuf[tok0 + t * 128:tok0 + (t + 1) * 128, :]), ht)
                    prologue_s1(ht, t, pb)
                    if pend[0] is not None:
                        pend[0]()

                    def s2(t=t, b=b, u=u, last=(t == nt - 1), tok0=tok0, ntok=ntok):
                        prologue_s2(l, 1, b, t, u, pb, banks=(6, 7))
                        if last:
                            kb.dma('sp', V(UTv[:, :, tok0:tok0 + ntok]), u[:, :, :ntok])
                    pend[0] = s2
            if pend[0] is not None:
                pend[0]()
            kb.barrier()

        def p3b(l):
            need_ctx = l < DEPTH - 1
            last = (l == DEPTH - 1)
            A.reset()
            W1 = A.alloc([8, 4096], BF16, 'W1')
            W2 = A.alloc([32, 1024], BF16, 'W2')
            load_w(W2, 'w2', l, 32)
            gate = A.alloc([1024], F32, 'gate')
            kb.dma('sp', gate, V(modrow[l, 0, 5120:6144].partition_broadcast(128)))
            hid = A.alloc([32, 512], BF16, 'hid')
            uTs = [A.alloc([8, 512], BF16, f'uT{i}') for i in range(2)]
            hts = [A.alloc([1024], F32, f'ht{i}') for i in range(4)]
            rl = [A.alloc([512], BF16, f'rl{i}') for i in range(1 if last else 2)]
            tmp = A.alloc([512], F32, 'tmp')
            if last:
                fg = A.alloc([1024], F32, 'fg')
                kb.dma('sp', fg, V(fng.partition_broadcast(128)))
                ssq = A.alloc([4], F32, 'ssq')
                lnv = A.alloc([4], F32, 'lnv')
                rstd = A.alloc([4], F32, 'rstd')
                junk = tmp.bc(BF16)
            UTv = UT.rearrange("(j p) t -> p j t", p=128)
            pi = 0
            blks = [bk for bk in blocks if not (bk[2] and not need_ctx)]

            def p3b_uload(bi_):
                tok0_, ntok_, isctx_ = blks[bi_]
                kb.dma('sp', uTs[bi_ % 2][:, :, :ntok_], V(UTv[:, :, tok0_:tok0_ + ntok_]))
            p3b_uload(0)
            for bi, (tok0, ntok, isctx) in enumerate(blks):
                nt = ntok // 128
                u = uTs[bi % 2]
                if bi + 1 < len(blks):
                    p3b_uload(bi + 1)
                if isctx:
                    kb.dma('sp', gate, V(modrow[l, 1, 5120:6144].partition_broadcast(128)))
                for t in range(nt):
                    kb.dma('sp', hts[t], V(hbuf[tok0 + t * 128:tok0 + (t + 1) * 128, :]))
                for f in range(32):
                    ps = PS[f % 3]
                    for c in range(8):
                        kb.mm(ps[:, :ntok], W1[:, c, f * 128:(f + 1) * 128], u[:, c, :ntok], start=(c == 0), stop=(c == 7))
                    r_ = rl[f % len(rl)]
                    kb.act(r_[:, :ntok], ps[:, :ntok], AF.Relu)
                    kb.tt('pool', hid[:, f, :ntok], r_[:, :ntok], r_[:, :ntok], ALU.mult)
                for t in range(nt):
                    ht = hts[t]
                    for n in range(2):
                        ps = PS[3 + pi % 4]
                        pi += 1
                        for f in range(32):
                            kb.mm(ps, hid[:, f, t * 128:(t + 1) * 128], W2[:, f, n * 512:(n + 1) * 512],
                                  start=(f == 0), stop=(f == 31))
                        sl = slice(n * 512, (n + 1) * 512)
                        kb.tt('dve', tmp, ps, gate[:, sl], ALU.mult)
                        kb.tt('dve', ht[:, sl], tmp, ht[:, sl], ALU.add)
                    if last:
                        kb.act(junk, ht, AF.Square, accum=ssq[:, t:t + 1])
                        kb.act(lnv[:, t:t + 1], ssq[:, t:t + 1], AF.Ln, scale=1.0 / D, bias=epst[:, 0:1])
                        kb.act(rstd[:, t:t + 1], lnv[:, t:t + 1], AF.Exp, scale=-0.5)
                        kb.stt(ht, ht, rstd[:, t:t + 1], fg, ALU.mult, ALU.mult)
                        kb.dma('sp', V(out[tok0 + t * 128:tok0 + (t + 1) * 128, :]), ht)
                    else:
                        kb.dma('sp', V(hbuf[tok0 + t * 128:tok0 + (t + 1) * 128, :]), ht)
            kb.barrier()

        def convert_layer(l, defer=False):
            j = l // 2
            if l % 2 == 0:
                convert_w('wqkv', j, defer)
                convert_w('wo_a', j, defer)
            else:
                convert_w('win', j, defer)
                convert_w('wo_r', j, defer)
            convert_w('w1', l, defer)
            convert_w('w2', l, defer)
        convert_layer(layers[0])
        p0()
        for l in layers:
            if l % 2 == 0:
                p1_attn(l)
                if stop_after == ('p1', l):
                    break
                if l == layers[0]:
                    for l2 in layers[1:]:
                        convert_layer(l2, defer=True)
                p2_attn(l)
                emit_conversions(1.0)
            else:
                p1_ret(l)
                if stop_after == ('p1', l):
                    break
                p2_ret(l)
            if stop_after == ('p2', l):
                break
            p3a(l)
            if stop_after == ('p3a', l):
                break
            p3b(l)
            if stop_after == ('p3b', l):
                break
        block = stack.enter_context(nc.Block())
        S.emit(nc, stack, block)
    return nc


def tile_w(w, kc):
    n = w.shape[-1]
    return np.ascontiguousarray(w.reshape(kc, 128, n).transpose(1, 0, 2))


def make_in_maps(inputs, T, nb):
    f = lambda a: np.ascontiguousarray(np.asarray(a, dtype=np.float32))
    hc = host_consts(T)
    shared = dict(hc)
    shared['cctxT'] = np.ascontiguousarray(f(inputs['c_ctx']).reshape(8, 128).T)
    shared['ada_w'] = np.stack([tile_w(f(inputs['ada_w'][i]), 8) for i in range(4)])
    shared['ada_b'] = f(inputs['ada_b'])
    shared['wqkv'] = np.stack([tile_w(f(inputs['attn_w_qkv'][i]), 8) for i in range(2)])
    shared['wo_a'] = np.stack([tile_w(f(inputs['attn_w_o'][i]), 8) for i in range(2)])
    shared['lam'] = f(inputs['attn_lambda']).reshape(2, 256)
    shared['subg'] = np.ascontiguousarray(f(inputs['attn_subln_g']).T)
    shared['win'] = np.stack([tile_w(f(inputs['ret_w_in'][i]), 8) for i in range(2)])
    shared['wo_r'] = np.stack([tile_w(f(inputs['ret_w_o'][i]), 16) for i in range(2)])
    shared['dlog'] = f(inputs['ret_decay_logit']).reshape(2, 8)
    shared['w1'] = np.stack([tile_w(f(inputs['mlp_w1'][i]), 8) for i in range(4)])
    shared['w2'] = np.stack([tile_w(f(inputs['mlp_w2'][i]), 32) for i in range(4)])
    shared['fng'] = f(inputs['final_norm_g'])
    xs = f(inputs['x'])
    cs = f(inputs['c'])
    cx = f(inputs['ctx'])
    maps = []
    for b in range(nb):
        m = dict(shared)
        m['x'] = np.ascontiguousarray(xs[b, :T])
        m['ctx'] = np.ascontiguousarray(cx[b])
        m['cT'] = np.ascontiguousarray(cs[b].reshape(8, 128).T)
        maps.append(m)
    return maps


def kernel(**inputs):
    T = 4096
    nb = 8
    nc = build(T)
    maps = make_in_maps(inputs, T, nb)
    res = run_bass_kernel_spmd(nc, maps, core_ids=list(range(nb)))
    return np.stack([np.asarray(res.results[b]["out"], dtype=np.float32) for b in range(nb)], 0)
```
